# Optimizing a Trainium2 kernel written in Bass

```python
import jax, jax.numpy as jnp
from jax import lax
import numpy as np

D_MODEL = 1024
BATCH = 8
SEQ = 2048
DEPTH = 1

D_MIX = D_MODEL
SB_HEAD_DIM = 64
SB_WIDTH = D_MIX // 2
SB_HEADS = SB_WIDTH // SB_HEAD_DIM
GDN_HEAD_DIM = 128
GDN_WIDTH = D_MIX - SB_WIDTH
GDN_HEADS = GDN_WIDTH // GDN_HEAD_DIM
CONV_WIDTH = 4
CHUNK = 64
BLOCK_Q = 128
EPS = 1e-6
IN_SPLITS = (SB_WIDTH, SB_WIDTH, SB_WIDTH, SB_WIDTH,
             GDN_WIDTH, GDN_WIDTH, GDN_WIDTH, GDN_WIDTH,
             GDN_HEADS, GDN_HEADS)
D_IN = sum(IN_SPLITS)

kernel_name = "hybrid_stickbreak_gated_deltanet_block"


def rmsnorm(x, w):
    x32 = x.astype(jnp.float32)
    y = x32 * lax.rsqrt(jnp.mean(x32 * x32, axis=-1, keepdims=True) + EPS)
    return (y * w.astype(jnp.float32)).astype(x.dtype)


def l2norm(x):
    x32 = x.astype(jnp.float32)
    return x32 * lax.rsqrt(jnp.sum(x32 * x32, axis=-1, keepdims=True) + EPS)


def causal_depthwise_conv(x, w):
    k_taps, t_len = w.shape[0], x.shape[1]
    xp = jnp.pad(x, ((0, 0), (k_taps - 1, 0), (0, 0)))
    out = xp[:, 0:t_len] * w[0]
    for i in range(1, k_taps):
        out = out + xp[:, i:i + t_len] * w[i]
    return out


def stick_breaking_attention(q, k, v):
    t_len, d = q.shape[2], q.shape[3]
    scale = float(1.0 / np.sqrt(d))
    outs = []
    for blk in range(t_len // BLOCK_Q):
        start, end = blk * BLOCK_Q, (blk + 1) * BLOCK_Q
        qb = q[:, :, start:end]
        kb, vb = k[:, :, :end], v[:, :, :end]
        z = jnp.einsum('bhqd,bhkd->bhqk', qb, kb).astype(jnp.float32) * scale
        t_idx = start + jnp.arange(BLOCK_Q)
        s_idx = jnp.arange(end)
        causal = s_idx[None, :] < t_idx[:, None]
        sp = jnp.where(causal, jax.nn.softplus(z), 0.0)
        tail = lax.cumsum(sp, axis=sp.ndim - 1, reverse=True) - sp
        attn = jnp.where(causal, jnp.exp(jax.nn.log_sigmoid(z) - tail), 0.0)
        outs.append(jnp.einsum('bhqk,bhkd->bhqd', attn.astype(v.dtype), vb))
    return jnp.concatenate(outs, axis=2)


def gated_delta_chunked(q, k, v, g, beta):
    out_dtype = v.dtype
    b, h, t_len, dk = q.shape
    dv = v.shape[-1]
    n_chunks = t_len // CHUNK
    q = q.astype(jnp.float32) * float(dk ** -0.5)
    k = k.astype(jnp.float32)
    v = v.astype(jnp.float32)
    resh = lambda a: a.reshape((b, h, n_chunks, CHUNK) + a.shape[3:])
    q, k, v, g, beta = resh(q), resh(k), resh(v), resh(g.astype(jnp.float32)), resh(beta.astype(jnp.float32))
    g = lax.cumsum(g, axis=g.ndim - 1)
    incl = jnp.tril(jnp.ones((CHUNK, CHUNK), dtype=bool))
    strict = jnp.tril(jnp.ones((CHUNK, CHUNK), dtype=bool), k=-1)
    decay = jnp.exp(jnp.where(incl, g[..., :, None] - g[..., None, :], -jnp.inf))
    k_beta = k * beta[..., None]
    v_beta = v * beta[..., None]
    a_mat = jnp.where(strict, jnp.einsum('bhnid,bhnjd->bhnij', k_beta, k) * decay, 0.0)
    eye = jnp.eye(CHUNK, dtype=jnp.float32)
    rhs = jnp.concatenate([v_beta, k_beta * jnp.exp(g)[..., None]], axis=-1)
    sol = lax.linalg.triangular_solve(a_mat + eye, rhs, left_side=True, lower=True,
                                      unit_diagonal=True)
    u, w = sol[..., :dv], sol[..., dv:]
    attn_intra = jnp.where(incl, jnp.einsum('bhnid,bhnjd->bhnij', q, k) * decay, 0.0)

    def step(state, inp):
        q_c, k_c, u_c, w_c, g_c, a_c = inp
        v_new = u_c - jnp.einsum('bhcd,bhde->bhce', w_c, state)
        o_c = jnp.einsum('bhcd,bhde->bhce', q_c * jnp.exp(g_c)[..., None], state) \
            + jnp.einsum('bhij,bhje->bhie', a_c, v_new)
        g_last = g_c[..., -1]
        state = state * jnp.exp(g_last)[..., None, None] + jnp.einsum(
            'bhcd,bhce->bhde', k_c * jnp.exp(g_last[..., None] - g_c)[..., None], v_new)
        return state, o_c

    mv = lambda a: jnp.moveaxis(a, 2, 0)
    state0 = jnp.zeros((b, h, dk, dv), dtype=jnp.float32)
    _, o = lax.scan(step, state0, (mv(q), mv(k), mv(u), mv(w), mv(g), mv(attn_intra)))
    o = jnp.moveaxis(o, 0, 2).reshape(b, h, t_len, dv)
    return o.astype(out_dtype)


def setup_inputs(seed: int = 0) -> dict:
    key = jax.random.key(seed)
    ks = jax.random.split(key, 10)
    f32 = jnp.float32
    x = jax.random.normal(ks[0], (BATCH, SEQ, D_MODEL), f32)
    norm1_w = 1.0 + 0.02 * jax.random.normal(ks[1], (DEPTH, D_MODEL), f32)
    w_in = jax.random.normal(ks[2], (DEPTH, D_MODEL, D_IN), f32) * D_MODEL ** -0.5
    sb_norm_w = 1.0 + 0.02 * jax.random.normal(ks[3], (DEPTH, SB_HEAD_DIM), f32)
    gdn_conv_w = jax.random.normal(ks[4], (DEPTH, CONV_WIDTH, 3 * GDN_WIDTH), f32) * CONV_WIDTH ** -0.5
    gdn_A_log = jnp.log(jax.random.uniform(ks[5], (DEPTH, GDN_HEADS), f32, 1.0, 16.0))
    dt = jnp.exp(jax.random.uniform(ks[6], (DEPTH, GDN_HEADS), f32, float(np.log(1e-3)), float(np.log(1e-1))))
    gdn_dt_bias = dt + jnp.log(-jnp.expm1(-dt))
    gdn_norm_w = 1.0 + 0.02 * jax.random.normal(ks[7], (DEPTH, GDN_HEAD_DIM), f32)
    w_out = jax.random.normal(ks[8], (DEPTH, D_MIX, D_MODEL), f32) * D_MIX ** -0.5
    final_norm_w = 1.0 + 0.02 * jax.random.normal(ks[9], (D_MODEL,), f32)
    return {"x": x, "norm1_w": norm1_w, "w_in": w_in, "sb_norm_w": sb_norm_w,
            "gdn_conv_w": gdn_conv_w, "gdn_A_log": gdn_A_log, "gdn_dt_bias": gdn_dt_bias,
            "gdn_norm_w": gdn_norm_w, "w_out": w_out, "final_norm_w": final_norm_w}


def reference(x, norm1_w, w_in, sb_norm_w, gdn_conv_w, gdn_A_log, gdn_dt_bias,
              gdn_norm_w, w_out, final_norm_w):
    b, t_len, _ = x.shape
    split_idx = [int(s) for s in np.cumsum(IN_SPLITS)[:-1]]
    to_heads = lambda a, n, d: a.reshape(b, t_len, n, d).transpose(0, 2, 1, 3)
    for layer in range(DEPTH):
        h = rmsnorm(x, norm1_w[layer])
        proj = jnp.einsum('btd,de->bte', h, w_in[layer])
        sb_q, sb_k, sb_v, sb_z, g_q, g_k, g_v, g_z, g_b, g_a = jnp.split(proj, split_idx, axis=-1)

        o_sb = stick_breaking_attention(to_heads(sb_q, SB_HEADS, SB_HEAD_DIM),
                                        to_heads(sb_k, SB_HEADS, SB_HEAD_DIM),
                                        to_heads(sb_v, SB_HEADS, SB_HEAD_DIM))
        o_sb = rmsnorm(o_sb.transpose(0, 2, 1, 3), sb_norm_w[layer]).reshape(b, t_len, SB_WIDTH)
        o_sb = o_sb * jax.nn.silu(sb_z)

        qkv = jax.nn.silu(causal_depthwise_conv(jnp.concatenate([g_q, g_k, g_v], axis=-1),
                                                gdn_conv_w[layer]))
        gq, gk, gv = jnp.split(qkv, 3, axis=-1)
        gq = l2norm(to_heads(gq, GDN_HEADS, GDN_HEAD_DIM)).astype(x.dtype)
        gk = l2norm(to_heads(gk, GDN_HEADS, GDN_HEAD_DIM)).astype(x.dtype)
        gv = to_heads(gv, GDN_HEADS, GDN_HEAD_DIM)
        beta = jax.nn.sigmoid(g_b.astype(jnp.float32)).transpose(0, 2, 1)
        decay = (-jnp.exp(gdn_A_log[layer].astype(jnp.float32))
                 * jax.nn.softplus(g_a.astype(jnp.float32) + gdn_dt_bias[layer].astype(jnp.float32)))
        decay = decay.transpose(0, 2, 1)
        o_gdn = gated_delta_chunked(gq, gk, gv, decay, beta)
        o_gdn = rmsnorm(o_gdn.transpose(0, 2, 1, 3), gdn_norm_w[layer]).reshape(b, t_len, GDN_WIDTH)
        o_gdn = o_gdn * jax.nn.silu(g_z)

        mixed = jnp.concatenate([o_sb, o_gdn], axis=-1)
        x = x + jnp.einsum('bte,ed->btd', mixed, w_out[layer])
    return rmsnorm(x, final_norm_w)
```

```python
import contextlib
import numpy as np
import concourse.bass as bass
import concourse.mybir as mybir
from concourse.bass_utils import run_bass_kernel_spmd

F32 = mybir.dt.float32
BF16 = mybir.dt.bfloat16
AF = mybir.ActivationFunctionType
ALU = mybir.AluOpType
AX = mybir.AxisListType

T = 2048
D = 1024
DIN = 4104
NT = 16
DC = 8
EPS = 1e-6
NEG = -30000.0

ENGS = ("pe", "act", "dve", "pool", "sp")


class Buf:
    def __init__(self, handle, name, psum=False):
        self.h = handle
        self.name = name
        self.psum = psum
        self.w = {}
        self.r = {}

    def __getitem__(self, idx):
        return self.h[idx]


class Prog:
    def __init__(self, nc, n_dma_sems=32):
        self.nc = nc
        self.ops = {e: [] for e in ENGS}
        self.cnt = {e: 0 for e in ENGS}
        self.known = {e: {} for e in ENGS}
        self.dma_pool = n_dma_sems
        self.dma_idx = 0
        self.dma_val = [0] * n_dma_sems
        self.final_tokens = []
        self.n_sw = 0

    def _deps(self, reads, writes, eng=None):
        deps = []
        for (b, k) in list(reads) + list(writes):
            t = b.w.get(k)
            if t is not None:
                deps.append(t)
        for (b, k) in reads:
            if b.psum:
                deps.extend(r for r in b.r.get(k, []) if r[0] != eng)
        for (b, k) in writes:
            deps.extend(b.r.get(k, []))
        return deps

    def _commit(self, tok, reads, writes):
        for (b, k) in reads:
            b.r.setdefault(k, []).append(tok)
        for (b, k) in writes:
            b.w[k] = tok
            b.r[k] = []

    def _waits_for(self, eng, deps, skip_same_pe=True):
        need = {}
        for (sk, val) in deps:
            if sk == eng and eng == "pe" and skip_same_pe:
                continue
            if val > need.get(sk, 0):
                need[sk] = val
        out = []
        kn = self.known[eng]
        for sk, val in need.items():
            if kn.get(sk, 0) >= val:
                continue
            kn[sk] = val
            out.append((sk, val))
        return out

    def op(self, eng, fn, reads=(), writes=(), extra=()):
        deps = self._deps(reads, writes, eng) + list(extra)
        waits = self._waits_for(eng, deps)
        self.cnt[eng] += 1
        tok = (eng, self.cnt[eng])
        self.ops[eng].append((waits, fn, ("eng", eng)))
        self._commit(tok, reads, writes)
        return tok

    def dma(self, queue, fn, reads=(), writes=(), final=False):
        deps = self._deps(reads, writes, queue)
        if queue == "pool":
            semkey = ("sw", self.n_sw)
            self.n_sw += 1
            waits = self._waits_for(queue, deps)
            tok = (semkey, 16)
            self.ops[queue].append((waits, fn, semkey))
            self._commit(tok, reads, writes)
            if final:
                self.final_tokens.append(tok)
            return tok
        i = self.dma_idx % self.dma_pool
        self.dma_idx += 1
        semkey = ("dma", i)
        if self.dma_val[i] > 0:
            deps.append((semkey, self.dma_val[i]))
        waits = self._waits_for(queue, deps)
        self.dma_val[i] += 16
        tok = (semkey, self.dma_val[i])
        self.ops[queue].append((waits, fn, ("dma", i)))
        self._commit(tok, reads, writes)
        if final:
            self.final_tokens.append(tok)
        return tok

    def barrier(self):
        toks = [(e, self.cnt[e]) for e in ENGS if self.cnt[e] > 0]
        for i in range(self.dma_pool):
            if self.dma_val[i] > 0:
                toks.append((("dma", i), self.dma_val[i]))
        for i in range(self.n_sw):
            toks.append((("sw", i), 16))
        for e in ENGS:
            waits = self._waits_for(e, toks, skip_same_pe=False)
            if waits:
                self.ops[e].append((waits, None, None))

    def emit(self, stack):
        nc = self.nc
        sems = {}
        for e in ENGS:
            sems[e] = stack.enter_context(nc.semaphore("s_" + e))
        for i in range(self.dma_pool):
            sems[("dma", i)] = stack.enter_context(nc.semaphore("d%d" % i))
        for i in range(self.n_sw):
            sems[("sw", i)] = stack.enter_context(nc.semaphore("w%d" % i))
        fw = self._waits_for("sp", self.final_tokens)
        if fw:
            self.ops["sp"].append((fw, None, None))
        block = stack.enter_context(nc.Block())

        def make(e):
            def body(eng):
                for (waits, fn, inc) in self.ops[e]:
                    for (sk, val) in waits:
                        eng.wait_ge(sems[sk], val)
                    if fn is None:
                        continue
                    ins = fn(eng)
                    if inc[0] == "eng":
                        ins.then_inc(sems[inc[1]], 1)
                    else:
                        ins.then_inc(sems[inc], 16)
            return body

        block.tensor(make("pe"))
        block.scalar(make("act"))
        block.vector(make("dve"))
        block.gpsimd(make("pool"))
        block.sync(make("sp"))


def build(dbg=None, stage=99, cut=99):
    dbg = dbg or {}
    nc = bass.Bass("TRN2", target_bir_lowering=False)
    x = nc.dram_tensor("x", [T, D], F32, kind="ExternalInput").ap()
    norm1_w = nc.dram_tensor("norm1_w", [D], F32, kind="ExternalInput").ap()
    w_in = nc.dram_tensor("w_in", [D, DIN], F32, kind="ExternalInput").ap()
    sb_norm_w = nc.dram_tensor("sb_norm_w", [64], F32, kind="ExternalInput").ap()
    conv_w = nc.dram_tensor("gdn_conv_w", [4, 1536], F32, kind="ExternalInput").ap()
    A_log = nc.dram_tensor("gdn_A_log", [4], F32, kind="ExternalInput").ap()
    dt_bias = nc.dram_tensor("gdn_dt_bias", [4], F32, kind="ExternalInput").ap()
    gdn_norm_w = nc.dram_tensor("gdn_norm_w", [128], F32, kind="ExternalInput").ap()
    w_out = nc.dram_tensor("w_out", [D, D], F32, kind="ExternalInput").ap()
    final_norm_w = nc.dram_tensor("final_norm_w", [D], F32, kind="ExternalInput").ap()
    y = nc.dram_tensor("y", [T, D], F32, kind="ExternalOutput").ap()
    dbg_out = {}

    P = Prog(nc)
    with contextlib.ExitStack() as st:
        def sbuf(stack, name, shape, dt=F32):
            return Buf(stack.enter_context(nc.sbuf_tensor(name, list(shape), dt)), name)

        def sb(name, shape, dt=F32):
            return sbuf(st, name, shape, dt)

        def dbg_dump(name, buf, key, shape, dt=F32, src=None):
            if name not in dbg:
                return
            d = nc.dram_tensor("dbg_" + name, list(shape), dt, kind="ExternalOutput").ap()
            dbg_out[name] = d
            keys = key if isinstance(key, list) else [key]
            P.dma("sp", lambda e: e.dma_start(out=d, in_=buf.h[:] if src is None else src),
                  reads=[(buf, k) for k in keys], final=True)

        psq = [st.enter_context(nc.psum_tensor("pq%d" % i, [128, 1024], F32)) for i in range(4)]

        class HView:
            def __init__(self, h, off):
                self.h_, self.off = h, off

            def __getitem__(self, idx):
                if not isinstance(idx, tuple):
                    idx = (idx, slice(None))
                p_, f_ = idx
                a = 0 if f_.start is None else f_.start
                b = 512 if f_.stop is None else f_.stop
                return self.h_[p_, self.off + a:self.off + b]

        ps = [Buf(HView(psq[i // 2], (i % 2) * 512), "ps%d" % i, psum=True) for i in range(8)]

        def act_fn(out, in_, func, reads, writes, **kw):
            return P.op("act", lambda e: e.activation(out=out, in_=in_, func=func, **kw), reads=reads, writes=writes)

        def rsqrt_small(dst_ap, src_ap, scale, reads, wkey):
            P.op("act", lambda e: e.activation(out=dst_ap, in_=src_ap, func=AF.Ln, scale=scale, bias=eps_c.h[:, 0:1]),
                 reads=list(reads) + [(eps_c, "all")], writes=[wkey])
            P.op("act", lambda e: e.activation(out=dst_ap, in_=dst_ap, func=AF.Exp, scale=-0.5), writes=[wkey])

        eps_c = sb("eps_c", [128, 1], F32)
        P.op("pool", lambda e: e.memset(eps_c.h[:], EPS), writes=[(eps_c, "all")])
        ident_bf = sb("ident_bf", [128, 128], BF16)
        ident_f = sb("ident_f", [128, 128], F32)
        negU = sb("negU", [128, 128], BF16)
        negOnes = sb("negOnes", [128, 128], BF16)
        ones_bf = sb("ones_bf", [128, 128], BF16)
        n1w = sb("n1w", [128, DC], F32)
        ssq = sb("ssq", [128, NT], F32)
        rstd = sb("rstd", [128, NT], F32)
        sbw_bc = sb("sbw_bc", [128, 64], F32)
        gnw_bc = sb("gnw_bc", [128, 128], F32)
        cw = sb("cw", [128, 12, 4], F32)
        mg = sb("mg", [128, NT, 1024], BF16)
        gqkv = sb("gqkv", [128, 12, T], BF16)
        bd = sb("bd", [128, NT, 8], F32)

        def const_tri(buf, val, cmp, fill=0.0, pattern=None, base=0, cm=1, sl=None):
            ap = buf.h[:] if sl is None else sl
            P.op("pool", lambda e: e.memset(ap, val), writes=[(buf, "all")])
            P.op("pool", lambda e: e.affine_select(out=ap, in_=ap, compare_op=cmp, fill=fill,
                                                   base=base, pattern=pattern,
                                                   channel_multiplier=cm),
                 writes=[(buf, "all")])

        const_tri(ident_bf, 1.0, ALU.is_equal, pattern=[[1, 128]], cm=-1)
        const_tri(ident_f, 1.0, ALU.is_equal, pattern=[[1, 128]], cm=-1)
        const_tri(negU, -1.0, ALU.is_ge, pattern=[[-1, 128]], cm=1)
        P.op("pool", lambda e: e.memset(negOnes.h[:], -1.0), writes=[(negOnes, "all")])
        P.op("pool", lambda e: e.memset(ones_bf.h[:], 1.0), writes=[(ones_bf, "all")])

        vrow = sb("vrow", [56, 128], F32)
        P.dma("act", lambda e: e.dma_start(out=vrow.h[0:8, :], in_=norm1_w.rearrange("(c p) -> c p", p=128)),
              writes=[(vrow, "a")])
        P.dma("act", lambda e: e.dma_start(out=vrow.h[8:56, :], in_=conv_w.rearrange("i (c p) -> (i c) p", p=128)),
              writes=[(vrow, "b")])
        P.op("pe", lambda e: e.transpose(out=ps[7].h[:, 0:56], in_=vrow.h[:, :], identity=ident_f.h[0:56, 0:56]),
             reads=[(vrow, "a"), (vrow, "b"), (ident_f, "all")], writes=[(ps[7], "all")])
        P.op("dve", lambda e: e.tensor_copy(out=n1w.h[:], in_=ps[7].h[:, 0:8]), reads=[(ps[7], "all")], writes=[(n1w, "all")])
        P.op("dve", lambda e: e.tensor_copy(out=cw.h[:].rearrange("p c i -> p i c"),
                                            in_=ps[7].h[:, 8:56].rearrange("p (i c) -> p i c", i=4)),
             reads=[(ps[7], "all")], writes=[(cw, "all")])
        P.dma("act", lambda e: e.dma_start(out=sbw_bc.h[:], in_=sb_norm_w.partition_broadcast(128)),
              writes=[(sbw_bc, "all")])
        P.dma("act", lambda e: e.dma_start(out=gnw_bc.h[:], in_=gdn_norm_w.partition_broadcast(128)),
              writes=[(gnw_bc, "all")])

        s1 = contextlib.ExitStack()
        qT = sbuf(s1, "qT", [128, 4, T], BF16)
        kT = sbuf(s1, "kT", [128, 4, T], BF16)
        vS = sbuf(s1, "vS", [128, NT, 512], BF16)

        s2 = contextlib.ExitStack()
        hT = sbuf(s2, "hT", [128, DC, T], BF16)
        s3 = contextlib.ExitStack()
        xs = [sbuf(s3, "xs%d" % i, [128, D], F32) for i in range(3)]
        xn = [sbuf(s3, "xn%d" % i, [128, D], BF16) for i in range(3)]
        junk = sbuf(s3, "junk", [128, D], BF16)

        def A1(tt):
            b = tt % 3
            P.dma("sp", lambda e: e.dma_start(out=xs[b].h[:], in_=x[tt * 128:(tt + 1) * 128, :]),
                  writes=[(xs[b], "all")])
            P.op("act", lambda e: e.activation(out=junk.h[:], in_=xs[b].h[:], func=AF.Square,
                                               accum_out=ssq.h[:, tt:tt + 1]),
                 reads=[(xs[b], "all")], writes=[(junk, "all"), (ssq, tt)])
            rsqrt_small(rstd.h[:, tt:tt + 1], ssq.h[:, tt:tt + 1], 1.0 / D, [(ssq, tt)], (rstd, tt))
            P.op("dve", lambda e: e.tensor_scalar(out=xn[b].h[:], in0=xs[b].h[:],
                                                  scalar1=rstd.h[:, tt:tt + 1], scalar2=None, op0=ALU.mult),
                 reads=[(xs[b], "all"), (rstd, tt)], writes=[(xn[b], "all")])

        def A2(tt):
            b = tt % 3
            pb = ps[tt % 2]
            pbf = pb.h[:].bitcast(BF16)
            for dc in range(DC):
                P.op("pe", lambda e, dc=dc: e.transpose(out=pbf[:, dc * 128:(dc + 1) * 128],
                                                       in_=xn[b].h[:, dc * 128:(dc + 1) * 128],
                                                       identity=ident_bf.h[:]),
                     reads=[(xn[b], "all"), (ident_bf, "all")], writes=[(pb, "all")])
            src = pbf.rearrange("p (c t) -> p c t", c=DC)
            P.op("dve", lambda e: e.tensor_tensor(
                out=hT.h[:, :, tt * 128:(tt + 1) * 128], in0=src,
                in1=n1w.h[:].unsqueeze(2).to_broadcast([128, DC, 128]), op=ALU.mult),
                reads=[(pb, "all"), (n1w, "all")], writes=[(hT, tt)])

        for t in range(NT + 1):
            if t < NT:
                A1(t)
            if t >= 1:
                A2(t - 1)
        HT_ALL = [(hT, tt) for tt in range(NT)]
        P.barrier()
        s3.close()

        wbf = [sbuf(s2, "wbf%d" % i, [128, DC, 512], BF16) for i in range(3)]
        cin = [sbuf(s2, "cin%d" % i, [128, T + 4], BF16) for i in range(2)]
        dgw = [sbuf(s2, "dgw%d" % i, [128, 4, 128], BF16) for i in range(2)]
        w_view = w_in.rearrange("(c p) e -> p c e", p=128)
        gcount = [0]

        def load_group(col0, ncols):
            b = gcount[0] % 3
            gcount[0] += 1
            P.dma("pool", lambda e: e.dma_start(out=wbf[b].h[:, :, 0:ncols], in_=w_view[:, :, col0:col0 + ncols]),
                  writes=[(wbf[b], "all")])
            return wbf[b]

        pj_banks = [2, 3, 4, 5]
        pj_i = [0]

        def proj_feat(wb, mc, tg, evac):
            pb = ps[pj_banks[pj_i[0] % 4]]
            pj_i[0] += 1
            for dc in range(DC):
                P.op("pe", lambda e, dc=dc, pb=pb: e.matmul(pb.h[:], lhsT=wb.h[:, dc, mc * 128:(mc + 1) * 128],
                                                         rhs=hT.h[:, dc, tg * 512:(tg + 1) * 512],
                                                         start=(dc == 0), stop=(dc == DC - 1)),
                     reads=[(wb, "all")] + HT_ALL[tg * 4:(tg + 1) * 4], writes=[(pb, "all")])
            evac(pb)

        def proj_tok(wb, tt, evac, ncols=512):
            pb = ps[pj_banks[pj_i[0] % 4]]
            pj_i[0] += 1
            for dc in range(DC):
                P.op("pe", lambda e, dc=dc, pb=pb: e.matmul(pb.h[:, 0:ncols], lhsT=hT.h[:, dc, tt * 128:(tt + 1) * 128],
                                                         rhs=wb.h[:, dc, 0:ncols],
                                                         start=(dc == 0), stop=(dc == DC - 1)),
                     reads=[(wb, "all"), (hT, tt)], writes=[(pb, "all")])
            evac(pb)

        ev_i = [0]

        def evac_copy(dst_ap, dst_rw, scale=None, ncols=512):
            def f(pb):
                ev_i[0] += 1
                src = pb.h[:, 0:ncols]
                if ev_i[0] % 2 == 0:
                    if scale is None:
                        P.op("act", lambda e: e.copy(out=dst_ap, in_=src), reads=[(pb, "all")], writes=dst_rw)
                    else:
                        P.op("act", lambda e: e.activation(out=dst_ap, in_=src, func=AF.Copy, scale=scale),
                             reads=[(pb, "all")], writes=dst_rw)
                else:
                    if scale is None:
                        P.op("dve", lambda e: e.tensor_copy(out=dst_ap, in_=src), reads=[(pb, "all")], writes=dst_rw)
                    else:
                        P.op("dve", lambda e: e.tensor_scalar(out=dst_ap, in0=src, scalar1=scale, scalar2=None,
                                                            op0=ALU.mult), reads=[(pb, "all")], writes=dst_rw)
            return f

        def gate_group(col0, half):
            wb = load_group(col0, 512)
            for tt in range(NT):
                def ev(pb, tt=tt):
                    P.op("act", lambda e: e.activation(out=mg.h[:, tt, half * 512:(half + 1) * 512], in_=pb.h[:],
                                                       func=AF.Silu),
                         reads=[(pb, "all")], writes=[(mg, (tt, half))])
                proj_tok(wb, tt, ev)

        for b2 in range(2):
            P.op("pool", lambda e, b2=b2: e.memset(cin[b2].h[:, 0:4], 0.0), writes=[(cin[b2], "pad")])
        cci = [0]

        PJB = [2, 3, 4]
        CVB = [5, 6]
        ONB = [7, 0]
        cnt3 = {"pj": 0, "cv": 0, "on": 0, "st": 0}

        class Ch:
            pass

        chunks = []
        for gi in range(3):
            for mc in range(4):
                c_ = Ch()
                c_.gi, c_.mc, c_.cc = gi, mc, gi * 4 + mc
                c_.ci = cin[len(chunks) % 2]
                c_.dg = dgw[len(chunks) % 2]
                c_.pbs = {}
                c_.sbufs = {}
                chunks.append(c_)
        wbs = {}

        def st_proj(c_, tg):
            if c_.gi not in wbs:
                wbs[c_.gi] = load_group(2048 + 512 * c_.gi, 512)
            wb_ = wbs[c_.gi]
            if tg == 0:
                for i in range(4):
                    P.op("dve", lambda e, i=i: e.tensor_scalar(out=c_.dg.h[:, i, :], in0=ident_f.h[:],
                                                             scalar1=cw.h[:, c_.cc, i:i + 1], scalar2=None, op0=ALU.mult),
                         reads=[(ident_f, "all"), (cw, "all")], writes=[(c_.dg, i)])
            pb = ps[PJB[cnt3["pj"] % 3]]
            cnt3["pj"] += 1
            for dc in range(DC):
                P.op("pe", lambda e, dc=dc: e.matmul(pb.h[:], lhsT=wb_.h[:, dc, c_.mc * 128:(c_.mc + 1) * 128],
                                                    rhs=hT.h[:, dc, tg * 512:(tg + 1) * 512],
                                                    start=(dc == 0), stop=(dc == DC - 1)),
                     reads=[(wb_, "all")] + HT_ALL[tg * 4:(tg + 1) * 4], writes=[(pb, "all")])
            evac_copy(c_.ci.h[:, 4 + tg * 512:4 + (tg + 1) * 512], [(c_.ci, tg)])(pb)

        def st_conv(c_, tg):
            ci, dg = c_.ci, c_.dg
            pb = ps[CVB[cnt3["cv"] % 2]]
            cnt3["cv"] += 1
            rd = [(ci, tg)] + ([(ci, tg - 1)] if tg > 0 else [(ci, "pad")])
            for i in range(4):
                P.op("pe", lambda e, i=i: e.matmul(
                    pb.h[:], lhsT=dg.h[:, i, :], rhs=ci.h[:, 1 + i + tg * 512:1 + i + (tg + 1) * 512],
                    start=(i == 0), stop=(i == 3)),
                    reads=rd + [(dg, i)], writes=[(pb, "all")])
            dst = gqkv.h[:, c_.cc, tg * 512:(tg + 1) * 512]
            act_fn(dst, pb.h[:], AF.Silu, [(pb, "all")], [(gqkv, (c_.cc, tg))])

        def st_ones(c_, tg):
            return

        NCHK = len(chunks)
        for i in range(NCHK + 1):
            cur = chunks[i] if i < NCHK else None
            prv = chunks[i - 1] if i >= 1 else None
            for tg in range(4):
                if cur is not None:
                    st_proj(cur, tg)
                if prv is not None:
                    st_conv(prv, tg)
                    if tg >= 1:
                        st_ones(prv, tg - 1)
            if prv is not None:
                st_ones(prv, 3)
        gate_group(3584, 1)
        wb = load_group(4096, 8)
        for tt in range(NT):
            proj_tok(wb, tt, evac_copy(bd.h[:, tt, :], [(bd, tt)], ncols=8), ncols=8)

        gate_group(1536, 0)

        sqn = [sbuf(s2, "sqn%d" % i, [128, 512], BF16) for i in range(3)]
        rn = [sbuf(s2, "rn%d" % i, [128, 512], F32) for i in range(2)]
        l2jobs = [(cc, tg) for cc in range(8) for tg in range(4)]
        L2B = [6, 7, 0, 1]

        def l2_s1(i):
            cc, tg = l2jobs[i]
            q_ = sqn[i % 3]
            src = gqkv.h[:, cc, tg * 512:(tg + 1) * 512]
            P.op("dve", lambda e: e.tensor_tensor(out=q_.h[:], in0=src, in1=src, op=ALU.mult),
                 reads=[(gqkv, (cc, tg))], writes=[(q_, "all")])

        def l2_s1b(i):
            q_ = sqn[i % 3]
            pb_ = ps[L2B[i % 4]]
            P.op("pe", lambda e: e.matmul(pb_.h[:], lhsT=ones_bf.h[:], rhs=q_.h[:], start=True, stop=True),
                 reads=[(ones_bf, "all"), (q_, "all")], writes=[(pb_, "all")])

        def l2_s2(i):
            pb_ = ps[L2B[i % 4]]
            r_ = rn[i % 2]
            act_fn(r_.h[:], pb_.h[:], AF.Ln, [(pb_, "all"), (eps_c, "all")], [(r_, "all")], bias=eps_c.h[:, 0:1])
            act_fn(r_.h[:], r_.h[:], AF.Exp, [], [(r_, "all")], scale=-0.5)

        def l2_s3(i):
            cc, tg = l2jobs[i]
            r_ = rn[i % 2]
            dst = gqkv.h[:, cc, tg * 512:(tg + 1) * 512]
            sc = (128.0 ** -0.5) if cc < 4 else 1.0
            P.op("dve", lambda e: e.scalar_tensor_tensor(out=dst, in0=dst, scalar=sc, in1=r_.h[:],
                                                        op0=ALU.mult, op1=ALU.mult),
                 reads=[(r_, "all")], writes=[(gqkv, (cc, tg))])

        l2t = [0]

        def l2_step():
            t_ = l2t[0]
            if t_ >= len(l2jobs) + 3:
                return
            l2t[0] += 1
            if 0 <= t_ - 1 < len(l2jobs):
                l2_s1b(t_ - 1)
            if t_ < len(l2jobs):
                l2_s1(t_)
            if 0 <= t_ - 2 < len(l2jobs):
                l2_s2(t_ - 2)
            if 0 <= t_ - 3 < len(l2jobs):
                l2_s3(t_ - 3)

        wb = load_group(0, 512)
        for mc in range(4):
            for tg in range(4):
                proj_feat(wb, mc, tg, evac_copy(qT.h[:, mc, tg * 512:(tg + 1) * 512], [(qT, (mc, tg))], scale=0.125))
                l2_step()
        wb = load_group(512, 512)
        for mc in range(4):
            for tg in range(4):
                proj_feat(wb, mc, tg, evac_copy(kT.h[:, mc, tg * 512:(tg + 1) * 512], [(kT, (mc, tg))]))
                l2_step()
        wb = load_group(1024, 512)
        for tt in range(NT):
            proj_tok(wb, tt, evac_copy(vS.h[:, tt, :], [(vS, tt)]))
            l2_step()
        while l2t[0] < len(l2jobs) + 3:
            l2_step()

        dbg_dump("qT", qT, [(mc, tg) for mc in range(4) for tg in range(4)], [128, 4, T], BF16)
        dbg_dump("gqkv", gqkv, [(cc, tg) for cc in range(12) for tg in range(4)], [128, 12, T], BF16)
        dbg_dump("bd", bd, list(range(NT)), [128, NT, 8], F32)
        P.barrier()
        s2.close()

        s4 = contextlib.ExitStack()
        maskSB = sbuf(s4, "maskSB", [128, 4, 512], BF16)
        for i in range(4):
            const_tri(maskSB, 0.0, ALU.is_gt, fill=NEG, pattern=[[1, 512]],
                      base=-128 * i, cm=-1, sl=maskSB.h[:, i, :])
        Epr = [sbuf(s4, "Ep%d" % i, [128, 2, 512], F32) for i in range(2)]
        SPp = [sbuf(s4, "SPp%d" % i, [128, 2, 512], BF16) for i in range(3)]
        SrF = [sbuf(s4, "SrF%d" % i, [128, 512], F32) for i in range(2)]
        SrB = [sbuf(s4, "SrB%d" % i, [128, 512], BF16) for i in range(3)]
        Apr = [sbuf(s4, "Ap%d" % i, [128, 2, 512], BF16) for i in range(3)]
        oSqs = [sbuf(s4, "oSq%d" % i, [128, 4, 512], F32) for i in range(2)]
        sqt = sbuf(s4, "sqt", [128, 4, 512], F32)
        ss4 = sbuf(s4, "ss4", [128, 32], F32)

        class Blk:
            pass

        pairs = []
        gidx = 0
        for qg in range(4):
            for h in range(8):
                n = 4 * (qg + 1)
                for jp in range(n // 2):
                    q_ = Blk()
                    q_.h, q_.qg, q_.n, q_.jp = h, qg, n, jp
                    q_.kb = [n - 1 - 2 * jp, n - 2 - 2 * jp]
                    q_.g = gidx
                    q_.idx = len(pairs)
                    q_.zi = q_.idx % 3
                    q_.zb = [ps[2 * q_.zi], ps[2 * q_.zi + 1]]
                    q_.E = Epr[q_.idx % 2]
                    q_.SP = SPp[q_.idx % 3]
                    q_.SrF = SrF[gidx % 2]
                    q_.SrBout = SrB[q_.idx % 3]
                    q_.SrBin = SrB[(q_.idx - 1) % 3]
                    q_.A = Apr[q_.idx % 3]
                    q_.ob = ps[6 + (gidx % 2)]
                    q_.cp = 128 * max(0, q_.kb[1] - 4 * qg)
                    q_.cpn = 128 * max(0, q_.kb[1] - 2 - 4 * qg)
                    q_.first = (jp == 0)
                    q_.last = (jp == n // 2 - 1)
                    pairs.append(q_)
                gidx += 1
        first_av = {}

        def head_norm_gate(src, nh, hd, wbc, tt0, ntt, col0, tmp, ssb, rkeys, tag, defer=None):
            width = nh * hd
            v3 = lambda ap: ap.rearrange("p a (h d) -> p (a h) d", d=hd)
            nn = ntt * nh
            half = col0 // 512
            mkeys = [(mg, (tt, half)) for tt in range(tt0, tt0 + ntt)]
            steps = [
                lambda: P.op("dve", lambda e: e.tensor_tensor(out=tmp.h[:, 0:ntt, 0:width], in0=src.h[:, 0:ntt, 0:width],
                                                            in1=src.h[:, 0:ntt, 0:width], op=ALU.mult),
                             reads=rkeys, writes=[(tmp, "all")]),
                lambda: (P.op("dve", lambda e: e.tensor_reduce(out=ssb.h[:, 0:nn], in_=v3(tmp.h[:, 0:ntt, 0:width]),
                                                             axis=AX.X, op=ALU.add),
                              reads=[(tmp, "all")], writes=[(ssb, "all")]),
                         rsqrt_small(ssb.h[:, 0:nn], ssb.h[:, 0:nn], 1.0 / hd, [], (ssb, "all"))),
                lambda: P.op("dve", lambda e: e.tensor_tensor(out=v3(tmp.h[:, 0:ntt, 0:width]), in0=v3(src.h[:, 0:ntt, 0:width]),
                                                            in1=ssb.h[:, 0:nn].unsqueeze(2).to_broadcast([128, nn, hd]),
                                                            op=ALU.mult),
                             reads=rkeys + [(ssb, "all")], writes=[(tmp, "all")]),
                lambda: P.op("dve", lambda e: e.tensor_tensor(out=v3(tmp.h[:, 0:ntt, 0:width]), in0=v3(tmp.h[:, 0:ntt, 0:width]),
                                                            in1=wbc.h[:, 0:hd].unsqueeze(1).to_broadcast([128, nn, hd]),
                                                            op=ALU.mult),
                             reads=[(wbc, "all")], writes=[(tmp, "all")]),
                lambda: P.op("dve", lambda e: e.tensor_tensor(out=mg.h[:, tt0:tt0 + ntt, col0:col0 + width],
                                                            in0=tmp.h[:, 0:ntt, 0:width],
                                                            in1=mg.h[:, tt0:tt0 + ntt, col0:col0 + width], op=ALU.mult),
                             reads=[(tmp, "all")], writes=mkeys),
            ]
            if defer is None:
                for f in steps:
                    f()
            else:
                defer.extend(steps)

        def zpair(q_):
            return psq[q_.zi][:, :].rearrange("p (b c) -> p b c", b=2)

        def ZRW(q_):
            return [(q_.zb[0], "all"), (q_.zb[1], "all")]

        def S1(q_):
            h, qg, cp = q_.h, q_.qg, q_.cp
            c, p0 = h // 2, 64 * (h % 2)
            for bi in range(2):
                kb = q_.kb[bi]
                zb = q_.zb[bi]
                diag = kb >= 4 * qg
                P.op("pe", lambda e, kb=kb, zb=zb, diag=diag: e.matmul(
                    zb.h[:, cp:], lhsT=kT.h[p0:p0 + 64, c, kb * 128:(kb + 1) * 128],
                    rhs=qT.h[p0:p0 + 64, c, qg * 512 + cp:(qg + 1) * 512], start=True, stop=not diag),
                    reads=[(qT, (c, qg)), (kT, (c, kb // 4))], writes=[(zb, "all")])
                if diag:
                    P.op("pe", lambda e, kb=kb, zb=zb: e.matmul(zb.h[:, cp:], lhsT=ident_bf.h[:],
                                                             rhs=maskSB.h[:, kb - 4 * qg, cp:], start=False, stop=True),
                         reads=[(ident_bf, "all"), (maskSB, "all")], writes=[(zb, "all")])

        def S2(q_):
            cp, E, SP_ = q_.cp, q_.E, q_.SP
            P.op("act", lambda e: e.activation(out=E.h[:, :, cp:], in_=zpair(q_)[:, :, cp:], func=AF.Exp),
                 reads=ZRW(q_), writes=[(E, "all")])
            P.op("act", lambda e: e.activation(out=SP_.h[:, :, cp:], in_=E.h[:, :, cp:], func=AF.Ln, bias=1.0),
                 reads=[(E, "all")], writes=[(SP_, "all")])

        def S3(q_):
            if q_.last:
                return
            F_, SP_, Bo, cp, cpn = q_.SrF, q_.SP, q_.SrBout, q_.cp, q_.cpn
            if q_.first:
                if cp > 0:
                    P.op("dve", lambda e: e.memset(F_.h[:, 0:cp], 0.0), writes=[(F_, "all")])
                P.op("dve", lambda e: e.tensor_tensor(out=F_.h[:, cp:], in0=SP_.h[:, 0, cp:], in1=SP_.h[:, 1, cp:], op=ALU.add),
                     reads=[(SP_, "all")], writes=[(F_, "all")])
            else:
                for bi in range(2):
                    P.op("dve", lambda e, bi=bi: e.tensor_tensor(out=F_.h[:, cp:], in0=F_.h[:, cp:], in1=SP_.h[:, bi, cp:],
                                                                op=ALU.add),
                         reads=[(SP_, "all")], writes=[(F_, "all")])
            P.op("dve", lambda e: e.tensor_copy(out=Bo.h[:, cpn:], in_=F_.h[:, cpn:]), reads=[(F_, "all")],
                 writes=[(Bo, "all")])

        def S4(q_):
            SP_, Bi, cp = q_.SP, q_.SrBin, q_.cp
            for bi in range(2):
                zb = q_.zb[bi]
                P.op("pe", lambda e, bi=bi, zb=zb: e.matmul(zb.h[:, cp:], lhsT=negU.h[:], rhs=SP_.h[:, bi, cp:],
                                                           start=False, stop=True, skip_group_check=True),
                     reads=[(negU, "all"), (SP_, "all")], writes=[(zb, "all")])
                if bi == 1:
                    P.op("pe", lambda e, zb=zb: e.matmul(zb.h[:, cp:], lhsT=negOnes.h[:], rhs=SP_.h[:, 0, cp:],
                                                        start=False, stop=True, skip_group_check=True),
                         reads=[(negOnes, "all"), (SP_, "all")], writes=[(zb, "all")])
                if not q_.first:
                    P.op("pe", lambda e, zb=zb: e.matmul(zb.h[:, cp:], lhsT=negOnes.h[:], rhs=Bi.h[:, cp:],
                                                        start=False, stop=True, skip_group_check=True),
                         reads=[(negOnes, "all"), (Bi, "all")], writes=[(zb, "all")])

        def S5(q_):
            cp, A_ = q_.cp, q_.A
            P.op("act", lambda e: e.activation(out=A_.h[:, :, cp:], in_=zpair(q_)[:, :, cp:], func=AF.Exp),
                 reads=ZRW(q_), writes=[(A_, "all")])

        def S6(q_):
            h, qg, ob, A_ = q_.h, q_.qg, q_.ob, q_.A
            for bi in range(2):
                kb = q_.kb[bi]
                for qc in range(4):
                    if kb - 4 * qg > qc:
                        continue
                    st_flag = q_.g not in first_av
                    first_av[q_.g] = True
                    P.op("pe", lambda e, qc=qc, st_flag=st_flag, bi=bi, kb=kb: e.matmul(
                        ob.h[:, qc * 64:(qc + 1) * 64], lhsT=A_.h[:, bi, qc * 128:(qc + 1) * 128],
                        rhs=vS.h[:, kb, h * 64:(h + 1) * 64], start=st_flag, stop=True, skip_group_check=True),
                        reads=[(A_, "all"), (vS, kb)], writes=[(ob, "all")])
            if q_.last:
                oSq = oSqs[qg % 2]
                P.op("dve", lambda e: e.tensor_copy(
                    out=oSq.h[:, :, h * 64:(h + 1) * 64],
                    in_=ob.h[:, 0:256].rearrange("p (a b) -> p a b", a=4)),
                    reads=[(ob, "all")], writes=[(oSq, h)])
                if h == 7:
                    if "oS" in dbg:
                        d = nc.dram_tensor("dbg_oS%d" % qg, [128, 4, 512], F32, kind="ExternalOutput").ap()
                        P.dma("sp", lambda e, d=d: e.dma_start(out=d, in_=oSq.h[:]),
                              reads=[(oSq, hh) for hh in range(8)], final=True)
                    head_norm_gate(oSq, 8, 64, sbw_bc, 4 * qg, 4, 0, sqt, ss4, [(oSq, hh) for hh in range(8)], "sb",
                                   defer=sb_defer)

        sb_defer = []
        NB = len(pairs)
        for t in range(NB + 3):
            if 0 <= t - 2 < NB:
                S4(pairs[t - 2])
            if t < NB:
                S1(pairs[t])
            if 0 <= t - 1 < NB:
                S2(pairs[t - 1])
                S3(pairs[t - 1])
            if 0 <= t - 2 < NB:
                S5(pairs[t - 2])
            if 0 <= t - 3 < NB:
                S6(pairs[t - 3])
            if sb_defer and t % 2 == 0:
                sb_defer.pop(0)()
        while sb_defer:
            sb_defer.pop(0)()
        P.barrier()
        s4.close()
        s1.close()

        if stage < 2.5:
            for tt in range(NT):
                P.op("pool", lambda e, tt=tt: e.memset(mg.h[:, tt, 512:1024], 0.0), writes=[(mg, (tt, 1))])
        else:
            s6 = contextlib.ExitStack()
            g6 = lambda name, shape, dt=F32: sbuf(s6, name, shape, dt)
            ones_f = g6("ones_f", [128, 128], F32)
            Lcum = g6("Lcum", [128, 128], F32)
            Lsel = g6("Lsel", [128, 128], F32)
            maskD = g6("maskD", [128, 128], F32)
            maskDs = g6("maskDs", [128, 128], F32)
            P.op("pool", lambda e: e.memset(ones_f.h[:], 1.0), writes=[(ones_f, "all")])
            const_tri(Lcum, 1.0, ALU.is_ge, pattern=[[1, 128]], cm=-1)
            P.op("pool", lambda e: e.memset(Lcum.h[0:64, 64:128], 0.0), writes=[(Lcum, "all")])
            for hf in range(2):
                const_tri(Lsel, 1.0, ALU.is_equal, pattern=[[0, 64]], cm=1, base=-(63 + 64 * hf),
                          sl=Lsel.h[:, 64 * hf:64 * hf + 64])
            const_tri(maskD, 0.0, ALU.is_ge, fill=NEG, pattern=[[-1, 128]], cm=1)
            P.op("pool", lambda e: e.memset(maskD.h[64:128, 0:64], NEG), writes=[(maskD, "all")])
            const_tri(maskDs, 0.0, ALU.is_gt, fill=NEG, pattern=[[-1, 128]], cm=1)
            P.op("pool", lambda e: e.memset(maskDs.h[64:128, 0:64], NEG), writes=[(maskDs, "all")])

            par = g6("par", [128, 8], F32)
            beta = g6("beta", [128, NT, 4], F32)
            gdec = g6("gdec", [128, NT, 4], F32)
            gc = g6("gc", [128, NT, 4], F32)
            ngc = g6("ngc", [128, NT, 4], F32)
            ekd = g6("ekd", [128, NT, 4], F32)
            begc = g6("begc", [128, NT, 4], F32)
            eglB = g6("eglB", [128, 4, 32], F32)
            P.dma("sp", lambda e: e.dma_start(out=par.h[:, 0:4], in_=dt_bias.partition_broadcast(128)), writes=[(par, "a")])
            P.dma("sp", lambda e: e.dma_start(out=par.h[:, 4:8], in_=A_log.partition_broadcast(128)), writes=[(par, "b")])
            act_fn(par.h[:, 4:8], par.h[:, 4:8], AF.Exp, [], [(par, "b")])
            P.op("dve", lambda e: e.tensor_scalar(out=par.h[:, 4:8], in0=par.h[:, 4:8], scalar1=-1.0, scalar2=None,
                                                  op0=ALU.mult), writes=[(par, "b")])
            BD_ALL = [(bd, tt) for tt in range(NT)]
            act_fn(beta.h[:], bd.h[:, :, 0:4], AF.Sigmoid, BD_ALL, [(beta, "all")])
            P.op("dve", lambda e: e.tensor_tensor(out=gdec.h[:], in0=bd.h[:, :, 4:8],
                                                  in1=par.h[:, 0:4].unsqueeze(1).to_broadcast([128, NT, 4]), op=ALU.add),
                 reads=BD_ALL + [(par, "a")], writes=[(gdec, "all")])
            act_fn(gdec.h[:], gdec.h[:], AF.Exp, [], [(gdec, "all")])
            act_fn(gdec.h[:], gdec.h[:], AF.Ln, [], [(gdec, "all")], bias=1.0)
            P.op("dve", lambda e: e.tensor_tensor(out=gdec.h[:], in0=gdec.h[:],
                                                  in1=par.h[:, 4:8].unsqueeze(1).to_broadcast([128, NT, 4]), op=ALU.mult),
                 reads=[(par, "b")], writes=[(gdec, "all")])
            flat = lambda b_: b_.h[:].rearrange("p a b -> p (a b)")
            pb = ps[6]
            P.op("pe", lambda e: e.matmul(pb.h[:, 0:64], lhsT=Lcum.h[:], rhs=flat(gdec), start=True, stop=True),
                 reads=[(Lcum, "all"), (gdec, "all")], writes=[(pb, "all")])
            P.op("dve", lambda e: e.tensor_copy(out=flat(gc), in_=pb.h[:, 0:64]), reads=[(pb, "all")], writes=[(gc, "all")])
            P.op("dve", lambda e: e.tensor_scalar(out=flat(ngc), in0=flat(gc), scalar1=-1.0, scalar2=None, op0=ALU.mult),
                 reads=[(gc, "all")], writes=[(ngc, "all")])
            pb2 = ps[7]
            P.op("pe", lambda e: e.matmul(pb2.h[:, 0:64], lhsT=Lsel.h[:], rhs=flat(gc), start=True, stop=True),
                 reads=[(Lsel, "all"), (gc, "all")], writes=[(pb2, "all")])
            P.op("dve", lambda e: e.tensor_tensor(out=flat(ekd), in0=pb2.h[:, 0:64], in1=flat(gc), op=ALU.subtract),
                 reads=[(pb2, "all"), (gc, "all")], writes=[(ekd, "all")])
            act_fn(ekd.h[:], ekd.h[:], AF.Exp, [], [(ekd, "all")])
            act_fn(begc.h[:], gc.h[:], AF.Exp, [(gc, "all")], [(begc, "all")])
            P.op("dve", lambda e: e.tensor_tensor(out=begc.h[:], in0=begc.h[:], in1=beta.h[:], op=ALU.mult),
                 reads=[(beta, "all")], writes=[(begc, "all")])
            gcb = g6("gcb", [128, NT, 4], F32)
            act_fn(gcb.h[:], beta.h[:], AF.Ln, [(beta, "all")], [(gcb, "all")])
            P.op("dve", lambda e: e.tensor_tensor(out=gcb.h[:], in0=gcb.h[:], in1=gc.h[:], op=ALU.add),
                 reads=[(gc, "all")], writes=[(gcb, "all")])
            dbg_dump("gc", gc, "all", [128, NT, 4], F32)
            dbg_dump("beta", beta, "all", [128, NT, 4], F32)

            for tt in range(NT):
                for h in range(4):
                    P.op("pool", lambda e, tt=tt, h=h: e.tensor_tensor(
                        out=mg.h[:, tt, 512 + h * 128:512 + (h + 1) * 128],
                        in0=mg.h[:, tt, 512 + h * 128:512 + (h + 1) * 128], in1=gnw_bc.h[:], op=ALU.mult),
                        reads=[(gnw_bc, "all")], writes=[(mg, (tt, 1))])
            uB = g6("uB", [128, NT, 4, 128], BF16)
            wT = g6("wT", [128, 4, T], BF16)
            qgT = g6("qgT", [128, 4, T], BF16)
            aiT = g6("aiT", [128, NT, 4, 64], BF16)
            kdB = g6("kdB", [128, NT, 4, 128], BF16)

            Sf = [g6("Sf%d" % h, [128, 128], F32) for h in range(4)]
            Sb = [g6("Sb%d" % h, [128, 128], BF16) for h in range(4)]
            vnwA = g6("vnwA", [128, 4, 128], BF16)
            oG = g6("oG", [128, 1, 512], F32)
            sqg = g6("sqg", [128, 1, 128], F32)
            ssg = g6("ssg", [128, 4], F32)
            for h in range(4):
                P.op("pool", lambda e, h=h: e.memset(Sf[h].h[:], 0.0), writes=[(Sf[h], "all")])
                P.op("pool", lambda e, h=h: e.memset(Sb[h].h[:], 0.0), writes=[(Sb[h], "all")])
            s7 = contextlib.ExitStack()
            g7 = lambda name, shape, dt=F32: sbuf(s7, name, shape, dt)
            NCH = 3
            dgc = [g7("dgc%d" % i, [128, 128], F32) for i in range(2)]
            GBm3 = g7("GBm", [128, 1, 512], F32)
            EGt3 = g7("EGt", [128, 1, 512], F32)
            GBs = [g7("GBs%d" % i, [128, 512], F32) for i in range(NCH)]
            Ab = [g7("Ab%d" % i, [128, 512], BF16) for i in range(NCH)]
            ATb = [g7("ATb%d" % i, [128, 512], BF16) for i in range(NCH)]
            Pb = [[g7("Pb%d_%d" % (i, j), [128, 512], BF16) for j in range(2)] for i in range(NCH)]
            PTb = [[g7("PTb%d_%d" % (i, j), [128, 512], BF16) for j in range(2)] for i in range(NCH)]
            Gb = [[g7("Gb%d" % i, [128, 512], BF16)] * 2 for i in range(NCH)]
            Dm = [PTb[i][1] for i in range(NCH)]
            AIb = [Pb[i][1] for i in range(NCH)]
            vb = [g7("vb%d" % i, [128, 512], BF16) for i in range(NCH)]
            rw = [g7("rw%d" % i, [128, 512], BF16) for i in range(NCH)]
            bank_i = [0]

            for i_, b_ in enumerate(ps):
                b_.bidx = i_
            free_banks = [4, 5, 6, 7]

            def get_banks(n):
                while len(free_banks) < n:
                    yield
                return [ps[free_banks.pop(0)] for _ in range(n)]

            def rel(*bs):
                for b_ in bs:
                    free_banks.append(b_.bidx)

            dgi = [0]

            class Chain:
                pass

            def chain_steps(h, tq, ci):
                tts = [4 * tq + k for k in range(4)]
                sl = lambda k: slice(k * 128, (k + 1) * 128)
                tsl = lambda k: slice(tts[k] * 128, (tts[k] + 1) * 128)
                qn = lambda k: gqkv.h[:, h, tsl(k)]
                kn = lambda k: gqkv.h[:, 4 + h, tsl(k)]
                vn_ = lambda k: gqkv.h[:, 8 + h, tsl(k)]
                QR = [(gqkv, (h, tq))]
                KR = [(gqkv, (4 + h, tq))]
                VR = [(gqkv, (8 + h, tq))]
                (pGB,) = yield from get_banks(1)
                for k in range(4):
                    d_ = dgc[dgi[0] % 2]
                    dgi[0] += 1
                    P.op("act", lambda e, d_=d_, k=k: e.activation(out=d_.h[:], in_=ident_f.h[:], func=AF.Copy,
                                                                  scale=gc.h[:, tts[k], h:h + 1]),
                         reads=[(ident_f, "all"), (gc, "all")], writes=[(d_, "all")])
                    P.op("pe", lambda e, d_=d_, k=k: e.matmul(pGB.h[:, sl(k)], lhsT=ones_f.h[:], rhs=d_.h[:],
                                                            start=True, stop=True),
                         reads=[(ones_f, "all"), (d_, "all")], writes=[(pGB, "all")])
                yield
                v4 = lambda ap: ap.rearrange("p (a b) -> p a b", a=4)
                P.op("dve", lambda e: e.tensor_tensor(out=v4(GBm3.h[:, 0, :]), in0=v4(pGB.h[:]),
                                                      in1=maskD.h[:].unsqueeze(1).to_broadcast([128, 4, 128]),
                                                      op=ALU.subtract),
                     reads=[(pGB, "all"), (maskD, "all")], writes=[(GBm3, "all")])
                P.op("dve", lambda e: e.tensor_tensor(out=v4(GBs[ci].h[:]), in0=v4(pGB.h[:]),
                                                      in1=maskDs.h[:].unsqueeze(1).to_broadcast([128, 4, 128]),
                                                      op=ALU.subtract),
                     reads=[(pGB, "all"), (maskDs, "all")], writes=[(GBs[ci], "all")])
                act_fn(EGt3.h[:, 0, :], pGB.h[:], AF.Exp, [(pGB, "all")], [(EGt3, "all")])
                P.op("dve", lambda e: e.tensor_tensor(out=qgT.h[:, h, tq * 512:(tq + 1) * 512],
                                                      in0=gqkv.h[:, h, tq * 512:(tq + 1) * 512], in1=EGt3.h[:, 0, :],
                                                      op=ALU.mult),
                     reads=QR + [(EGt3, "all")], writes=[(qgT, (h, tq))])
                P.op("dve", lambda e: e.tensor_copy(out=eglB.h[:, h, 8 * tq:8 * tq + 8],
                                                    in_=EGt3.h[:, 0, :].rearrange("p (c t) -> p c t", t=64)[:, :, 63]),
                     reads=[(EGt3, "all")], writes=[(eglB, (h, tq))])
                for k in range(4):
                    act_fn(Dm[ci].h[:, sl(k)], GBm3.h[:, 0, sl(k)], AF.Exp, [(GBm3, "all"), (gc, "all")],
                           [(Dm[ci], "all")], bias=gc.h[:, tts[k], h:h + 1], scale=-1.0)
                    act_fn(GBs[ci].h[:, sl(k)], GBs[ci].h[:, sl(k)], AF.Exp, [(gcb, "all")],
                           [(GBs[ci], "all")], bias=gcb.h[:, tts[k], h:h + 1], scale=-1.0)
                rel(pGB)
                pK, pQ = yield from get_banks(2)
                for k in range(4):
                    P.op("pe", lambda e, k=k: e.matmul(pK.h[:, sl(k)], lhsT=kn(k), rhs=kn(k), start=True, stop=True),
                         reads=KR, writes=[(pK, "all")])
                for k in range(4):
                    P.op("pe", lambda e, k=k: e.matmul(pQ.h[:, sl(k)], lhsT=qn(k), rhs=kn(k), start=True, stop=True),
                         reads=KR + QR, writes=[(pQ, "all")])
                yield
                P.op("dve", lambda e: e.tensor_tensor(out=Ab[ci].h[:], in0=pK.h[:], in1=GBs[ci].h[:], op=ALU.mult),
                     reads=[(pK, "all"), (GBs[ci], "all")], writes=[(Ab[ci], "all")])
                P.op("dve", lambda e: e.tensor_tensor(out=AIb[ci].h[:], in0=pQ.h[:], in1=Dm[ci].h[:], op=ALU.mult),
                     reads=[(pQ, "all"), (Dm[ci], "all")], writes=[(AIb[ci], "all")])
                rel(pK, pQ)
                pT1, pT2 = yield from get_banks(2)
                pT1b = pT1.h[:].bitcast(BF16)
                for k in range(4):
                    P.op("pe", lambda e, k=k: e.transpose(out=pT1b[:, sl(k)], in_=Ab[ci].h[:, sl(k)], identity=ident_bf.h[:]),
                         reads=[(Ab[ci], "all"), (ident_bf, "all")], writes=[(pT1, "all")])
                for k in range(4):
                    P.op("pe", lambda e, k=k: e.transpose(out=pT1b[:, 512 + k * 128:512 + (k + 1) * 128],
                                                         in_=AIb[ci].h[:, sl(k)], identity=ident_bf.h[:]),
                         reads=[(AIb[ci], "all"), (ident_bf, "all")], writes=[(pT1, "all")])
                pT2b = pT2.h[:].bitcast(BF16)
                for k in range(4):
                    P.op("pe", lambda e, k=k: e.transpose(out=pT2b[:, sl(k)], in_=kn(k), identity=ident_bf.h[:]),
                         reads=KR + [(ident_bf, "all")], writes=[(pT2, "all")])
                for k in range(4):
                    P.op("pe", lambda e, k=k: e.transpose(out=pT2b[:, 512 + k * 128:512 + (k + 1) * 128], in_=vn_(k),
                                                         identity=ident_bf.h[:]),
                         reads=VR + [(ident_bf, "all")], writes=[(pT2, "all")])
                yield
                act_fn(ATb[ci].h[:], pT1b[:, 0:512], AF.Copy, [(pT1, "all")], [(ATb[ci], "all")])
                for hf_ in range(2):
                    rr = slice(64 * hf_, 64 * hf_ + 64)
                    P.op("dve", lambda e, rr=rr: e.tensor_copy(
                        out=aiT.h[rr, tts[0]:tts[0] + 4, h, :],
                        in_=pT1b[rr, 512:1024].rearrange("p (a b) -> p a b", a=4)[:, :, rr]),
                        reads=[(pT1, "all")], writes=[(aiT, (h, tq, hf_))])
                G0 = Gb[ci][0]
                P.op("dve", lambda e: e.tensor_tensor(
                    out=G0.h[:].rearrange("p (a b) -> p a b", a=4),
                    in0=ident_bf.h[:].unsqueeze(1).to_broadcast([128, 4, 128]),
                    in1=ATb[ci].h[:].rearrange("p (a b) -> p a b", a=4), op=ALU.subtract),
                    reads=[(ATb[ci], "all"), (ident_bf, "all")], writes=[(G0, "all")])
                for k in range(4):
                    act_fn(rw[ci].h[:, sl(k)], pT2b[:, sl(k)], AF.Copy, [(pT2, "all"), (begc, "all")], [(rw[ci], k)],
                           scale=begc.h[:, tts[k], h:h + 1])
                    act_fn(kdB.h[:, tts[k], h, :], pT2b[:, sl(k)], AF.Copy, [(pT2, "all"), (ekd, "all")],
                           [(kdB, (h, tts[k]))], scale=ekd.h[:, tts[k], h:h + 1])
                    act_fn(vb[ci].h[:, sl(k)], pT2b[:, 512 + k * 128:512 + (k + 1) * 128], AF.Copy,
                           [(pT2, "all"), (beta, "all")], [(vb[ci], k)], scale=beta.h[:, tts[k], h:h + 1])
                rel(pT1, pT2)
                Pc, PTc = Ab[ci], ATb[ci]
                Gc = G0
                for lv in range(5):
                    last = (lv == 4)
                    Pn = Pb[ci][lv % 2]
                    PTn = PTb[ci][lv % 2]
                    Gn = Gb[ci][(lv + 1) % 2]
                    if last:
                        (pP,) = yield from get_banks(1)
                    else:
                        pP, pPT = yield from get_banks(2)
                    for k in range(4):
                        P.op("pe", lambda e, k=k, pP=pP, Pc=Pc, PTc=PTc: e.matmul(
                            pP.h[:, sl(k)], lhsT=PTc.h[:, sl(k)], rhs=Pc.h[:, sl(k)], start=True, stop=True),
                            reads=[(Pc, "all"), (PTc, "all")], writes=[(pP, "all")])
                    if not last:
                        for k in range(4):
                            P.op("pe", lambda e, k=k, pPT=pPT, Pc=Pc, PTc=PTc: e.matmul(
                                pPT.h[:, sl(k)], lhsT=Pc.h[:, sl(k)], rhs=PTc.h[:, sl(k)], start=True, stop=True),
                                reads=[(Pc, "all"), (PTc, "all")], writes=[(pPT, "all")])
                    yield
                    act_fn(Pn.h[:], pP.h[:], AF.Copy, [(pP, "all")], [(Pn, "all")])
                    if not last:
                        P.op("dve", lambda e, PTn=PTn, pPT=pPT: e.tensor_copy(out=PTn.h[:], in_=pPT.h[:]),
                             reads=[(pPT, "all")], writes=[(PTn, "all")])
                    rel(pP)
                    if not last:
                        rel(pPT)
                    (pG,) = yield from get_banks(1)
                    for k in range(4):
                        P.op("pe", lambda e, k=k, pG=pG, Pn=Pn, Gc=Gc: e.matmul(
                            pG.h[:, sl(k)], lhsT=Pn.h[:, sl(k)], rhs=Gc.h[:, sl(k)], start=True, stop=True),
                            reads=[(Pn, "all"), (Gc, "all")], writes=[(pG, "all")])
                    yield
                    P.op("dve", lambda e, pG=pG, Gc=Gc, Gn=Gn: e.tensor_tensor(out=Gn.h[:], in0=pG.h[:], in1=Gc.h[:], op=ALU.add),
                         reads=[(pG, "all"), (Gc, "all")], writes=[(Gn, "all")])
                    rel(pG)
                    Pc, PTc, Gc = Pn, PTn, Gn
                GT = Gc
                pU, pW = yield from get_banks(2)
                for k in range(4):
                    P.op("pe", lambda e, k=k: e.matmul(pU.h[:, sl(k)], lhsT=GT.h[:, sl(k)], rhs=vb[ci].h[:, sl(k)],
                                                      start=True, stop=True),
                         reads=[(GT, "all"), (vb[ci], k)], writes=[(pU, "all")])
                for k in range(4):
                    P.op("pe", lambda e, k=k: e.matmul(pW.h[:, sl(k)], lhsT=rw[ci].h[:, sl(k)], rhs=GT.h[:, sl(k)],
                                                      start=True, stop=True),
                         reads=[(GT, "all"), (rw[ci], k)], writes=[(pW, "all")])
                yield
                act_fn(uB.h[:, tts[0]:tts[0] + 4, h, :], pU.h[:].rearrange("p (a b) -> p a b", a=4), AF.Copy,
                       [(pU, "all")], [(uB, (h, tq))])
                P.op("dve", lambda e: e.tensor_copy(out=wT.h[:, h, tq * 512:(tq + 1) * 512], in_=pW.h[:]),
                     reads=[(pW, "all")], writes=[(wT, (h, tq))])
                rel(pU, pW)

            jobs = [(h, tq) for tq in range(4) for h in range(4)]
            if cut < 99:
                jobs = jobs[:NCH] if cut > 0 else []
            dbg_dump("uB", uB, [(h, tq) for h in range(4) for tq in range(4)], [128, NT, 4, 128], BF16)
            dbg_dump("wT", wT, [(h, tq) for h in range(4) for tq in range(4)], [128, 4, T], BF16)


            def tail_load(tt):
                b = tt % 2
                P.dma("sp", lambda e: e.dma_start(out=xr[b].h[:], in_=x[tt * 128:(tt + 1) * 128, :]),
                      writes=[(xr[b], "all")])

            DENSE_TAIL = True
            tasks = []
            emitted = {}
            slot = [0]
            last_ids = {}

            def add_task(kind, fn, deps, name=None, delay=0):
                tid = len(emitted) + len(tasks)
                tasks.append({"kind": kind, "fn": fn, "deps": [d for d in deps if d is not None], "id": tid,
                              "nb": slot[0] + delay})
                if name is not None:
                    last_ids[name] = tid
                return tid

            def pump(kind, n=1):
                slot[0] += 1
                cnt = 0
                i = 0
                while i < len(tasks):
                    t_ = tasks[i]
                    ok = all((d in emitted) and emitted[d] < slot[0] for d in t_["deps"]) and slot[0] > t_["nb"]
                    if ok and (t_["kind"] == "act" or (t_["kind"] == kind and cnt < n)):
                        if t_["kind"] != "act":
                            cnt += 1
                        tasks.pop(i)
                        t_["fn"]()
                        emitted[t_["id"]] = slot[0]
                        continue
                    i += 1

            def force_tasks(pred):
                last = -1
                for i, t_ in enumerate(tasks):
                    if pred(t_):
                        last = i
                slot[0] += 1
                for _ in range(last + 1):
                    t_ = tasks.pop(0)
                    t_["fn"]()
                    emitted[t_["id"]] = slot[0]

            def add_tile_tasks(tt, po):
                b = tt % 2
                pb = ps[6]
                pbf = pb.h[:].bitcast(BF16)
                po2 = ps[7]

                def n2(hs, gate):
                    def f():
                        for h in hs:
                            P.op("act", lambda e, h=h: e.activation(
                                out=oG.h[:, 0, h * 128:(h + 1) * 128], in_=po.h[:, h * 128:(h + 1) * 128],
                                func=AF.Copy, scale=ssg.h[:, h:h + 1]),
                                reads=[(po, "all"), (ssg, "all")], writes=[(oG, "all")])
                        if gate:
                            P.op("pool", lambda e: e.tensor_tensor(out=mg.h[:, tt, 512:1024], in0=oG.h[:, 0, :],
                                                                   in1=mg.h[:, tt, 512:1024], op=ALU.mult),
                                 reads=[(oG, "all")], writes=[(mg, (tt, 1))])
                    return f

                def tr(ecs, evac):
                    def f():
                        for ec in ecs:
                            P.op("pe", lambda e, ec=ec: e.transpose(out=pbf[:, ec * 128:(ec + 1) * 128],
                                                                   in_=mg.h[:, tt, ec * 128:(ec + 1) * 128],
                                                                   identity=ident_bf.h[:]),
                                 reads=[(mg, (tt, ec // 4)), (ident_bf, "all")], writes=[(pb, "all")])
                    return f

                def evac_mT():
                    src = pbf.rearrange("p (c t) -> p c t", c=DC)
                    P.op("act", lambda e: e.copy(out=mT[b].h[:], in_=src), reads=[(pb, "all")], writes=[(mT[b], "all")])

                def op_mm(hf, ecs):
                    def f():
                        for ec in ecs:
                            P.op("pe", lambda e, ec=ec: e.matmul(
                                po2.h[:], lhsT=mT[b].h[:, ec, :], rhs=wo_bf.h[:, ec, hf * 512:(hf + 1) * 512],
                                start=(ec == 0), stop=(ec == DC - 1)),
                                reads=[(mT[b], "all"), (wo_bf, (ec, hf))], writes=[(po2, "all")])
                    return f

                def add_res(hf):
                    def f():
                        P.op("dve", lambda e: e.tensor_tensor(
                            out=x2[b].h[:, hf * 512:(hf + 1) * 512], in0=po2.h[:], in1=xr[b].h[:, hf * 512:(hf + 1) * 512],
                            op=ALU.add), reads=[(po2, "all"), (xr[b], "all")], writes=[(x2[b], hf)])
                    return f

                def fin():
                    P.op("act", lambda e: e.activation(out=junkT.h[:], in_=x2[b].h[:], func=AF.Square,
                                                       accum_out=ssq2.h[:, tt:tt + 1]),
                         reads=[(x2[b], 0), (x2[b], 1)], writes=[(junkT, "all"), (ssq2, tt)])
                    rsqrt_small(ssq2.h[:, tt:tt + 1], ssq2.h[:, tt:tt + 1], 1.0 / D, [], (ssq2, tt))
                    P.op("act", lambda e: e.activation(out=x2[b].h[:], in_=x2[b].h[:], func=AF.Copy,
                                                       scale=ssq2.h[:, tt:tt + 1]),
                         reads=[(ssq2, tt)], writes=[(x2[b], 0), (x2[b], 1)])
                    P.op("pool", lambda e: e.tensor_tensor(out=x2[b].h[:], in0=x2[b].h[:], in1=fnw_bc.h[:], op=ALU.mult),
                         reads=[(fnw_bc, "all")], writes=[(x2[b], 0), (x2[b], 1)])
                    P.dma("sp", lambda e: e.dma_start(out=y[tt * 128:(tt + 1) * 128, :], in_=x2[b].h[:]),
                          reads=[(x2[b], 0), (x2[b], 1)], final=True)
                    if tt + 2 < NT:
                        tail_load(tt + 2)

                L = last_ids.get
                n2b = None
                if po is not None:
                    n2a = add_task("act", n2([0, 1], False), [], "n2a", delay=1)
                    n2b = add_task("act", n2([2, 3], True), [n2a], "n2b")
                    tasks[-1]["is_n2"] = True
                    tasks[-2]["is_n2"] = True
                if DENSE_TAIL and po is not None:
                    return
                t1 = add_task("pe", tr([0, 1, 2, 3], False), [n2b, L("tev")])
                t2 = add_task("pe", tr([4, 5, 6, 7], True), [n2b, L("tev")])
                tev = add_task("act", evac_mT, [t1, t2, L("m1b")], "tev")
                m0a = add_task("pe", op_mm(0, [0, 1, 2, 3]), [tev, L("r1")])
                m0b = add_task("pe", op_mm(0, [4, 5, 6, 7]), [tev, L("r1")])
                r0 = add_task("dve", add_res(0), [m0b, L("fin2")])
                m1a = add_task("pe", op_mm(1, [0, 1, 2, 3]), [r0])
                m1b = add_task("pe", op_mm(1, [4, 5, 6, 7]), [r0], "m1b")
                r1 = add_task("dve", add_res(1), [m1b], "r1")
                last_ids["fin2"] = last_ids.get("fin1")
                add_task("act", fin, [r1], "fin1")

            def drain_tasks():
                force_tasks(lambda t_: True)

            def scan_chunk(c):
                tt, hf = c // 2, c % 2
                r0 = 64 * hf
                rs = slice(r0, r0 + 64)
                cs = slice(c * 64, (c + 1) * 64)
                pv = ps[0]
                po = ps[2 + (tt % 2)]
                pS = ps[1]
                for h in range(4):
                    P.op("pe", lambda e, h=h: e.matmul(pv.h[rs, h * 128:(h + 1) * 128], lhsT=wT.h[:, h, cs], rhs=Sb[h].h[:],
                                                      start=True, stop=True, skip_group_check=True),
                         reads=[(wT, (h, tt // 4)), (Sb[h], "all")], writes=[(pv, "all")])
                for h in range(4):
                    P.op("pe", lambda e, h=h: e.matmul(po.h[rs, h * 128:(h + 1) * 128], lhsT=qgT.h[:, h, cs], rhs=Sb[h].h[:],
                                                      start=(h == 0), stop=False, skip_group_check=True),
                         reads=[(qgT, (h, tt // 4)), (Sb[h], "all")], writes=[(po, "all")])
                pump("pe", 2)
                yield
                P.op("dve", lambda e: e.tensor_tensor(out=vnwA.h[rs, :, :].rearrange("p a b -> p (a b)"),
                                                      in0=uB.h[rs, tt, :, :].rearrange("p a b -> p (a b)"),
                                                      in1=pv.h[rs, :], op=ALU.subtract),
                     reads=[(uB, (h, tt // 4)) for h in range(4)] + [(pv, "all")], writes=[(vnwA, hf)])
                pump("dve", 1)
                yield
                for h in range(4):
                    P.op("pe", lambda e, h=h: e.matmul(pS.h[:, h * 128:(h + 1) * 128], lhsT=kdB.h[rs, tt, h, :],
                                                      rhs=vnwA.h[rs, h, :], start=True, stop=True, skip_group_check=True),
                         reads=[(kdB, (h, tt)), (vnwA, hf)], writes=[(pS, "all")])
                for h in range(4):
                    P.op("pe", lambda e, h=h: e.matmul(po.h[rs, h * 128:(h + 1) * 128], lhsT=aiT.h[rs, tt, h, :],
                                                      rhs=vnwA.h[rs, h, :], start=False, stop=True, skip_group_check=True),
                         reads=[(aiT, (h, tt // 4, hf)), (vnwA, hf)], writes=[(po, "all")])
                pump("pe", 1)
                yield
                for h in range(4):
                    P.op("dve", lambda e, h=h: e.scalar_tensor_tensor(
                        out=Sb[h].h[:], in0=Sf[h].h[:], scalar=eglB.h[:, h, c:c + 1], in1=pS.h[:, h * 128:(h + 1) * 128],
                        op0=ALU.mult, op1=ALU.add),
                        reads=[(eglB, (h, tt // 4)), (pS, "all"), (Sf[h], "all")], writes=[(Sb[h], "all")])
                for h in range(4):
                    P.op("dve", lambda e, h=h: e.scalar_tensor_tensor(
                        out=Sf[h].h[:], in0=Sf[h].h[:], scalar=eglB.h[:, h, c:c + 1], in1=pS.h[:, h * 128:(h + 1) * 128],
                        op0=ALU.mult, op1=ALU.add),
                        reads=[(eglB, (h, tt // 4)), (pS, "all")], writes=[(Sf[h], "all")])
                if hf == 1:
                    if "oG" in dbg:
                        P.op("act", lambda e: e.copy(out=oG.h[:, 0, :], in_=po.h[:]),
                             reads=[(po, "all")], writes=[(oG, "all")])
                        d = nc.dram_tensor("dbg_oG%d" % tt, [128, 512], F32, kind="ExternalOutput").ap()
                        P.dma("sp", lambda e: e.dma_start(out=d, in_=oG.h[:, 0, :]), reads=[(oG, "all")], final=True)
                    force_tasks(lambda t_: t_.get("is_n2", False))
                    for h in range(4):
                        P.op("act", lambda e, h=h: e.activation(out=sqg.h[:, 0, :],
                                                              in_=po.h[:, h * 128:(h + 1) * 128], func=AF.Square,
                                                              accum_out=ssg.h[:, h:h + 1]),
                             reads=[(po, "all")], writes=[(sqg, "all"), (ssg, "all")])
                    rsqrt_small(ssg.h[:, 0:4], ssg.h[:, 0:4], 1.0 / 128, [], (ssg, "all"))
                    add_tile_tasks(tt, po)
                pump("dve", 1)

            if stage >= 3:
                pending = list(jobs)
                slots = [None] * NCH
                slot_job = [None] * NCH
                STAGGER = 7
                start_round = [i * STAGGER for i in range(NCH)]
                done_tq = [0, 0, 0, 0]
                scan_gen = [None]
                next_chunk = [0]

                def ready_upto():
                    k = 0
                    while k < 4 and done_tq[k] == 4:
                        k += 1
                    return 8 * k

                def scan_step():
                    if scan_gen[0] is None:
                        if next_chunk[0] >= ready_upto():
                            return False
                        scan_gen[0] = scan_chunk(next_chunk[0])
                        next_chunk[0] += 1
                    try:
                        next(scan_gen[0])
                    except StopIteration:
                        scan_gen[0] = None
                    return True

                rnd = 0
                while pending or any(g_ is not None for g_ in slots):
                    for ci in range(NCH):
                        if slots[ci] is None and pending and rnd >= start_round[ci]:
                            h_, tq_ = pending.pop(0)
                            slots[ci] = chain_steps(h_, tq_, ci)
                            slot_job[ci] = tq_
                        if slots[ci] is not None:
                            try:
                                next(slots[ci])
                            except StopIteration:
                                slots[ci] = None
                                done_tq[slot_job[ci]] += 1
                    scan_step()
                    rnd += 1
                P.barrier()
                s7.close()
                fnw_bc = g6("fnw_bc", [128, D], F32)
                P.dma("sp", lambda e: e.dma_start(out=fnw_bc.h[:], in_=final_norm_w.partition_broadcast(128)),
                      writes=[(fnw_bc, "all")])
                wo_bf = g6("wo_bf", [128, DC, D], BF16)
                mT = [g6("mT0", [128, DC, 128], BF16)] * 2
                xr = [g6("xr%d" % i, [128, D], F32) for i in range(2)]
                x2 = [g6("x2%d" % i, [128, D], F32) for i in range(2)]
                ssq2 = g6("ssq2", [128, NT], F32)
                junkT = g6("junkT", [128, D], BF16)
                wo_view = w_out.rearrange("(c p) e -> p c e", p=128)
                for hf in range(2):
                    P.dma("pool", lambda e, hf=hf: e.dma_start(out=wo_bf.h[:, :, hf * 512:(hf + 1) * 512],
                                                             in_=wo_view[:, :, hf * 512:(hf + 1) * 512]),
                          writes=[(wo_bf, (dc, hf)) for dc in range(DC)])


                tail_load(0)
                tail_load(1)
                while next_chunk[0] < 32 or scan_gen[0] is not None:
                    scan_step()
                drain_tasks()
                if DENSE_TAIL:
                    mTd = [mT[0], oG]

                    def mT_ap(i):
                        if i == 0:
                            return mT[0].h[:]
                        return oG.h[:, 0, :].bitcast(BF16).rearrange("p (c t) -> p c t", c=DC)

                    def TA(tt):
                        pb = ps[tt % 2]
                        pbf = pb.h[:].bitcast(BF16)
                        for ec in range(DC):
                            P.op("pe", lambda e, ec=ec: e.transpose(out=pbf[:, ec * 128:(ec + 1) * 128],
                                                                   in_=mg.h[:, tt, ec * 128:(ec + 1) * 128],
                                                                   identity=ident_bf.h[:]),
                                 reads=[(mg, (tt, ec // 4)), (ident_bf, "all")], writes=[(pb, "all")])
                        src = pbf.rearrange("p (c t) -> p c t", c=DC)
                        mb = mTd[tt % 2]
                        P.op("act", lambda e: e.copy(out=mT_ap(tt % 2), in_=src), reads=[(pb, "all")], writes=[(mb, "all")])

                    def TB(tt):
                        b = tt % 2
                        mb = mTd[tt % 2]
                        for hf in range(2):
                            po2 = ps[2 + (2 * tt + hf) % 4]
                            for ec in range(DC):
                                P.op("pe", lambda e, ec=ec, hf=hf, po2=po2: e.matmul(
                                    po2.h[:], lhsT=mT_ap(tt % 2)[:, ec, :], rhs=wo_bf.h[:, ec, hf * 512:(hf + 1) * 512],
                                    start=(ec == 0), stop=(ec == DC - 1)),
                                    reads=[(mb, "all"), (wo_bf, (ec, hf))], writes=[(po2, "all")])
                            P.op("dve", lambda e, hf=hf, po2=po2: e.tensor_tensor(
                                out=x2[b].h[:, hf * 512:(hf + 1) * 512], in0=po2.h[:],
                                in1=xr[b].h[:, hf * 512:(hf + 1) * 512], op=ALU.add),
                                reads=[(po2, "all"), (xr[b], "all")], writes=[(x2[b], hf)])

                    def TC(tt):
                        b = tt % 2
                        P.op("act", lambda e: e.activation(out=junkT.h[:], in_=x2[b].h[:], func=AF.Square,
                                                           accum_out=ssq2.h[:, tt:tt + 1]),
                             reads=[(x2[b], 0), (x2[b], 1)], writes=[(junkT, "all"), (ssq2, tt)])
                        rsqrt_small(ssq2.h[:, tt:tt + 1], ssq2.h[:, tt:tt + 1], 1.0 / D, [], (ssq2, tt))
                        P.op("act", lambda e: e.activation(out=x2[b].h[:], in_=x2[b].h[:], func=AF.Copy,
                                                           scale=ssq2.h[:, tt:tt + 1]),
                             reads=[(ssq2, tt)], writes=[(x2[b], 0), (x2[b], 1)])
                        P.op("pool", lambda e: e.tensor_tensor(out=x2[b].h[:], in0=x2[b].h[:], in1=fnw_bc.h[:], op=ALU.mult),
                             reads=[(fnw_bc, "all")], writes=[(x2[b], 0), (x2[b], 1)])
                        P.dma("sp", lambda e: e.dma_start(out=y[tt * 128:(tt + 1) * 128, :], in_=x2[b].h[:]),
                              reads=[(x2[b], 0), (x2[b], 1)], final=True)
                        if tt + 2 < NT:
                            tail_load(tt + 2)

                    for t in range(NT + 2):
                        if t < NT:
                            TA(t)
                        if 0 <= t - 1 < NT:
                            TB(t - 1)
                        if 0 <= t - 2 < NT:
                            TC(t - 2)
            else:
                pending = list(jobs)
                slots = [None] * NCH
                STAGGER = 7
                start_round = [i * STAGGER for i in range(NCH)]
                rnd = 0
                while pending or any(g_ is not None for g_ in slots):
                    for ci in range(NCH):
                        if slots[ci] is None and pending and rnd >= start_round[ci]:
                            h_, tq_ = pending.pop(0)
                            slots[ci] = chain_steps(h_, tq_, ci)
                        if slots[ci] is not None:
                            try:
                                next(slots[ci])
                            except StopIteration:
                                slots[ci] = None
                    rnd += 1

                P.barrier()
                s7.close()
                fnw_bc = g6("fnw_bc", [128, D], F32)
                P.dma("sp", lambda e: e.dma_start(out=fnw_bc.h[:], in_=final_norm_w.partition_broadcast(128)),
                      writes=[(fnw_bc, "all")])
                wo_bf = g6("wo_bf", [128, DC, D], BF16)
                mT = [g6("mT0", [128, DC, 128], BF16)] * 2
                xr = [g6("xr%d" % i, [128, D], F32) for i in range(2)]
                x2 = [g6("x2%d" % i, [128, D], F32) for i in range(2)]
                ssq2 = g6("ssq2", [128, NT], F32)
                junkT = g6("junkT", [128, D], BF16)
                wo_view = w_out.rearrange("(c p) e -> p c e", p=128)
                for hf in range(2):
                    P.dma("pool", lambda e, hf=hf: e.dma_start(out=wo_bf.h[:, :, hf * 512:(hf + 1) * 512],
                                                             in_=wo_view[:, :, hf * 512:(hf + 1) * 512]),
                          writes=[(wo_bf, (dc, hf)) for dc in range(DC)])


                for tt in range(NT):
                    P.op("pool", lambda e, tt=tt: e.memset(mg.h[:, tt, 512:1024], 0.0), writes=[(mg, (tt, 1))])
                tail_load(0)
                tail_load(1)
                for tt in range(NT):
                    add_tile_tasks(tt, None)
                    drain_tasks()
            dbg_dump("mg", mg, [(tt, hf) for tt in range(NT) for hf in range(2)], [128, NT, 1024], BF16)

        P.emit(st)
        if stage >= 2.5:
            s6.close()
    return nc, dbg_out


def kernel(**inputs):
    nc, _ = build()
    in_maps = _in_maps(inputs)
    res = run_bass_kernel_spmd(nc, in_maps, core_ids=list(range(8)))
    return np.stack([r["y"] for r in res.results], axis=0)


def _in_maps(inputs):
    f = lambda a: np.ascontiguousarray(np.asarray(a, dtype=np.float32))
    maps = []
    for b in range(8):
        maps.append({
            "x": f(inputs["x"][b]),
            "norm1_w": f(inputs["norm1_w"][0]),
            "w_in": f(inputs["w_in"][0]),
            "sb_norm_w": f(inputs["sb_norm_w"][0]),
            "gdn_conv_w": f(inputs["gdn_conv_w"][0]),
            "gdn_A_log": f(inputs["gdn_A_log"][0]),
            "gdn_dt_bias": f(inputs["gdn_dt_bias"][0]),
            "gdn_norm_w": f(inputs["gdn_norm_w"][0]),
            "w_out": f(inputs["w_out"][0]),
            "final_norm_w": f(inputs["final_norm_w"]),
        })
    return maps
```

```python
import contextlib
import numpy as np
import concourse.bass as bass
import concourse.mybir as mybir
from concourse.bass_utils import run_bass_kernel_spmd

F32 = mybir.dt.float32
BF16 = mybir.dt.bfloat16
AF = mybir.ActivationFunctionType
ALU = mybir.AluOpType
AX = mybir.AxisListType

T = 2048
D = 1024
DIN = 4104
NT = 16
DC = 8
EPS = 1e-6
NEG = -30000.0

ENGS = ("pe", "act", "dve", "pool", "sp")


class Buf:
    def __init__(self, handle, name, psum=False):
        self.h = handle
        self.name = name
        self.psum = psum
        self.w = {}
        self.r = {}

    def __getitem__(self, idx):
        return self.h[idx]


class Prog:
    def __init__(self, nc, n_dma_sems=32):
        self.nc = nc
        self.ops = {e: [] for e in ENGS}
        self.cnt = {e: 0 for e in ENGS}
        self.known = {e: {} for e in ENGS}
        self.dma_pool = n_dma_sems
        self.dma_idx = 0
        self.dma_val = [0] * n_dma_sems
        self.final_tokens = []
        self.n_sw = 0

    def _deps(self, reads, writes, eng=None):
        deps = []
        for (b, k) in list(reads) + list(writes):
            t = b.w.get(k)
            if t is not None:
                deps.append(t)
        for (b, k) in reads:
            if b.psum:
                deps.extend(r for r in b.r.get(k, []) if r[0] != eng)
        for (b, k) in writes:
            deps.extend(b.r.get(k, []))
        return deps

    def _commit(self, tok, reads, writes):
        for (b, k) in reads:
            b.r.setdefault(k, []).append(tok)
        for (b, k) in writes:
            b.w[k] = tok
            b.r[k] = []

    def _waits_for(self, eng, deps, skip_same_pe=True):
        need = {}
        for (sk, val) in deps:
            if sk == eng and eng == "pe" and skip_same_pe:
                continue
            if val > need.get(sk, 0):
                need[sk] = val
        out = []
        kn = self.known[eng]
        for sk, val in need.items():
            if kn.get(sk, 0) >= val:
                continue
            kn[sk] = val
            out.append((sk, val))
        return out

    def op(self, eng, fn, reads=(), writes=(), extra=()):
        deps = self._deps(reads, writes, eng) + list(extra)
        waits = self._waits_for(eng, deps)
        self.cnt[eng] += 1
        tok = (eng, self.cnt[eng])
        self.ops[eng].append((waits, fn, ("eng", eng)))
        self._commit(tok, reads, writes)
        return tok

    def dma(self, queue, fn, reads=(), writes=(), final=False):
        deps = self._deps(reads, writes, queue)
        if queue == "pool":
            semkey = ("sw", self.n_sw)
            self.n_sw += 1
            waits = self._waits_for(queue, deps)
            tok = (semkey, 16)
            self.ops[queue].append((waits, fn, semkey))
            self._commit(tok, reads, writes)
            if final:
                self.final_tokens.append(tok)
            return tok
        i = self.dma_idx % self.dma_pool
        self.dma_idx += 1
        semkey = ("dma", i)
        if self.dma_val[i] > 0:
            deps.append((semkey, self.dma_val[i]))
        waits = self._waits_for(queue, deps)
        self.dma_val[i] += 16
        tok = (semkey, self.dma_val[i])
        self.ops[queue].append((waits, fn, ("dma", i)))
        self._commit(tok, reads, writes)
        if final:
            self.final_tokens.append(tok)
        return tok

    def barrier(self):
        toks = [(e, self.cnt[e]) for e in ENGS if self.cnt[e] > 0]
        for i in range(self.dma_pool):
            if self.dma_val[i] > 0:
                toks.append((("dma", i), self.dma_val[i]))
        for i in range(self.n_sw):
            toks.append((("sw", i), 16))
        for e in ENGS:
            waits = self._waits_for(e, toks, skip_same_pe=False)
            if waits:
                self.ops[e].append((waits, None, None))

    def emit(self, stack):
        nc = self.nc
        sems = {}
        for e in ENGS:
            sems[e] = stack.enter_context(nc.semaphore("s_" + e))
        for i in range(self.dma_pool):
            sems[("dma", i)] = stack.enter_context(nc.semaphore("d%d" % i))
        for i in range(self.n_sw):
            sems[("sw", i)] = stack.enter_context(nc.semaphore("w%d" % i))
        fw = self._waits_for("sp", self.final_tokens)
        if fw:
            self.ops["sp"].append((fw, None, None))
        block = stack.enter_context(nc.Block())

        def make(e):
            def body(eng):
                for (waits, fn, inc) in self.ops[e]:
                    for (sk, val) in waits:
                        eng.wait_ge(sems[sk], val)
                    if fn is None:
                        continue
                    ins = fn(eng)
                    if inc[0] == "eng":
                        ins.then_inc(sems[inc[1]], 1)
                    else:
                        ins.then_inc(sems[inc], 16)
            return body

        block.tensor(make("pe"))
        block.scalar(make("act"))
        block.vector(make("dve"))
        block.gpsimd(make("pool"))
        block.sync(make("sp"))


def build(dbg=None, stage=99, cut=99):
    dbg = dbg or {}
    nc = bass.Bass("TRN2", target_bir_lowering=False)
    x = nc.dram_tensor("x", [T, D], F32, kind="ExternalInput").ap()
    norm1_w = nc.dram_tensor("norm1_w", [D], F32, kind="ExternalInput").ap()
    w_in = nc.dram_tensor("w_in", [D, DIN], F32, kind="ExternalInput").ap()
    sb_norm_w = nc.dram_tensor("sb_norm_w", [64], F32, kind="ExternalInput").ap()
    conv_w = nc.dram_tensor("gdn_conv_w", [4, 1536], F32, kind="ExternalInput").ap()
    A_log = nc.dram_tensor("gdn_A_log", [4], F32, kind="ExternalInput").ap()
    dt_bias = nc.dram_tensor("gdn_dt_bias", [4], F32, kind="ExternalInput").ap()
    gdn_norm_w = nc.dram_tensor("gdn_norm_w", [128], F32, kind="ExternalInput").ap()
    w_out = nc.dram_tensor("w_out", [D, D], F32, kind="ExternalInput").ap()
    final_norm_w = nc.dram_tensor("final_norm_w", [D], F32, kind="ExternalInput").ap()
    y = nc.dram_tensor("y", [T, D], F32, kind="ExternalOutput").ap()
    dbg_out = {}

    P = Prog(nc)
    with contextlib.ExitStack() as st:
        def sbuf(stack, name, shape, dt=F32):
            return Buf(stack.enter_context(nc.sbuf_tensor(name, list(shape), dt)), name)

        def sb(name, shape, dt=F32):
            return sbuf(st, name, shape, dt)

        def dbg_dump(name, buf, key, shape, dt=F32, src=None):
            if name not in dbg:
                return
            d = nc.dram_tensor("dbg_" + name, list(shape), dt, kind="ExternalOutput").ap()
            dbg_out[name] = d
            keys = key if isinstance(key, list) else [key]
            P.dma("sp", lambda e: e.dma_start(out=d, in_=buf.h[:] if src is None else src),
                  reads=[(buf, k) for k in keys], final=True)

        psq = [st.enter_context(nc.psum_tensor("pq%d" % i, [128, 1024], F32)) for i in range(4)]

        class HView:
            def __init__(self, h, off):
                self.h_, self.off = h, off

            def __getitem__(self, idx):
                if not isinstance(idx, tuple):
                    idx = (idx, slice(None))
                p_, f_ = idx
                a = 0 if f_.start is None else f_.start
                b = 512 if f_.stop is None else f_.stop
                return self.h_[p_, self.off + a:self.off + b]

        ps = [Buf(HView(psq[i // 2], (i % 2) * 512), "ps%d" % i, psum=True) for i in range(8)]

        def act_fn(out, in_, func, reads, writes, **kw):
            return P.op("act", lambda e: e.activation(out=out, in_=in_, func=func, **kw), reads=reads, writes=writes)

        def rsqrt_small(dst_ap, src_ap, scale, reads, wkey):
            P.op("act", lambda e: e.activation(out=dst_ap, in_=src_ap, func=AF.Ln, scale=scale, bias=eps_c.h[:, 0:1]),
                 reads=list(reads) + [(eps_c, "all")], writes=[wkey])
            P.op("act", lambda e: e.activation(out=dst_ap, in_=dst_ap, func=AF.Exp, scale=-0.5), writes=[wkey])

        eps_c = sb("eps_c", [128, 1], F32)
        P.op("pool", lambda e: e.memset(eps_c.h[:], EPS), writes=[(eps_c, "all")])
        ident_bf = sb("ident_bf", [128, 128], BF16)
        ident_f = sb("ident_f", [128, 128], F32)
        negU = sb("negU", [128, 128], BF16)
        negOnes = sb("negOnes", [128, 128], BF16)
        ones_bf = sb("ones_bf", [128, 128], BF16)
        n1w = sb("n1w", [128, DC], F32)
        ssq = sb("ssq", [128, NT], F32)
        rstd = sb("rstd", [128, NT], F32)
        sbw_bc = sb("sbw_bc", [128, 64], F32)
        gnw_bc = sb("gnw_bc", [128, 128], F32)
        cw = sb("cw", [128, 12, 4], F32)
        mg = sb("mg", [128, NT, 1024], BF16)
        gqkv = sb("gqkv", [128, 12, T], BF16)
        bd = sb("bd", [128, NT, 8], F32)

        def const_tri(buf, val, cmp, fill=0.0, pattern=None, base=0, cm=1, sl=None):
            ap = buf.h[:] if sl is None else sl
            P.op("pool", lambda e: e.memset(ap, val), writes=[(buf, "all")])
            P.op("pool", lambda e: e.affine_select(out=ap, in_=ap, compare_op=cmp, fill=fill,
                                                   base=base, pattern=pattern,
                                                   channel_multiplier=cm),
                 writes=[(buf, "all")])

        const_tri(ident_bf, 1.0, ALU.is_equal, pattern=[[1, 128]], cm=-1)
        const_tri(ident_f, 1.0, ALU.is_equal, pattern=[[1, 128]], cm=-1)
        const_tri(negU, -1.0, ALU.is_ge, pattern=[[-1, 128]], cm=1)
        P.op("pool", lambda e: e.memset(negOnes.h[:], -1.0), writes=[(negOnes, "all")])
        P.op("pool", lambda e: e.memset(ones_bf.h[:], 1.0), writes=[(ones_bf, "all")])

        vrow = sb("vrow", [56, 128], F32)
        P.dma("act", lambda e: e.dma_start(out=vrow.h[0:8, :], in_=norm1_w.rearrange("(c p) -> c p", p=128)),
              writes=[(vrow, "a")])
        P.dma("act", lambda e: e.dma_start(out=vrow.h[8:56, :], in_=conv_w.rearrange("i (c p) -> (i c) p", p=128)),
              writes=[(vrow, "b")])
        P.op("pe", lambda e: e.transpose(out=ps[7].h[:, 0:56], in_=vrow.h[:, :], identity=ident_f.h[0:56, 0:56]),
             reads=[(vrow, "a"), (vrow, "b"), (ident_f, "all")], writes=[(ps[7], "all")])
        P.op("dve", lambda e: e.tensor_copy(out=n1w.h[:], in_=ps[7].h[:, 0:8]), reads=[(ps[7], "all")], writes=[(n1w, "all")])
        P.op("dve", lambda e: e.tensor_copy(out=cw.h[:].rearrange("p c i -> p i c"),
                                            in_=ps[7].h[:, 8:56].rearrange("p (i c) -> p i c", i=4)),
             reads=[(ps[7], "all")], writes=[(cw, "all")])
        P.dma("act", lambda e: e.dma_start(out=sbw_bc.h[:], in_=sb_norm_w.partition_broadcast(128)),
              writes=[(sbw_bc, "all")])
        P.dma("act", lambda e: e.dma_start(out=gnw_bc.h[:], in_=gdn_norm_w.partition_broadcast(128)),
              writes=[(gnw_bc, "all")])

        s1 = contextlib.ExitStack()
        qT = sbuf(s1, "qT", [128, 4, T], BF16)
        kT = sbuf(s1, "kT", [128, 4, T], BF16)
        vS = sbuf(s1, "vS", [128, NT, 512], BF16)

        s2 = contextlib.ExitStack()
        hT = sbuf(s2, "hT", [128, DC, T], BF16)
        s3 = contextlib.ExitStack()
        xs = [sbuf(s3, "xs%d" % i, [128, D], F32) for i in range(3)]
        xn = [sbuf(s3, "xn%d" % i, [128, D], BF16) for i in range(3)]
        junk = sbuf(s3, "junk", [128, D], BF16)

        def A1(tt):
            b = tt % 3
            P.dma("sp", lambda e: e.dma_start(out=xs[b].h[:], in_=x[tt * 128:(tt + 1) * 128, :]),
                  writes=[(xs[b], "all")])
            P.op("act", lambda e: e.activation(out=junk.h[:], in_=xs[b].h[:], func=AF.Square,
                                               accum_out=ssq.h[:, tt:tt + 1]),
                 reads=[(xs[b], "all")], writes=[(junk, "all"), (ssq, tt)])
            rsqrt_small(rstd.h[:, tt:tt + 1], ssq.h[:, tt:tt + 1], 1.0 / D, [(ssq, tt)], (rstd, tt))
            P.op("dve", lambda e: e.tensor_scalar(out=xn[b].h[:], in0=xs[b].h[:],
                                                  scalar1=rstd.h[:, tt:tt + 1], scalar2=None, op0=ALU.mult),
                 reads=[(xs[b], "all"), (rstd, tt)], writes=[(xn[b], "all")])

        def A2(tt):
            b = tt % 3
            pb = ps[tt % 2]
            pbf = pb.h[:].bitcast(BF16)
            for dc in range(DC):
                P.op("pe", lambda e, dc=dc: e.transpose(out=pbf[:, dc * 128:(dc + 1) * 128],
                                                       in_=xn[b].h[:, dc * 128:(dc + 1) * 128],
                                                       identity=ident_bf.h[:]),
                     reads=[(xn[b], "all"), (ident_bf, "all")], writes=[(pb, "all")])
            src = pbf.rearrange("p (c t) -> p c t", c=DC)
            P.op("dve", lambda e: e.tensor_tensor(
                out=hT.h[:, :, tt * 128:(tt + 1) * 128], in0=src,
                in1=n1w.h[:].unsqueeze(2).to_broadcast([128, DC, 128]), op=ALU.mult),
                reads=[(pb, "all"), (n1w, "all")], writes=[(hT, tt)])

        for t in range(NT + 1):
            if t < NT:
                A1(t)
            if t >= 1:
                A2(t - 1)
        HT_ALL = [(hT, tt) for tt in range(NT)]
        P.barrier()
        s3.close()

        wbf = [sbuf(s2, "wbf%d" % i, [128, DC, 512], BF16) for i in range(3)]
        cin = [sbuf(s2, "cin%d" % i, [128, T + 4], BF16) for i in range(2)]
        dgw = [sbuf(s2, "dgw%d" % i, [128, 4, 128], BF16) for i in range(2)]
        w_view = w_in.rearrange("(c p) e -> p c e", p=128)
        gcount = [0]

        def load_group(col0, ncols):
            b = gcount[0] % 3
            gcount[0] += 1
            P.dma("pool", lambda e: e.dma_start(out=wbf[b].h[:, :, 0:ncols], in_=w_view[:, :, col0:col0 + ncols]),
                  writes=[(wbf[b], "all")])
            return wbf[b]

        pj_banks = [2, 3, 4, 5]
        pj_i = [0]

        def proj_feat(wb, mc, tg, evac):
            pb = ps[pj_banks[pj_i[0] % 4]]
            pj_i[0] += 1
            for dc in range(DC):
                P.op("pe", lambda e, dc=dc, pb=pb: e.matmul(pb.h[:], lhsT=wb.h[:, dc, mc * 128:(mc + 1) * 128],
                                                         rhs=hT.h[:, dc, tg * 512:(tg + 1) * 512],
                                                         start=(dc == 0), stop=(dc == DC - 1)),
                     reads=[(wb, "all")] + HT_ALL[tg * 4:(tg + 1) * 4], writes=[(pb, "all")])
            evac(pb)

        def proj_tok(wb, tt, evac, ncols=512):
            pb = ps[pj_banks[pj_i[0] % 4]]
            pj_i[0] += 1
            for dc in range(DC):
                P.op("pe", lambda e, dc=dc, pb=pb: e.matmul(pb.h[:, 0:ncols], lhsT=hT.h[:, dc, tt * 128:(tt + 1) * 128],
                                                         rhs=wb.h[:, dc, 0:ncols],
                                                         start=(dc == 0), stop=(dc == DC - 1)),
                     reads=[(wb, "all"), (hT, tt)], writes=[(pb, "all")])
            evac(pb)

        ev_i = [0]

        def evac_copy(dst_ap, dst_rw, scale=None, ncols=512):
            def f(pb):
                ev_i[0] += 1
                src = pb.h[:, 0:ncols]
                if ev_i[0] % 2 == 0:
                    if scale is None:
                        P.op("act", lambda e: e.copy(out=dst_ap, in_=src), reads=[(pb, "all")], writes=dst_rw)
                    else:
                        P.op("act", lambda e: e.activation(out=dst_ap, in_=src, func=AF.Copy, scale=scale),
                             reads=[(pb, "all")], writes=dst_rw)
                else:
                    if scale is None:
                        P.op("dve", lambda e: e.tensor_copy(out=dst_ap, in_=src), reads=[(pb, "all")], writes=dst_rw)
                    else:
                        P.op("dve", lambda e: e.tensor_scalar(out=dst_ap, in0=src, scalar1=scale, scalar2=None,
                                                            op0=ALU.mult), reads=[(pb, "all")], writes=dst_rw)
            return f

        def gate_group(col0, half):
            wb = load_group(col0, 512)
            for tt in range(NT):
                def ev(pb, tt=tt):
                    P.op("act", lambda e: e.activation(out=mg.h[:, tt, half * 512:(half + 1) * 512], in_=pb.h[:],
                                                       func=AF.Silu),
                         reads=[(pb, "all")], writes=[(mg, (tt, half))])
                proj_tok(wb, tt, ev)

        for b2 in range(2):
            P.op("pool", lambda e, b2=b2: e.memset(cin[b2].h[:, 0:4], 0.0), writes=[(cin[b2], "pad")])
        cci = [0]

        PJB = [2, 3, 4]
        CVB = [5, 6]
        ONB = [7, 0]
        cnt3 = {"pj": 0, "cv": 0, "on": 0, "st": 0}

        class Ch:
            pass

        chunks = []
        for gi in range(3):
            for mc in range(4):
                c_ = Ch()
                c_.gi, c_.mc, c_.cc = gi, mc, gi * 4 + mc
                c_.ci = cin[len(chunks) % 2]
                c_.dg = dgw[len(chunks) % 2]
                c_.pbs = {}
                c_.sbufs = {}
                chunks.append(c_)
        wbs = {}

        def st_proj(c_, tg):
            if c_.gi not in wbs:
                wbs[c_.gi] = load_group(2048 + 512 * c_.gi, 512)
            wb_ = wbs[c_.gi]
            if tg == 0:
                for i in range(4):
                    P.op("dve", lambda e, i=i: e.tensor_scalar(out=c_.dg.h[:, i, :], in0=ident_f.h[:],
                                                             scalar1=cw.h[:, c_.cc, i:i + 1], scalar2=None, op0=ALU.mult),
                         reads=[(ident_f, "all"), (cw, "all")], writes=[(c_.dg, i)])
            pb = ps[PJB[cnt3["pj"] % 3]]
            cnt3["pj"] += 1
            for dc in range(DC):
                P.op("pe", lambda e, dc=dc: e.matmul(pb.h[:], lhsT=wb_.h[:, dc, c_.mc * 128:(c_.mc + 1) * 128],
                                                    rhs=hT.h[:, dc, tg * 512:(tg + 1) * 512],
                                                    start=(dc == 0), stop=(dc == DC - 1)),
                     reads=[(wb_, "all")] + HT_ALL[tg * 4:(tg + 1) * 4], writes=[(pb, "all")])
            evac_copy(c_.ci.h[:, 4 + tg * 512:4 + (tg + 1) * 512], [(c_.ci, tg)])(pb)

        def st_conv(c_, tg):
            ci, dg = c_.ci, c_.dg
            pb = ps[CVB[cnt3["cv"] % 2]]
            cnt3["cv"] += 1
            rd = [(ci, tg)] + ([(ci, tg - 1)] if tg > 0 else [(ci, "pad")])
            for i in range(4):
                P.op("pe", lambda e, i=i: e.matmul(
                    pb.h[:], lhsT=dg.h[:, i, :], rhs=ci.h[:, 1 + i + tg * 512:1 + i + (tg + 1) * 512],
                    start=(i == 0), stop=(i == 3)),
                    reads=rd + [(dg, i)], writes=[(pb, "all")])
            dst = gqkv.h[:, c_.cc, tg * 512:(tg + 1) * 512]
            act_fn(dst, pb.h[:], AF.Silu, [(pb, "all")], [(gqkv, (c_.cc, tg))])

        def st_ones(c_, tg):
            return

        NCHK = len(chunks)
        for i in range(NCHK + 1):
            cur = chunks[i] if i < NCHK else None
            prv = chunks[i - 1] if i >= 1 else None
            for tg in range(4):
                if cur is not None:
                    st_proj(cur, tg)
                if prv is not None:
                    st_conv(prv, tg)
                    if tg >= 1:
                        st_ones(prv, tg - 1)
            if prv is not None:
                st_ones(prv, 3)
        gate_group(3584, 1)
        wb = load_group(4096, 8)
        for tt in range(NT):
            proj_tok(wb, tt, evac_copy(bd.h[:, tt, :], [(bd, tt)], ncols=8), ncols=8)

        gate_group(1536, 0)

        sqn = [sbuf(s2, "sqn%d" % i, [128, 512], BF16) for i in range(3)]
        rn = [sbuf(s2, "rn%d" % i, [128, 512], F32) for i in range(2)]
        l2jobs = [(cc, tg) for cc in range(8) for tg in range(4)]
        L2B = [6, 7, 0, 1]

        def l2_s1(i):
            cc, tg = l2jobs[i]
            q_ = sqn[i % 3]
            src = gqkv.h[:, cc, tg * 512:(tg + 1) * 512]
            P.op("dve", lambda e: e.tensor_tensor(out=q_.h[:], in0=src, in1=src, op=ALU.mult),
                 reads=[(gqkv, (cc, tg))], writes=[(q_, "all")])

        def l2_s1b(i):
            q_ = sqn[i % 3]
            pb_ = ps[L2B[i % 4]]
            P.op("pe", lambda e: e.matmul(pb_.h[:], lhsT=ones_bf.h[:], rhs=q_.h[:], start=True, stop=True),
                 reads=[(ones_bf, "all"), (q_, "all")], writes=[(pb_, "all")])

        def l2_s2(i):
            pb_ = ps[L2B[i % 4]]
            r_ = rn[i % 2]
            act_fn(r_.h[:], pb_.h[:], AF.Ln, [(pb_, "all"), (eps_c, "all")], [(r_, "all")], bias=eps_c.h[:, 0:1])
            act_fn(r_.h[:], r_.h[:], AF.Exp, [], [(r_, "all")], scale=-0.5)

        def l2_s3(i):
            cc, tg = l2jobs[i]
            r_ = rn[i % 2]
            dst = gqkv.h[:, cc, tg * 512:(tg + 1) * 512]
            sc = (128.0 ** -0.5) if cc < 4 else 1.0
            P.op("dve", lambda e: e.scalar_tensor_tensor(out=dst, in0=dst, scalar=sc, in1=r_.h[:],
                                                        op0=ALU.mult, op1=ALU.mult),
                 reads=[(r_, "all")], writes=[(gqkv, (cc, tg))])

        l2t = [0]

        def l2_step():
            t_ = l2t[0]
            if t_ >= len(l2jobs) + 3:
                return
            l2t[0] += 1
            if 0 <= t_ - 1 < len(l2jobs):
                l2_s1b(t_ - 1)
            if t_ < len(l2jobs):
                l2_s1(t_)
            if 0 <= t_ - 2 < len(l2jobs):
                l2_s2(t_ - 2)
            if 0 <= t_ - 3 < len(l2jobs):
                l2_s3(t_ - 3)

        wb = load_group(0, 512)
        for mc in range(4):
            for tg in range(4):
                proj_feat(wb, mc, tg, evac_copy(qT.h[:, mc, tg * 512:(tg + 1) * 512], [(qT, (mc, tg))], scale=0.125))
                l2_step()
        wb = load_group(512, 512)
        for mc in range(4):
            for tg in range(4):
                proj_feat(wb, mc, tg, evac_copy(kT.h[:, mc, tg * 512:(tg + 1) * 512], [(kT, (mc, tg))]))
                l2_step()
        wb = load_group(1024, 512)
        for tt in range(NT):
            proj_tok(wb, tt, evac_copy(vS.h[:, tt, :], [(vS, tt)]))
            l2_step()
        while l2t[0] < len(l2jobs) + 3:
            l2_step()

        dbg_dump("qT", qT, [(mc, tg) for mc in range(4) for tg in range(4)], [128, 4, T], BF16)
        dbg_dump("gqkv", gqkv, [(cc, tg) for cc in range(12) for tg in range(4)], [128, 12, T], BF16)
        dbg_dump("bd", bd, list(range(NT)), [128, NT, 8], F32)
        P.barrier()
        s2.close()

        s4 = contextlib.ExitStack()
        maskSB = sbuf(s4, "maskSB", [128, 4, 512], BF16)
        for i in range(4):
            const_tri(maskSB, 0.0, ALU.is_gt, fill=NEG, pattern=[[1, 512]],
                      base=-128 * i, cm=-1, sl=maskSB.h[:, i, :])
        Epr = [sbuf(s4, "Ep%d" % i, [128, 2, 512], F32) for i in range(2)]
        SPp = [sbuf(s4, "SPp%d" % i, [128, 2, 512], BF16) for i in range(4)]
        SrF = [sbuf(s4, "SrF%d" % i, [128, 512], F32) for i in range(2)]
        SrB = [sbuf(s4, "SrB%d" % i, [128, 512], BF16) for i in range(3)]
        Apr = [sbuf(s4, "Ap%d" % i, [128, 2, 512], BF16) for i in range(4)]
        oSqs = [sbuf(s4, "oSq%d" % i, [128, 4, 512], F32) for i in range(2)]
        sqt = sbuf(s4, "sqt", [128, 4, 512], F32)
        ss4 = sbuf(s4, "ss4", [128, 32], F32)

        class Blk:
            pass

        pairs = []
        gidx = 0
        for qg in range(4):
            for h in range(8):
                n = 4 * (qg + 1)
                for jp in range(n // 2):
                    q_ = Blk()
                    q_.h, q_.qg, q_.n, q_.jp = h, qg, n, jp
                    q_.kb = [n - 1 - 2 * jp, n - 2 - 2 * jp]
                    q_.g = gidx
                    q_.idx = len(pairs)
                    q_.zi = q_.idx % 3
                    q_.zb = [ps[2 * q_.zi], ps[2 * q_.zi + 1]]
                    q_.E = Epr[q_.idx % 2]
                    q_.SP = SPp[q_.idx % 4]
                    q_.SrF = SrF[gidx % 2]
                    q_.SrBout = SrB[q_.idx % 3]
                    q_.SrBin = SrB[(q_.idx - 1) % 3]
                    q_.A = Apr[q_.idx % 4]
                    q_.ob = ps[6 + (gidx % 2)]
                    q_.cp = 128 * max(0, q_.kb[1] - 4 * qg)
                    q_.cpn = 128 * max(0, q_.kb[1] - 2 - 4 * qg)
                    q_.first = (jp == 0)
                    q_.last = (jp == n // 2 - 1)
                    pairs.append(q_)
                gidx += 1
        first_av = {}

        def head_norm_gate(src, nh, hd, wbc, tt0, ntt, col0, tmp, ssb, rkeys, tag, defer=None):
            width = nh * hd
            v3 = lambda ap: ap.rearrange("p a (h d) -> p (a h) d", d=hd)
            nn = ntt * nh
            half = col0 // 512
            mkeys = [(mg, (tt, half)) for tt in range(tt0, tt0 + ntt)]
            steps = [
                lambda: P.op("dve", lambda e: e.tensor_tensor(out=tmp.h[:, 0:ntt, 0:width], in0=src.h[:, 0:ntt, 0:width],
                                                            in1=src.h[:, 0:ntt, 0:width], op=ALU.mult),
                             reads=rkeys, writes=[(tmp, "all")]),
                lambda: (P.op("dve", lambda e: e.tensor_reduce(out=ssb.h[:, 0:nn], in_=v3(tmp.h[:, 0:ntt, 0:width]),
                                                             axis=AX.X, op=ALU.add),
                              reads=[(tmp, "all")], writes=[(ssb, "all")]),
                         rsqrt_small(ssb.h[:, 0:nn], ssb.h[:, 0:nn], 1.0 / hd, [], (ssb, "all"))),
                lambda: P.op("dve", lambda e: e.tensor_tensor(out=v3(tmp.h[:, 0:ntt, 0:width]), in0=v3(src.h[:, 0:ntt, 0:width]),
                                                            in1=ssb.h[:, 0:nn].unsqueeze(2).to_broadcast([128, nn, hd]),
                                                            op=ALU.mult),
                             reads=rkeys + [(ssb, "all")], writes=[(tmp, "all")]),
                lambda: P.op("dve", lambda e: e.tensor_tensor(out=v3(tmp.h[:, 0:ntt, 0:width]), in0=v3(tmp.h[:, 0:ntt, 0:width]),
                                                            in1=wbc.h[:, 0:hd].unsqueeze(1).to_broadcast([128, nn, hd]),
                                                            op=ALU.mult),
                             reads=[(wbc, "all")], writes=[(tmp, "all")]),
                lambda: P.op("dve", lambda e: e.tensor_tensor(out=mg.h[:, tt0:tt0 + ntt, col0:col0 + width],
                                                            in0=tmp.h[:, 0:ntt, 0:width],
                                                            in1=mg.h[:, tt0:tt0 + ntt, col0:col0 + width], op=ALU.mult),
                             reads=[(tmp, "all")], writes=mkeys),
            ]
            if defer is None:
                for f in steps:
                    f()
            else:
                defer.extend(steps)

        def zpair(q_):
            return psq[q_.zi][:, :].rearrange("p (b c) -> p b c", b=2)

        def ZRW(q_):
            return [(q_.zb[0], "all"), (q_.zb[1], "all")]

        def S1(q_):
            h, qg, cp = q_.h, q_.qg, q_.cp
            c, p0 = h // 2, 64 * (h % 2)
            for bi in range(2):
                kb = q_.kb[bi]
                zb = q_.zb[bi]
                diag = kb >= 4 * qg
                P.op("pe", lambda e, kb=kb, zb=zb, diag=diag: e.matmul(
                    zb.h[:, cp:], lhsT=kT.h[p0:p0 + 64, c, kb * 128:(kb + 1) * 128],
                    rhs=qT.h[p0:p0 + 64, c, qg * 512 + cp:(qg + 1) * 512], start=True, stop=not diag),
                    reads=[(qT, (c, qg)), (kT, (c, kb // 4))], writes=[(zb, "all")])
                if diag:
                    P.op("pe", lambda e, kb=kb, zb=zb: e.matmul(zb.h[:, cp:], lhsT=ident_bf.h[:],
                                                             rhs=maskSB.h[:, kb - 4 * qg, cp:], start=False, stop=True),
                         reads=[(ident_bf, "all"), (maskSB, "all")], writes=[(zb, "all")])

        def S2(q_):
            cp, E, SP_ = q_.cp, q_.E, q_.SP
            P.op("act", lambda e: e.activation(out=E.h[:, :, cp:], in_=zpair(q_)[:, :, cp:], func=AF.Exp),
                 reads=ZRW(q_), writes=[(E, "all")])
            P.op("act", lambda e: e.activation(out=SP_.h[:, :, cp:], in_=E.h[:, :, cp:], func=AF.Ln, bias=1.0),
                 reads=[(E, "all")], writes=[(SP_, "all")])

        def S3(q_):
            if q_.last:
                return
            F_, SP_, Bo, cp, cpn = q_.SrF, q_.SP, q_.SrBout, q_.cp, q_.cpn
            if q_.first:
                if cp > 0:
                    P.op("dve", lambda e: e.memset(F_.h[:, 0:cp], 0.0), writes=[(F_, "all")])
                P.op("dve", lambda e: e.tensor_tensor(out=F_.h[:, cp:], in0=SP_.h[:, 0, cp:], in1=SP_.h[:, 1, cp:], op=ALU.add),
                     reads=[(SP_, "all")], writes=[(F_, "all")])
            else:
                for bi in range(2):
                    P.op("dve", lambda e, bi=bi: e.tensor_tensor(out=F_.h[:, cp:], in0=F_.h[:, cp:], in1=SP_.h[:, bi, cp:],
                                                                op=ALU.add),
                         reads=[(SP_, "all")], writes=[(F_, "all")])
            P.op("dve", lambda e: e.tensor_copy(out=Bo.h[:, cpn:], in_=F_.h[:, cpn:]), reads=[(F_, "all")],
                 writes=[(Bo, "all")])

        def S4(q_):
            SP_, Bi, cp = q_.SP, q_.SrBin, q_.cp
            for bi in range(2):
                zb = q_.zb[bi]
                P.op("pe", lambda e, bi=bi, zb=zb: e.matmul(zb.h[:, cp:], lhsT=negU.h[:], rhs=SP_.h[:, bi, cp:],
                                                           start=False, stop=True, skip_group_check=True),
                     reads=[(negU, "all"), (SP_, "all")], writes=[(zb, "all")])
                if bi == 1:
                    P.op("pe", lambda e, zb=zb: e.matmul(zb.h[:, cp:], lhsT=negOnes.h[:], rhs=SP_.h[:, 0, cp:],
                                                        start=False, stop=True, skip_group_check=True),
                         reads=[(negOnes, "all"), (SP_, "all")], writes=[(zb, "all")])
                if not q_.first:
                    P.op("pe", lambda e, zb=zb: e.matmul(zb.h[:, cp:], lhsT=negOnes.h[:], rhs=Bi.h[:, cp:],
                                                        start=False, stop=True, skip_group_check=True),
                         reads=[(negOnes, "all"), (Bi, "all")], writes=[(zb, "all")])

        def S5(q_):
            cp, A_ = q_.cp, q_.A
            P.op("act", lambda e: e.activation(out=A_.h[:, :, cp:], in_=zpair(q_)[:, :, cp:], func=AF.Exp),
                 reads=ZRW(q_), writes=[(A_, "all")])

        def S6(q_):
            h, qg, ob, A_ = q_.h, q_.qg, q_.ob, q_.A
            for bi in range(2):
                kb = q_.kb[bi]
                for qc in range(4):
                    if kb - 4 * qg > qc:
                        continue
                    st_flag = q_.g not in first_av
                    first_av[q_.g] = True
                    P.op("pe", lambda e, qc=qc, st_flag=st_flag, bi=bi, kb=kb: e.matmul(
                        ob.h[:, qc * 64:(qc + 1) * 64], lhsT=A_.h[:, bi, qc * 128:(qc + 1) * 128],
                        rhs=vS.h[:, kb, h * 64:(h + 1) * 64], start=st_flag, stop=True, skip_group_check=True),
                        reads=[(A_, "all"), (vS, kb)], writes=[(ob, "all")])
            if q_.last:
                oSq = oSqs[qg % 2]
                P.op("dve", lambda e: e.tensor_copy(
                    out=oSq.h[:, :, h * 64:(h + 1) * 64],
                    in_=ob.h[:, 0:256].rearrange("p (a b) -> p a b", a=4)),
                    reads=[(ob, "all")], writes=[(oSq, h)])
                if h == 7:
                    if "oS" in dbg:
                        d = nc.dram_tensor("dbg_oS%d" % qg, [128, 4, 512], F32, kind="ExternalOutput").ap()
                        P.dma("sp", lambda e, d=d: e.dma_start(out=d, in_=oSq.h[:]),
                              reads=[(oSq, hh) for hh in range(8)], final=True)
                    head_norm_gate(oSq, 8, 64, sbw_bc, 4 * qg, 4, 0, sqt, ss4, [(oSq, hh) for hh in range(8)], "sb",
                                   defer=sb_defer)

        sb_defer = []
        NB = len(pairs)
        for t in range(NB + 3):
            if 0 <= t - 2 < NB:
                S4(pairs[t - 2])
            if t < NB:
                S1(pairs[t])
            if 0 <= t - 1 < NB:
                S2(pairs[t - 1])
                S3(pairs[t - 1])
            if 0 <= t - 2 < NB:
                S5(pairs[t - 2])
            if 0 <= t - 3 < NB:
                S6(pairs[t - 3])
            if sb_defer and t % 3 == 0:
                sb_defer.pop(0)()
        while sb_defer:
            sb_defer.pop(0)()
        P.barrier()
        s4.close()
        s1.close()

        if stage < 2.5:
            for tt in range(NT):
                P.op("pool", lambda e, tt=tt: e.memset(mg.h[:, tt, 512:1024], 0.0), writes=[(mg, (tt, 1))])
        else:
            s6 = contextlib.ExitStack()
            g6 = lambda name, shape, dt=F32: sbuf(s6, name, shape, dt)
            ones_f = g6("ones_f", [128, 128], F32)
            Lcum = g6("Lcum", [128, 128], F32)
            Lsel = g6("Lsel", [128, 128], F32)
            maskD = g6("maskD", [128, 128], F32)
            maskDs = g6("maskDs", [128, 128], F32)
            P.op("pool", lambda e: e.memset(ones_f.h[:], 1.0), writes=[(ones_f, "all")])
            const_tri(Lcum, 1.0, ALU.is_ge, pattern=[[1, 128]], cm=-1)
            P.op("pool", lambda e: e.memset(Lcum.h[0:64, 64:128], 0.0), writes=[(Lcum, "all")])
            for hf in range(2):
                const_tri(Lsel, 1.0, ALU.is_equal, pattern=[[0, 64]], cm=1, base=-(63 + 64 * hf),
                          sl=Lsel.h[:, 64 * hf:64 * hf + 64])
            const_tri(maskD, 0.0, ALU.is_ge, fill=NEG, pattern=[[-1, 128]], cm=1)
            P.op("pool", lambda e: e.memset(maskD.h[64:128, 0:64], NEG), writes=[(maskD, "all")])
            const_tri(maskDs, 0.0, ALU.is_gt, fill=NEG, pattern=[[-1, 128]], cm=1)
            P.op("pool", lambda e: e.memset(maskDs.h[64:128, 0:64], NEG), writes=[(maskDs, "all")])

            par = g6("par", [128, 8], F32)
            beta = g6("beta", [128, NT, 4], F32)
            gdec = g6("gdec", [128, NT, 4], F32)
            gc = g6("gc", [128, NT, 4], F32)
            ngc = g6("ngc", [128, NT, 4], F32)
            ekd = g6("ekd", [128, NT, 4], F32)
            begc = g6("begc", [128, NT, 4], F32)
            eglB = g6("eglB", [128, 4, 32], F32)
            P.dma("sp", lambda e: e.dma_start(out=par.h[:, 0:4], in_=dt_bias.partition_broadcast(128)), writes=[(par, "a")])
            P.dma("sp", lambda e: e.dma_start(out=par.h[:, 4:8], in_=A_log.partition_broadcast(128)), writes=[(par, "b")])
            act_fn(par.h[:, 4:8], par.h[:, 4:8], AF.Exp, [], [(par, "b")])
            P.op("dve", lambda e: e.tensor_scalar(out=par.h[:, 4:8], in0=par.h[:, 4:8], scalar1=-1.0, scalar2=None,
                                                  op0=ALU.mult), writes=[(par, "b")])
            BD_ALL = [(bd, tt) for tt in range(NT)]
            act_fn(beta.h[:], bd.h[:, :, 0:4], AF.Sigmoid, BD_ALL, [(beta, "all")])
            P.op("dve", lambda e: e.tensor_tensor(out=gdec.h[:], in0=bd.h[:, :, 4:8],
                                                  in1=par.h[:, 0:4].unsqueeze(1).to_broadcast([128, NT, 4]), op=ALU.add),
                 reads=BD_ALL + [(par, "a")], writes=[(gdec, "all")])
            act_fn(gdec.h[:], gdec.h[:], AF.Exp, [], [(gdec, "all")])
            act_fn(gdec.h[:], gdec.h[:], AF.Ln, [], [(gdec, "all")], bias=1.0)
            P.op("dve", lambda e: e.tensor_tensor(out=gdec.h[:], in0=gdec.h[:],
                                                  in1=par.h[:, 4:8].unsqueeze(1).to_broadcast([128, NT, 4]), op=ALU.mult),
                 reads=[(par, "b")], writes=[(gdec, "all")])
            flat = lambda b_: b_.h[:].rearrange("p a b -> p (a b)")
            pb = ps[6]
            P.op("pe", lambda e: e.matmul(pb.h[:, 0:64], lhsT=Lcum.h[:], rhs=flat(gdec), start=True, stop=True),
                 reads=[(Lcum, "all"), (gdec, "all")], writes=[(pb, "all")])
            P.op("dve", lambda e: e.tensor_copy(out=flat(gc), in_=pb.h[:, 0:64]), reads=[(pb, "all")], writes=[(gc, "all")])
            P.op("dve", lambda e: e.tensor_scalar(out=flat(ngc), in0=flat(gc), scalar1=-1.0, scalar2=None, op0=ALU.mult),
                 reads=[(gc, "all")], writes=[(ngc, "all")])
            pb2 = ps[7]
            P.op("pe", lambda e: e.matmul(pb2.h[:, 0:64], lhsT=Lsel.h[:], rhs=flat(gc), start=True, stop=True),
                 reads=[(Lsel, "all"), (gc, "all")], writes=[(pb2, "all")])
            P.op("dve", lambda e: e.tensor_tensor(out=flat(ekd), in0=pb2.h[:, 0:64], in1=flat(gc), op=ALU.subtract),
                 reads=[(pb2, "all"), (gc, "all")], writes=[(ekd, "all")])
            act_fn(ekd.h[:], ekd.h[:], AF.Exp, [], [(ekd, "all")])
            act_fn(begc.h[:], gc.h[:], AF.Exp, [(gc, "all")], [(begc, "all")])
            P.op("dve", lambda e: e.tensor_tensor(out=begc.h[:], in0=begc.h[:], in1=beta.h[:], op=ALU.mult),
                 reads=[(beta, "all")], writes=[(begc, "all")])
            gcb = g6("gcb", [128, NT, 4], F32)
            act_fn(gcb.h[:], beta.h[:], AF.Ln, [(beta, "all")], [(gcb, "all")])
            P.op("dve", lambda e: e.tensor_tensor(out=gcb.h[:], in0=gcb.h[:], in1=gc.h[:], op=ALU.add),
                 reads=[(gc, "all")], writes=[(gcb, "all")])
            dbg_dump("gc", gc, "all", [128, NT, 4], F32)
            dbg_dump("beta", beta, "all", [128, NT, 4], F32)

            for tt in range(NT):
                for h in range(4):
                    P.op("pool", lambda e, tt=tt, h=h: e.tensor_tensor(
                        out=mg.h[:, tt, 512 + h * 128:512 + (h + 1) * 128],
                        in0=mg.h[:, tt, 512 + h * 128:512 + (h + 1) * 128], in1=gnw_bc.h[:], op=ALU.mult),
                        reads=[(gnw_bc, "all")], writes=[(mg, (tt, 1))])
            uB = g6("uB", [128, NT, 4, 128], BF16)
            wT = g6("wT", [128, 4, T], BF16)
            qgT = g6("qgT", [128, 4, T], BF16)
            aiT = g6("aiT", [128, NT, 4, 64], BF16)
            kdB = g6("kdB", [128, NT, 4, 128], BF16)

            Sf = [g6("Sf%d" % h, [128, 128], F32) for h in range(4)]
            Sb = [g6("Sb%d" % h, [128, 128], BF16) for h in range(4)]
            vnwA = g6("vnwA", [128, 4, 128], BF16)
            oG = g6("oG", [128, 1, 512], F32)
            sqg = g6("sqg", [128, 1, 128], F32)
            ssg = g6("ssg", [128, 4], F32)
            for h in range(4):
                P.op("pool", lambda e, h=h: e.memset(Sf[h].h[:], 0.0), writes=[(Sf[h], "all")])
                P.op("pool", lambda e, h=h: e.memset(Sb[h].h[:], 0.0), writes=[(Sb[h], "all")])
            s7 = contextlib.ExitStack()
            g7 = lambda name, shape, dt=F32: sbuf(s7, name, shape, dt)
            NCH = 3
            dgc = [g7("dgc%d" % i, [128, 128], F32) for i in range(2)]
            GBm3 = g7("GBm", [128, 1, 512], F32)
            EGt3 = g7("EGt", [128, 1, 512], F32)
            GBs = [g7("GBs%d" % i, [128, 512], F32) for i in range(NCH)]
            Ab = [g7("Ab%d" % i, [128, 512], BF16) for i in range(NCH)]
            ATb = [g7("ATb%d" % i, [128, 512], BF16) for i in range(NCH)]
            Pb = [[g7("Pb%d_%d" % (i, j), [128, 512], BF16) for j in range(2)] for i in range(NCH)]
            PTb = [[g7("PTb%d_%d" % (i, j), [128, 512], BF16) for j in range(2)] for i in range(NCH)]
            Gb = [[g7("Gb%d" % i, [128, 512], BF16)] * 2 for i in range(NCH)]
            Dm = [PTb[i][1] for i in range(NCH)]
            AIb = [Pb[i][1] for i in range(NCH)]
            vb = [g7("vb%d" % i, [128, 512], BF16) for i in range(NCH)]
            rw = [g7("rw%d" % i, [128, 512], BF16) for i in range(NCH)]
            bank_i = [0]

            for i_, b_ in enumerate(ps):
                b_.bidx = i_
            free_banks = [4, 5, 6, 7]

            def get_banks(n):
                while len(free_banks) < n:
                    yield
                return [ps[free_banks.pop(0)] for _ in range(n)]

            def rel(*bs):
                for b_ in bs:
                    free_banks.append(b_.bidx)

            dgi = [0]

            class Chain:
                pass

            def chain_steps(h, tq, ci):
                tts = [4 * tq + k for k in range(4)]
                sl = lambda k: slice(k * 128, (k + 1) * 128)
                tsl = lambda k: slice(tts[k] * 128, (tts[k] + 1) * 128)
                qn = lambda k: gqkv.h[:, h, tsl(k)]
                kn = lambda k: gqkv.h[:, 4 + h, tsl(k)]
                vn_ = lambda k: gqkv.h[:, 8 + h, tsl(k)]
                QR = [(gqkv, (h, tq))]
                KR = [(gqkv, (4 + h, tq))]
                VR = [(gqkv, (8 + h, tq))]
                (pGB,) = yield from get_banks(1)
                for k in range(4):
                    d_ = dgc[dgi[0] % 2]
                    dgi[0] += 1
                    P.op("act", lambda e, d_=d_, k=k: e.activation(out=d_.h[:], in_=ident_f.h[:], func=AF.Copy,
                                                                  scale=gc.h[:, tts[k], h:h + 1]),
                         reads=[(ident_f, "all"), (gc, "all")], writes=[(d_, "all")])
                    P.op("pe", lambda e, d_=d_, k=k: e.matmul(pGB.h[:, sl(k)], lhsT=ones_f.h[:], rhs=d_.h[:],
                                                            start=True, stop=True),
                         reads=[(ones_f, "all"), (d_, "all")], writes=[(pGB, "all")])
                yield
                v4 = lambda ap: ap.rearrange("p (a b) -> p a b", a=4)
                P.op("dve", lambda e: e.tensor_tensor(out=v4(GBm3.h[:, 0, :]), in0=v4(pGB.h[:]),
                                                      in1=maskD.h[:].unsqueeze(1).to_broadcast([128, 4, 128]),
                                                      op=ALU.subtract),
                     reads=[(pGB, "all"), (maskD, "all")], writes=[(GBm3, "all")])
                P.op("dve", lambda e: e.tensor_tensor(out=v4(GBs[ci].h[:]), in0=v4(pGB.h[:]),
                                                      in1=maskDs.h[:].unsqueeze(1).to_broadcast([128, 4, 128]),
                                                      op=ALU.subtract),
                     reads=[(pGB, "all"), (maskDs, "all")], writes=[(GBs[ci], "all")])
                act_fn(EGt3.h[:, 0, :], pGB.h[:], AF.Exp, [(pGB, "all")], [(EGt3, "all")])
                P.op("dve", lambda e: e.tensor_tensor(out=qgT.h[:, h, tq * 512:(tq + 1) * 512],
                                                      in0=gqkv.h[:, h, tq * 512:(tq + 1) * 512], in1=EGt3.h[:, 0, :],
                                                      op=ALU.mult),
                     reads=QR + [(EGt3, "all")], writes=[(qgT, (h, tq))])
                P.op("dve", lambda e: e.tensor_copy(out=eglB.h[:, h, 8 * tq:8 * tq + 8],
                                                    in_=EGt3.h[:, 0, :].rearrange("p (c t) -> p c t", t=64)[:, :, 63]),
                     reads=[(EGt3, "all")], writes=[(eglB, (h, tq))])
                for k in range(4):
                    act_fn(Dm[ci].h[:, sl(k)], GBm3.h[:, 0, sl(k)], AF.Exp, [(GBm3, "all"), (gc, "all")],
                           [(Dm[ci], "all")], bias=gc.h[:, tts[k], h:h + 1], scale=-1.0)
                    act_fn(GBs[ci].h[:, sl(k)], GBs[ci].h[:, sl(k)], AF.Exp, [(gcb, "all")],
                           [(GBs[ci], "all")], bias=gcb.h[:, tts[k], h:h + 1], scale=-1.0)
                rel(pGB)
                pK, pQ = yield from get_banks(2)
                for k in range(4):
                    P.op("pe", lambda e, k=k: e.matmul(pK.h[:, sl(k)], lhsT=kn(k), rhs=kn(k), start=True, stop=True),
                         reads=KR, writes=[(pK, "all")])
                for k in range(4):
                    P.op("pe", lambda e, k=k: e.matmul(pQ.h[:, sl(k)], lhsT=qn(k), rhs=kn(k), start=True, stop=True),
                         reads=KR + QR, writes=[(pQ, "all")])
                yield
                P.op("dve", lambda e: e.tensor_tensor(out=Ab[ci].h[:], in0=pK.h[:], in1=GBs[ci].h[:], op=ALU.mult),
                     reads=[(pK, "all"), (GBs[ci], "all")], writes=[(Ab[ci], "all")])
                P.op("dve", lambda e: e.tensor_tensor(out=AIb[ci].h[:], in0=pQ.h[:], in1=Dm[ci].h[:], op=ALU.mult),
                     reads=[(pQ, "all"), (Dm[ci], "all")], writes=[(AIb[ci], "all")])
                rel(pK, pQ)
                pT1, pT2 = yield from get_banks(2)
                pT1b = pT1.h[:].bitcast(BF16)
                for k in range(4):
                    P.op("pe", lambda e, k=k: e.transpose(out=pT1b[:, sl(k)], in_=Ab[ci].h[:, sl(k)], identity=ident_bf.h[:]),
                         reads=[(Ab[ci], "all"), (ident_bf, "all")], writes=[(pT1, "all")])
                for k in range(4):
                    P.op("pe", lambda e, k=k: e.transpose(out=pT1b[:, 512 + k * 128:512 + (k + 1) * 128],
                                                         in_=AIb[ci].h[:, sl(k)], identity=ident_bf.h[:]),
                         reads=[(AIb[ci], "all"), (ident_bf, "all")], writes=[(pT1, "all")])
                pT2b = pT2.h[:].bitcast(BF16)
                for k in range(4):
                    P.op("pe", lambda e, k=k: e.transpose(out=pT2b[:, sl(k)], in_=kn(k), identity=ident_bf.h[:]),
                         reads=KR + [(ident_bf, "all")], writes=[(pT2, "all")])
                for k in range(4):
                    P.op("pe", lambda e, k=k: e.transpose(out=pT2b[:, 512 + k * 128:512 + (k + 1) * 128], in_=vn_(k),
                                                         identity=ident_bf.h[:]),
                         reads=VR + [(ident_bf, "all")], writes=[(pT2, "all")])
                yield
                act_fn(ATb[ci].h[:], pT1b[:, 0:512], AF.Copy, [(pT1, "all")], [(ATb[ci], "all")])
                for hf_ in range(2):
                    rr = slice(64 * hf_, 64 * hf_ + 64)
                    P.op("dve", lambda e, rr=rr: e.tensor_copy(
                        out=aiT.h[rr, tts[0]:tts[0] + 4, h, :],
                        in_=pT1b[rr, 512:1024].rearrange("p (a b) -> p a b", a=4)[:, :, rr]),
                        reads=[(pT1, "all")], writes=[(aiT, (h, tq, hf_))])
                G0 = Gb[ci][0]
                P.op("dve", lambda e: e.tensor_tensor(
                    out=G0.h[:].rearrange("p (a b) -> p a b", a=4),
                    in0=ident_bf.h[:].unsqueeze(1).to_broadcast([128, 4, 128]),
                    in1=ATb[ci].h[:].rearrange("p (a b) -> p a b", a=4), op=ALU.subtract),
                    reads=[(ATb[ci], "all"), (ident_bf, "all")], writes=[(G0, "all")])
                for k in range(4):
                    act_fn(rw[ci].h[:, sl(k)], pT2b[:, sl(k)], AF.Copy, [(pT2, "all"), (begc, "all")], [(rw[ci], k)],
                           scale=begc.h[:, tts[k], h:h + 1])
                    act_fn(kdB.h[:, tts[k], h, :], pT2b[:, sl(k)], AF.Copy, [(pT2, "all"), (ekd, "all")],
                           [(kdB, (h, tts[k]))], scale=ekd.h[:, tts[k], h:h + 1])
                    act_fn(vb[ci].h[:, sl(k)], pT2b[:, 512 + k * 128:512 + (k + 1) * 128], AF.Copy,
                           [(pT2, "all"), (beta, "all")], [(vb[ci], k)], scale=beta.h[:, tts[k], h:h + 1])
                rel(pT1, pT2)
                Pc, PTc = Ab[ci], ATb[ci]
                Gc = G0
                for lv in range(5):
                    last = (lv == 4)
                    Pn = Pb[ci][lv % 2]
                    PTn = PTb[ci][lv % 2]
                    Gn = Gb[ci][(lv + 1) % 2]
                    if last:
                        (pP,) = yield from get_banks(1)
                    else:
                        pP, pPT = yield from get_banks(2)
                    for k in range(4):
                        P.op("pe", lambda e, k=k, pP=pP, Pc=Pc, PTc=PTc: e.matmul(
                            pP.h[:, sl(k)], lhsT=PTc.h[:, sl(k)], rhs=Pc.h[:, sl(k)], start=True, stop=True),
                            reads=[(Pc, "all"), (PTc, "all")], writes=[(pP, "all")])
                    if not last:
                        for k in range(4):
                            P.op("pe", lambda e, k=k, pPT=pPT, Pc=Pc, PTc=PTc: e.matmul(
                                pPT.h[:, sl(k)], lhsT=Pc.h[:, sl(k)], rhs=PTc.h[:, sl(k)], start=True, stop=True),
                                reads=[(Pc, "all"), (PTc, "all")], writes=[(pPT, "all")])
                    yield
                    act_fn(Pn.h[:], pP.h[:], AF.Copy, [(pP, "all")], [(Pn, "all")])
                    if not last:
                        P.op("dve", lambda e, PTn=PTn, pPT=pPT: e.tensor_copy(out=PTn.h[:], in_=pPT.h[:]),
                             reads=[(pPT, "all")], writes=[(PTn, "all")])
                    rel(pP)
                    if not last:
                        rel(pPT)
                    (pG,) = yield from get_banks(1)
                    for k in range(4):
                        P.op("pe", lambda e, k=k, pG=pG, Pn=Pn, Gc=Gc: e.matmul(
                            pG.h[:, sl(k)], lhsT=Pn.h[:, sl(k)], rhs=Gc.h[:, sl(k)], start=True, stop=True),
                            reads=[(Pn, "all"), (Gc, "all")], writes=[(pG, "all")])
                    yield
                    P.op("dve", lambda e, pG=pG, Gc=Gc, Gn=Gn: e.tensor_tensor(out=Gn.h[:], in0=pG.h[:], in1=Gc.h[:], op=ALU.add),
                         reads=[(pG, "all"), (Gc, "all")], writes=[(Gn, "all")])
                    rel(pG)
                    Pc, PTc, Gc = Pn, PTn, Gn
                GT = Gc
                pU, pW = yield from get_banks(2)
                for k in range(4):
                    P.op("pe", lambda e, k=k: e.matmul(pU.h[:, sl(k)], lhsT=GT.h[:, sl(k)], rhs=vb[ci].h[:, sl(k)],
                                                      start=True, stop=True),
                         reads=[(GT, "all"), (vb[ci], k)], writes=[(pU, "all")])
                for k in range(4):
                    P.op("pe", lambda e, k=k: e.matmul(pW.h[:, sl(k)], lhsT=rw[ci].h[:, sl(k)], rhs=GT.h[:, sl(k)],
                                                      start=True, stop=True),
                         reads=[(GT, "all"), (rw[ci], k)], writes=[(pW, "all")])
                yield
                act_fn(uB.h[:, tts[0]:tts[0] + 4, h, :], pU.h[:].rearrange("p (a b) -> p a b", a=4), AF.Copy,
                       [(pU, "all")], [(uB, (h, tq))])
                P.op("dve", lambda e: e.tensor_copy(out=wT.h[:, h, tq * 512:(tq + 1) * 512], in_=pW.h[:]),
                     reads=[(pW, "all")], writes=[(wT, (h, tq))])
                rel(pU, pW)

            jobs = [(h, tq) for tq in range(4) for h in range(4)]
            if cut < 99:
                jobs = jobs[:NCH] if cut > 0 else []
            dbg_dump("uB", uB, [(h, tq) for h in range(4) for tq in range(4)], [128, NT, 4, 128], BF16)
            dbg_dump("wT", wT, [(h, tq) for h in range(4) for tq in range(4)], [128, 4, T], BF16)


            def tail_load(tt):
                b = tt % 2
                P.dma("sp", lambda e: e.dma_start(out=xr[b].h[:], in_=x[tt * 128:(tt + 1) * 128, :]),
                      writes=[(xr[b], "all")])

            DENSE_TAIL = True
            tasks = []
            emitted = {}
            slot = [0]
            last_ids = {}

            def add_task(kind, fn, deps, name=None, delay=0):
                tid = len(emitted) + len(tasks)
                tasks.append({"kind": kind, "fn": fn, "deps": [d for d in deps if d is not None], "id": tid,
                              "nb": slot[0] + delay})
                if name is not None:
                    last_ids[name] = tid
                return tid

            def pump(kind, n=1):
                slot[0] += 1
                cnt = 0
                i = 0
                while i < len(tasks):
                    t_ = tasks[i]
                    ok = all((d in emitted) and emitted[d] < slot[0] for d in t_["deps"]) and slot[0] > t_["nb"]
                    if ok and (t_["kind"] == "act" or (t_["kind"] == kind and cnt < n)):
                        if t_["kind"] != "act":
                            cnt += 1
                        tasks.pop(i)
                        t_["fn"]()
                        emitted[t_["id"]] = slot[0]
                        continue
                    i += 1

            def force_tasks(pred):
                last = -1
                for i, t_ in enumerate(tasks):
                    if pred(t_):
                        last = i
                slot[0] += 1
                for _ in range(last + 1):
                    t_ = tasks.pop(0)
                    t_["fn"]()
                    emitted[t_["id"]] = slot[0]

            def add_tile_tasks(tt, po):
                b = tt % 2
                pb = ps[6]
                pbf = pb.h[:].bitcast(BF16)
                po2 = ps[7]

                def n2(hs, gate):
                    def f():
                        for h in hs:
                            P.op("act", lambda e, h=h: e.activation(
                                out=oG.h[:, 0, h * 128:(h + 1) * 128], in_=po.h[:, h * 128:(h + 1) * 128],
                                func=AF.Copy, scale=ssg.h[:, h:h + 1]),
                                reads=[(po, "all"), (ssg, "all")], writes=[(oG, "all")])
                        if gate:
                            P.op("pool", lambda e: e.tensor_tensor(out=mg.h[:, tt, 512:1024], in0=oG.h[:, 0, :],
                                                                   in1=mg.h[:, tt, 512:1024], op=ALU.mult),
                                 reads=[(oG, "all")], writes=[(mg, (tt, 1))])
                    return f

                def tr(ecs, evac):
                    def f():
                        for ec in ecs:
                            P.op("pe", lambda e, ec=ec: e.transpose(out=pbf[:, ec * 128:(ec + 1) * 128],
                                                                   in_=mg.h[:, tt, ec * 128:(ec + 1) * 128],
                                                                   identity=ident_bf.h[:]),
                                 reads=[(mg, (tt, ec // 4)), (ident_bf, "all")], writes=[(pb, "all")])
                    return f

                def evac_mT():
                    src = pbf.rearrange("p (c t) -> p c t", c=DC)
                    P.op("act", lambda e: e.copy(out=mT[b].h[:], in_=src), reads=[(pb, "all")], writes=[(mT[b], "all")])

                def op_mm(hf, ecs):
                    def f():
                        for ec in ecs:
                            P.op("pe", lambda e, ec=ec: e.matmul(
                                po2.h[:], lhsT=mT[b].h[:, ec, :], rhs=wo_bf.h[:, ec, hf * 512:(hf + 1) * 512],
                                start=(ec == 0), stop=(ec == DC - 1)),
                                reads=[(mT[b], "all"), (wo_bf, (ec, hf))], writes=[(po2, "all")])
                    return f

                def add_res(hf):
                    def f():
                        P.op("dve", lambda e: e.tensor_tensor(
                            out=x2[b].h[:, hf * 512:(hf + 1) * 512], in0=po2.h[:], in1=xr[b].h[:, hf * 512:(hf + 1) * 512],
                            op=ALU.add), reads=[(po2, "all"), (xr[b], "all")], writes=[(x2[b], hf)])
                    return f

                def fin():
                    P.op("act", lambda e: e.activation(out=junkT.h[:], in_=x2[b].h[:], func=AF.Square,
                                                       accum_out=ssq2.h[:, tt:tt + 1]),
                         reads=[(x2[b], 0), (x2[b], 1)], writes=[(junkT, "all"), (ssq2, tt)])
                    rsqrt_small(ssq2.h[:, tt:tt + 1], ssq2.h[:, tt:tt + 1], 1.0 / D, [], (ssq2, tt))
                    P.op("act", lambda e: e.activation(out=x2[b].h[:], in_=x2[b].h[:], func=AF.Copy,
                                                       scale=ssq2.h[:, tt:tt + 1]),
                         reads=[(ssq2, tt)], writes=[(x2[b], 0), (x2[b], 1)])
                    P.op("pool", lambda e: e.tensor_tensor(out=x2[b].h[:], in0=x2[b].h[:], in1=fnw_bc.h[:], op=ALU.mult),
                         reads=[(fnw_bc, "all")], writes=[(x2[b], 0), (x2[b], 1)])
                    P.dma("sp", lambda e: e.dma_start(out=y[tt * 128:(tt + 1) * 128, :], in_=x2[b].h[:]),
                          reads=[(x2[b], 0), (x2[b], 1)], final=True)
                    if tt + 2 < NT:
                        tail_load(tt + 2)

                L = last_ids.get
                n2b = None
                if po is not None:
                    n2a = add_task("act", n2([0, 1], False), [], "n2a", delay=1)
                    n2b = add_task("act", n2([2, 3], True), [n2a], "n2b")
                    tasks[-1]["is_n2"] = True
                    tasks[-2]["is_n2"] = True
                if DENSE_TAIL and po is not None:
                    return
                t1 = add_task("pe", tr([0, 1, 2, 3], False), [n2b, L("tev")])
                t2 = add_task("pe", tr([4, 5, 6, 7], True), [n2b, L("tev")])
                tev = add_task("act", evac_mT, [t1, t2, L("m1b")], "tev")
                m0a = add_task("pe", op_mm(0, [0, 1, 2, 3]), [tev, L("r1")])
                m0b = add_task("pe", op_mm(0, [4, 5, 6, 7]), [tev, L("r1")])
                r0 = add_task("dve", add_res(0), [m0b, L("fin2")])
                m1a = add_task("pe", op_mm(1, [0, 1, 2, 3]), [r0])
                m1b = add_task("pe", op_mm(1, [4, 5, 6, 7]), [r0], "m1b")
                r1 = add_task("dve", add_res(1), [m1b], "r1")
                last_ids["fin2"] = last_ids.get("fin1")
                add_task("act", fin, [r1], "fin1")

            def drain_tasks():
                force_tasks(lambda t_: True)

            def scan_chunk(c):
                tt, hf = c // 2, c % 2
                r0 = 64 * hf
                rs = slice(r0, r0 + 64)
                cs = slice(c * 64, (c + 1) * 64)
                pv = ps[0]
                po = ps[2 + (tt % 2)]
                pS = ps[1]
                for h in range(4):
                    P.op("pe", lambda e, h=h: e.matmul(pv.h[rs, h * 128:(h + 1) * 128], lhsT=wT.h[:, h, cs], rhs=Sb[h].h[:],
                                                      start=True, stop=True, skip_group_check=True),
                         reads=[(wT, (h, tt // 4)), (Sb[h], "all")], writes=[(pv, "all")])
                for h in range(4):
                    P.op("pe", lambda e, h=h: e.matmul(po.h[rs, h * 128:(h + 1) * 128], lhsT=qgT.h[:, h, cs], rhs=Sb[h].h[:],
                                                      start=(h == 0), stop=False, skip_group_check=True),
                         reads=[(qgT, (h, tt // 4)), (Sb[h], "all")], writes=[(po, "all")])
                pump("pe", 2)
                yield
                P.op("dve", lambda e: e.tensor_tensor(out=vnwA.h[rs, :, :].rearrange("p a b -> p (a b)"),
                                                      in0=uB.h[rs, tt, :, :].rearrange("p a b -> p (a b)"),
                                                      in1=pv.h[rs, :], op=ALU.subtract),
                     reads=[(uB, (h, tt // 4)) for h in range(4)] + [(pv, "all")], writes=[(vnwA, hf)])
                pump("dve", 1)
                yield
                for h in range(4):
                    P.op("pe", lambda e, h=h: e.matmul(pS.h[:, h * 128:(h + 1) * 128], lhsT=kdB.h[rs, tt, h, :],
                                                      rhs=vnwA.h[rs, h, :], start=True, stop=True, skip_group_check=True),
                         reads=[(kdB, (h, tt)), (vnwA, hf)], writes=[(pS, "all")])
                for h in range(4):
                    P.op("pe", lambda e, h=h: e.matmul(po.h[rs, h * 128:(h + 1) * 128], lhsT=aiT.h[rs, tt, h, :],
                                                      rhs=vnwA.h[rs, h, :], start=False, stop=True, skip_group_check=True),
                         reads=[(aiT, (h, tt // 4, hf)), (vnwA, hf)], writes=[(po, "all")])
                pump("pe", 1)
                yield
                for h in range(4):
                    P.op("dve", lambda e, h=h: e.scalar_tensor_tensor(
                        out=Sb[h].h[:], in0=Sf[h].h[:], scalar=eglB.h[:, h, c:c + 1], in1=pS.h[:, h * 128:(h + 1) * 128],
                        op0=ALU.mult, op1=ALU.add),
                        reads=[(eglB, (h, tt // 4)), (pS, "all"), (Sf[h], "all")], writes=[(Sb[h], "all")])
                for h in range(4):
                    P.op("dve", lambda e, h=h: e.scalar_tensor_tensor(
                        out=Sf[h].h[:], in0=Sf[h].h[:], scalar=eglB.h[:, h, c:c + 1], in1=pS.h[:, h * 128:(h + 1) * 128],
                        op0=ALU.mult, op1=ALU.add),
                        reads=[(eglB, (h, tt // 4)), (pS, "all")], writes=[(Sf[h], "all")])
                if hf == 1:
                    if "oG" in dbg:
                        P.op("act", lambda e: e.copy(out=oG.h[:, 0, :], in_=po.h[:]),
                             reads=[(po, "all")], writes=[(oG, "all")])
                        d = nc.dram_tensor("dbg_oG%d" % tt, [128, 512], F32, kind="ExternalOutput").ap()
                        P.dma("sp", lambda e: e.dma_start(out=d, in_=oG.h[:, 0, :]), reads=[(oG, "all")], final=True)
                    force_tasks(lambda t_: t_.get("is_n2", False))
                    for h in range(4):
                        P.op("act", lambda e, h=h: e.activation(out=sqg.h[:, 0, :],
                                                              in_=po.h[:, h * 128:(h + 1) * 128], func=AF.Square,
                                                              accum_out=ssg.h[:, h:h + 1]),
                             reads=[(po, "all")], writes=[(sqg, "all"), (ssg, "all")])
                    rsqrt_small(ssg.h[:, 0:4], ssg.h[:, 0:4], 1.0 / 128, [], (ssg, "all"))
                    add_tile_tasks(tt, po)
                pump("dve", 1)

            if stage >= 3:
                pending = list(jobs)
                slots = [None] * NCH
                slot_job = [None] * NCH
                STAGGER = 7
                start_round = [i * STAGGER for i in range(NCH)]
                done_tq = [0, 0, 0, 0]
                scan_gen = [None]
                next_chunk = [0]

                def ready_upto():
                    k = 0
                    while k < 4 and done_tq[k] == 4:
                        k += 1
                    return 8 * k

                def scan_step():
                    if scan_gen[0] is None:
                        if next_chunk[0] >= ready_upto():
                            return False
                        scan_gen[0] = scan_chunk(next_chunk[0])
                        next_chunk[0] += 1
                    try:
                        next(scan_gen[0])
                    except StopIteration:
                        scan_gen[0] = None
                    return True

                rnd = 0
                while pending or any(g_ is not None for g_ in slots):
                    for ci in range(NCH):
                        if slots[ci] is None and pending and rnd >= start_round[ci]:
                            h_, tq_ = pending.pop(0)
                            slots[ci] = chain_steps(h_, tq_, ci)
                            slot_job[ci] = tq_
                        if slots[ci] is not None:
                            try:
                                next(slots[ci])
                            except StopIteration:
                                slots[ci] = None
                                done_tq[slot_job[ci]] += 1
                    scan_step()
                    rnd += 1
                P.barrier()
                s7.close()
                fnw_bc = g6("fnw_bc", [128, D], F32)
                P.dma("sp", lambda e: e.dma_start(out=fnw_bc.h[:], in_=final_norm_w.partition_broadcast(128)),
                      writes=[(fnw_bc, "all")])
                wo_bf = g6("wo_bf", [128, DC, D], BF16)
                mT = [g6("mT0", [128, DC, 128], BF16)] * 2
                xr = [g6("xr%d" % i, [128, D], F32) for i in range(2)]
                x2 = [g6("x2%d" % i, [128, D], F32) for i in range(2)]
                ssq2 = g6("ssq2", [128, NT], F32)
                junkT = g6("junkT", [128, D], BF16)
                wo_view = w_out.rearrange("(c p) e -> p c e", p=128)
                for hf in range(2):
                    P.dma("pool", lambda e, hf=hf: e.dma_start(out=wo_bf.h[:, :, hf * 512:(hf + 1) * 512],
                                                             in_=wo_view[:, :, hf * 512:(hf + 1) * 512]),
                          writes=[(wo_bf, (dc, hf)) for dc in range(DC)])


                tail_load(0)
                tail_load(1)
                while next_chunk[0] < 32 or scan_gen[0] is not None:
                    scan_step()
                drain_tasks()
                if DENSE_TAIL:
                    mTd = [mT[0], oG]

                    def mT_ap(i):
                        if i == 0:
                            return mT[0].h[:]
                        return oG.h[:, 0, :].bitcast(BF16).rearrange("p (c t) -> p c t", c=DC)

                    def TA(tt):
                        pb = ps[tt % 2]
                        pbf = pb.h[:].bitcast(BF16)
                        for ec in range(DC):
                            P.op("pe", lambda e, ec=ec: e.transpose(out=pbf[:, ec * 128:(ec + 1) * 128],
                                                                   in_=mg.h[:, tt, ec * 128:(ec + 1) * 128],
                                                                   identity=ident_bf.h[:]),
                                 reads=[(mg, (tt, ec // 4)), (ident_bf, "all")], writes=[(pb, "all")])
                        src = pbf.rearrange("p (c t) -> p c t", c=DC)
                        mb = mTd[tt % 2]
                        P.op("act", lambda e: e.copy(out=mT_ap(tt % 2), in_=src), reads=[(pb, "all")], writes=[(mb, "all")])

                    def TB(tt):
                        b = tt % 2
                        mb = mTd[tt % 2]
                        for hf in range(2):
                            po2 = ps[2 + (2 * tt + hf) % 4]
                            for ec in range(DC):
                                P.op("pe", lambda e, ec=ec, hf=hf, po2=po2: e.matmul(
                                    po2.h[:], lhsT=mT_ap(tt % 2)[:, ec, :], rhs=wo_bf.h[:, ec, hf * 512:(hf + 1) * 512],
                                    start=(ec == 0), stop=(ec == DC - 1)),
                                    reads=[(mb, "all"), (wo_bf, (ec, hf))], writes=[(po2, "all")])
                            P.op("dve", lambda e, hf=hf, po2=po2: e.tensor_tensor(
                                out=x2[b].h[:, hf * 512:(hf + 1) * 512], in0=po2.h[:],
                                in1=xr[b].h[:, hf * 512:(hf + 1) * 512], op=ALU.add),
                                reads=[(po2, "all"), (xr[b], "all")], writes=[(x2[b], hf)])

                    def TC(tt):
                        b = tt % 2
                        P.op("act", lambda e: e.activation(out=junkT.h[:], in_=x2[b].h[:], func=AF.Square,
                                                           accum_out=ssq2.h[:, tt:tt + 1]),
                             reads=[(x2[b], 0), (x2[b], 1)], writes=[(junkT, "all"), (ssq2, tt)])
                        rsqrt_small(ssq2.h[:, tt:tt + 1], ssq2.h[:, tt:tt + 1], 1.0 / D, [], (ssq2, tt))
                        P.op("act", lambda e: e.activation(out=x2[b].h[:], in_=x2[b].h[:], func=AF.Copy,
                                                           scale=ssq2.h[:, tt:tt + 1]),
                             reads=[(ssq2, tt)], writes=[(x2[b], 0), (x2[b], 1)])
                        P.op("pool", lambda e: e.tensor_tensor(out=x2[b].h[:], in0=x2[b].h[:], in1=fnw_bc.h[:], op=ALU.mult),
                             reads=[(fnw_bc, "all")], writes=[(x2[b], 0), (x2[b], 1)])
                        P.dma("sp", lambda e: e.dma_start(out=y[tt * 128:(tt + 1) * 128, :], in_=x2[b].h[:]),
                              reads=[(x2[b], 0), (x2[b], 1)], final=True)
                        if tt + 2 < NT:
                            tail_load(tt + 2)

                    for t in range(NT + 2):
                        if t < NT:
                            TA(t)
                        if 0 <= t - 1 < NT:
                            TB(t - 1)
                        if 0 <= t - 2 < NT:
                            TC(t - 2)
            else:
                pending = list(jobs)
                slots = [None] * NCH
                STAGGER = 7
                start_round = [i * STAGGER for i in range(NCH)]
                rnd = 0
                while pending or any(g_ is not None for g_ in slots):
                    for ci in range(NCH):
                        if slots[ci] is None and pending and rnd >= start_round[ci]:
                            h_, tq_ = pending.pop(0)
                            slots[ci] = chain_steps(h_, tq_, ci)
                        if slots[ci] is not None:
                            try:
                                next(slots[ci])
                            except StopIteration:
                                slots[ci] = None
                    rnd += 1

                P.barrier()
                s7.close()
                fnw_bc = g6("fnw_bc", [128, D], F32)
                P.dma("sp", lambda e: e.dma_start(out=fnw_bc.h[:], in_=final_norm_w.partition_broadcast(128)),
                      writes=[(fnw_bc, "all")])
                wo_bf = g6("wo_bf", [128, DC, D], BF16)
                mT = [g6("mT0", [128, DC, 128], BF16)] * 2
                xr = [g6("xr%d" % i, [128, D], F32) for i in range(2)]
                x2 = [g6("x2%d" % i, [128, D], F32) for i in range(2)]
                ssq2 = g6("ssq2", [128, NT], F32)
                junkT = g6("junkT", [128, D], BF16)
                wo_view = w_out.rearrange("(c p) e -> p c e", p=128)
                for hf in range(2):
                    P.dma("pool", lambda e, hf=hf: e.dma_start(out=wo_bf.h[:, :, hf * 512:(hf + 1) * 512],
                                                             in_=wo_view[:, :, hf * 512:(hf + 1) * 512]),
                          writes=[(wo_bf, (dc, hf)) for dc in range(DC)])


                for tt in range(NT):
                    P.op("pool", lambda e, tt=tt: e.memset(mg.h[:, tt, 512:1024], 0.0), writes=[(mg, (tt, 1))])
                tail_load(0)
                tail_load(1)
                for tt in range(NT):
                    add_tile_tasks(tt, None)
                    drain_tasks()
            dbg_dump("mg", mg, [(tt, hf) for tt in range(NT) for hf in range(2)], [128, NT, 1024], BF16)

        P.emit(st)
        if stage >= 2.5:
            s6.close()
    return nc, dbg_out


def kernel(**inputs):
    nc, _ = build()
    in_maps = _in_maps(inputs)
    res = run_bass_kernel_spmd(nc, in_maps, core_ids=list(range(8)))
    return np.stack([r["y"] for r in res.results], axis=0)


def _in_maps(inputs):
    f = lambda a: np.ascontiguousarray(np.asarray(a, dtype=np.float32))
    maps = []
    for b in range(8):
        maps.append({
            "x": f(inputs["x"][b]),
            "norm1_w": f(inputs["norm1_w"][0]),
            "w_in": f(inputs["w_in"][0]),
            "sb_norm_w": f(inputs["sb_norm_w"][0]),
            "gdn_conv_w": f(inputs["gdn_conv_w"][0]),
            "gdn_A_log": f(inputs["gdn_A_log"][0]),
            "gdn_dt_bias": f(inputs["gdn_dt_bias"][0]),
            "gdn_norm_w": f(inputs["gdn_norm_w"][0]),
            "w_out": f(inputs["w_out"][0]),
            "final_norm_w": f(inputs["final_norm_w"]),
        })
    return maps
```

```python
import contextlib
import numpy as np
import concourse.bass as bass
import concourse.mybir as mybir
from concourse.bass_utils import run_bass_kernel_spmd

F32 = mybir.dt.float32
BF16 = mybir.dt.bfloat16
AF = mybir.ActivationFunctionType
ALU = mybir.AluOpType
AX = mybir.AxisListType

T = 2048
D = 1024
DIN = 4104
NT = 16
DC = 8
EPS = 1e-6
NEG = -30000.0

ENGS = ("pe", "act", "dve", "pool", "sp")


class Buf:
    def __init__(self, handle, name, psum=False):
        self.h = handle
        self.name = name
        self.psum = psum
        self.w = {}
        self.r = {}

    def __getitem__(self, idx):
        return self.h[idx]


class Prog:
    def __init__(self, nc, n_dma_sems=32):
        self.nc = nc
        self.ops = {e: [] for e in ENGS}
        self.cnt = {e: 0 for e in ENGS}
        self.known = {e: {} for e in ENGS}
        self.dma_pool = n_dma_sems
        self.dma_idx = 0
        self.dma_val = [0] * n_dma_sems
        self.final_tokens = []
        self.n_sw = 0

    def _deps(self, reads, writes, eng=None):
        deps = []
        for (b, k) in list(reads) + list(writes):
            t = b.w.get(k)
            if t is not None:
                deps.append(t)
        for (b, k) in reads:
            if b.psum:
                deps.extend(r for r in b.r.get(k, []) if r[0] != eng)
        for (b, k) in writes:
            deps.extend(b.r.get(k, []))
        return deps

    def _commit(self, tok, reads, writes):
        for (b, k) in reads:
            b.r.setdefault(k, []).append(tok)
        for (b, k) in writes:
            b.w[k] = tok
            b.r[k] = []

    def _waits_for(self, eng, deps, skip_same_pe=True):
        need = {}
        for (sk, val) in deps:
            if sk == eng and eng == "pe" and skip_same_pe:
                continue
            if val > need.get(sk, 0):
                need[sk] = val
        out = []
        kn = self.known[eng]
        for sk, val in need.items():
            if kn.get(sk, 0) >= val:
                continue
            kn[sk] = val
            out.append((sk, val))
        return out

    def op(self, eng, fn, reads=(), writes=(), extra=()):
        deps = self._deps(reads, writes, eng) + list(extra)
        waits = self._waits_for(eng, deps)
        self.cnt[eng] += 1
        tok = (eng, self.cnt[eng])
        self.ops[eng].append((waits, fn, ("eng", eng)))
        self._commit(tok, reads, writes)
        return tok

    def dma(self, queue, fn, reads=(), writes=(), final=False):
        deps = self._deps(reads, writes, queue)
        if queue == "pool":
            semkey = ("sw", self.n_sw)
            self.n_sw += 1
            waits = self._waits_for(queue, deps)
            tok = (semkey, 16)
            self.ops[queue].append((waits, fn, semkey))
            self._commit(tok, reads, writes)
            if final:
                self.final_tokens.append(tok)
            return tok
        i = self.dma_idx % self.dma_pool
        self.dma_idx += 1
        semkey = ("dma", i)
        if self.dma_val[i] > 0:
            deps.append((semkey, self.dma_val[i]))
        waits = self._waits_for(queue, deps)
        self.dma_val[i] += 16
        tok = (semkey, self.dma_val[i])
        self.ops[queue].append((waits, fn, ("dma", i)))
        self._commit(tok, reads, writes)
        if final:
            self.final_tokens.append(tok)
        return tok

    def barrier(self):
        toks = [(e, self.cnt[e]) for e in ENGS if self.cnt[e] > 0]
        for i in range(self.dma_pool):
            if self.dma_val[i] > 0:
                toks.append((("dma", i), self.dma_val[i]))
        for i in range(self.n_sw):
            toks.append((("sw", i), 16))
        for e in ENGS:
            waits = self._waits_for(e, toks, skip_same_pe=False)
            if waits:
                self.ops[e].append((waits, None, None))

    def emit(self, stack):
        nc = self.nc
        sems = {}
        for e in ENGS:
            sems[e] = stack.enter_context(nc.semaphore("s_" + e))
        for i in range(self.dma_pool):
            sems[("dma", i)] = stack.enter_context(nc.semaphore("d%d" % i))
        for i in range(self.n_sw):
            sems[("sw", i)] = stack.enter_context(nc.semaphore("w%d" % i))
        fw = self._waits_for("sp", self.final_tokens)
        if fw:
            self.ops["sp"].append((fw, None, None))
        block = stack.enter_context(nc.Block())

        def make(e):
            def body(eng):
                for (waits, fn, inc) in self.ops[e]:
                    for (sk, val) in waits:
                        eng.wait_ge(sems[sk], val)
                    if fn is None:
                        continue
                    ins = fn(eng)
                    if inc[0] == "eng":
                        ins.then_inc(sems[inc[1]], 1)
                    else:
                        ins.then_inc(sems[inc], 16)
            return body

        block.tensor(make("pe"))
        block.scalar(make("act"))
        block.vector(make("dve"))
        block.gpsimd(make("pool"))
        block.sync(make("sp"))


def build(dbg=None, stage=99, cut=99):
    dbg = dbg or {}
    nc = bass.Bass("TRN2", target_bir_lowering=False)
    x = nc.dram_tensor("x", [T, D], F32, kind="ExternalInput").ap()
    norm1_w = nc.dram_tensor("norm1_w", [D], F32, kind="ExternalInput").ap()
    w_in = nc.dram_tensor("w_in", [D, DIN], F32, kind="ExternalInput").ap()
    sb_norm_w = nc.dram_tensor("sb_norm_w", [64], F32, kind="ExternalInput").ap()
    conv_w = nc.dram_tensor("gdn_conv_w", [4, 1536], F32, kind="ExternalInput").ap()
    A_log = nc.dram_tensor("gdn_A_log", [4], F32, kind="ExternalInput").ap()
    dt_bias = nc.dram_tensor("gdn_dt_bias", [4], F32, kind="ExternalInput").ap()
    gdn_norm_w = nc.dram_tensor("gdn_norm_w", [128], F32, kind="ExternalInput").ap()
    w_out = nc.dram_tensor("w_out", [D, D], F32, kind="ExternalInput").ap()
    final_norm_w = nc.dram_tensor("final_norm_w", [D], F32, kind="ExternalInput").ap()
    y = nc.dram_tensor("y", [T, D], F32, kind="ExternalOutput").ap()
    dbg_out = {}

    P = Prog(nc)
    with contextlib.ExitStack() as st:
        def sbuf(stack, name, shape, dt=F32):
            return Buf(stack.enter_context(nc.sbuf_tensor(name, list(shape), dt)), name)

        def sb(name, shape, dt=F32):
            return sbuf(st, name, shape, dt)

        def dbg_dump(name, buf, key, shape, dt=F32, src=None):
            if name not in dbg:
                return
            d = nc.dram_tensor("dbg_" + name, list(shape), dt, kind="ExternalOutput").ap()
            dbg_out[name] = d
            keys = key if isinstance(key, list) else [key]
            P.dma("sp", lambda e: e.dma_start(out=d, in_=buf.h[:] if src is None else src),
                  reads=[(buf, k) for k in keys], final=True)

        psq = [st.enter_context(nc.psum_tensor("pq%d" % i, [128, 1024], F32)) for i in range(4)]

        class HView:
            def __init__(self, h, off):
                self.h_, self.off = h, off

            def __getitem__(self, idx):
                if not isinstance(idx, tuple):
                    idx = (idx, slice(None))
                p_, f_ = idx
                a = 0 if f_.start is None else f_.start
                b = 512 if f_.stop is None else f_.stop
                return self.h_[p_, self.off + a:self.off + b]

        ps = [Buf(HView(psq[i // 2], (i % 2) * 512), "ps%d" % i, psum=True) for i in range(8)]

        def act_fn(out, in_, func, reads, writes, **kw):
            return P.op("act", lambda e: e.activation(out=out, in_=in_, func=func, **kw), reads=reads, writes=writes)

        def rsqrt_small(dst_ap, src_ap, scale, reads, wkey):
            P.op("act", lambda e: e.activation(out=dst_ap, in_=src_ap, func=AF.Ln, scale=scale, bias=eps_c.h[:, 0:1]),
                 reads=list(reads) + [(eps_c, "all")], writes=[wkey])
            P.op("act", lambda e: e.activation(out=dst_ap, in_=dst_ap, func=AF.Exp, scale=-0.5), writes=[wkey])

        eps_c = sb("eps_c", [128, 1], F32)
        P.op("pool", lambda e: e.memset(eps_c.h[:], EPS), writes=[(eps_c, "all")])
        ident_bf = sb("ident_bf", [128, 128], BF16)
        ident_f = sb("ident_f", [128, 128], F32)
        negU = sb("negU", [128, 128], BF16)
        negOnes = sb("negOnes", [128, 128], BF16)
        ones_bf = sb("ones_bf", [128, 128], BF16)
        n1w = sb("n1w", [128, DC], F32)
        ssq = sb("ssq", [128, NT], F32)
        rstd = sb("rstd", [128, NT], F32)
        sbw_bc = sb("sbw_bc", [128, 64], F32)
        gnw_bc = sb("gnw_bc", [128, 128], F32)
        cw = sb("cw", [128, 12, 4], F32)
        mg = sb("mg", [128, NT, 1024], BF16)
        gqkv = sb("gqkv", [128, 12, T], BF16)
        bd = sb("bd", [128, NT, 8], F32)

        def const_tri(buf, val, cmp, fill=0.0, pattern=None, base=0, cm=1, sl=None):
            ap = buf.h[:] if sl is None else sl
            P.op("pool", lambda e: e.memset(ap, val), writes=[(buf, "all")])
            P.op("pool", lambda e: e.affine_select(out=ap, in_=ap, compare_op=cmp, fill=fill,
                                                   base=base, pattern=pattern,
                                                   channel_multiplier=cm),
                 writes=[(buf, "all")])

        const_tri(ident_bf, 1.0, ALU.is_equal, pattern=[[1, 128]], cm=-1)
        const_tri(ident_f, 1.0, ALU.is_equal, pattern=[[1, 128]], cm=-1)
        const_tri(negU, -1.0, ALU.is_ge, pattern=[[-1, 128]], cm=1)
        P.op("pool", lambda e: e.memset(negOnes.h[:], -1.0), writes=[(negOnes, "all")])
        P.op("pool", lambda e: e.memset(ones_bf.h[:], 1.0), writes=[(ones_bf, "all")])

        vrow = sb("vrow", [56, 128], F32)
        P.dma("act", lambda e: e.dma_start(out=vrow.h[0:8, :], in_=norm1_w.rearrange("(c p) -> c p", p=128)),
              writes=[(vrow, "a")])
        P.dma("act", lambda e: e.dma_start(out=vrow.h[8:56, :], in_=conv_w.rearrange("i (c p) -> (i c) p", p=128)),
              writes=[(vrow, "b")])
        P.op("pe", lambda e: e.transpose(out=ps[7].h[:, 0:56], in_=vrow.h[:, :], identity=ident_f.h[0:56, 0:56]),
             reads=[(vrow, "a"), (vrow, "b"), (ident_f, "all")], writes=[(ps[7], "all")])
        P.op("dve", lambda e: e.tensor_copy(out=n1w.h[:], in_=ps[7].h[:, 0:8]), reads=[(ps[7], "all")], writes=[(n1w, "all")])
        P.op("dve", lambda e: e.tensor_copy(out=cw.h[:].rearrange("p c i -> p i c"),
                                            in_=ps[7].h[:, 8:56].rearrange("p (i c) -> p i c", i=4)),
             reads=[(ps[7], "all")], writes=[(cw, "all")])
        P.dma("act", lambda e: e.dma_start(out=sbw_bc.h[:], in_=sb_norm_w.partition_broadcast(128)),
              writes=[(sbw_bc, "all")])
        P.dma("act", lambda e: e.dma_start(out=gnw_bc.h[:], in_=gdn_norm_w.partition_broadcast(128)),
              writes=[(gnw_bc, "all")])

        s1 = contextlib.ExitStack()
        qT = sbuf(s1, "qT", [128, 4, T], BF16)
        kT = sbuf(s1, "kT", [128, 4, T], BF16)
        vS = sbuf(s1, "vS", [128, NT, 512], BF16)

        s2 = contextlib.ExitStack()
        hT = sbuf(s2, "hT", [128, DC, T], BF16)
        s3 = contextlib.ExitStack()
        xs = [sbuf(s3, "xs%d" % i, [128, D], F32) for i in range(4)]
        xn = [sbuf(s3, "xn%d" % i, [128, D], BF16) for i in range(4)]
        junk = sbuf(s3, "junk", [128, D], BF16)

        def A1(tt):
            b = tt % 4
            P.dma("sp", lambda e: e.dma_start(out=xs[b].h[:], in_=x[tt * 128:(tt + 1) * 128, :]),
                  writes=[(xs[b], "all")])
            P.op("act", lambda e: e.activation(out=junk.h[:], in_=xs[b].h[:], func=AF.Square,
                                               accum_out=ssq.h[:, tt:tt + 1]),
                 reads=[(xs[b], "all")], writes=[(junk, "all"), (ssq, tt)])
            rsqrt_small(rstd.h[:, tt:tt + 1], ssq.h[:, tt:tt + 1], 1.0 / D, [(ssq, tt)], (rstd, tt))
            P.op("dve", lambda e: e.tensor_scalar(out=xn[b].h[:], in0=xs[b].h[:],
                                                  scalar1=rstd.h[:, tt:tt + 1], scalar2=None, op0=ALU.mult),
                 reads=[(xs[b], "all"), (rstd, tt)], writes=[(xn[b], "all")])

        def A2(tt):
            b = tt % 4
            pb = ps[tt % 2]
            pbf = pb.h[:].bitcast(BF16)
            for dc in range(DC):
                P.op("pe", lambda e, dc=dc: e.transpose(out=pbf[:, dc * 128:(dc + 1) * 128],
                                                       in_=xn[b].h[:, dc * 128:(dc + 1) * 128],
                                                       identity=ident_bf.h[:]),
                     reads=[(xn[b], "all"), (ident_bf, "all")], writes=[(pb, "all")])
            src = pbf.rearrange("p (c t) -> p c t", c=DC)
            P.op("dve", lambda e: e.tensor_tensor(
                out=hT.h[:, :, tt * 128:(tt + 1) * 128], in0=src,
                in1=n1w.h[:].unsqueeze(2).to_broadcast([128, DC, 128]), op=ALU.mult),
                reads=[(pb, "all"), (n1w, "all")], writes=[(hT, tt)])

        for t in range(NT + 1):
            if t < NT:
                A1(t)
            if t >= 1:
                A2(t - 1)
        HT_ALL = [(hT, tt) for tt in range(NT)]
        P.barrier()
        s3.close()

        wbf = [sbuf(s2, "wbf%d" % i, [128, DC, 512], BF16) for i in range(3)]
        cin = [sbuf(s2, "cin%d" % i, [128, T + 4], BF16) for i in range(2)]
        dgw = [sbuf(s2, "dgw%d" % i, [128, 4, 128], BF16) for i in range(2)]
        w_view = w_in.rearrange("(c p) e -> p c e", p=128)
        gcount = [0]

        def load_group(col0, ncols):
            b = gcount[0] % 3
            gcount[0] += 1
            P.dma("pool", lambda e: e.dma_start(out=wbf[b].h[:, :, 0:ncols], in_=w_view[:, :, col0:col0 + ncols]),
                  writes=[(wbf[b], "all")])
            return wbf[b]

        pj_banks = [2, 3, 4, 5]
        pj_i = [0]

        def proj_feat(wb, mc, tg, evac):
            pb = ps[pj_banks[pj_i[0] % 4]]
            pj_i[0] += 1
            for dc in range(DC):
                P.op("pe", lambda e, dc=dc, pb=pb: e.matmul(pb.h[:], lhsT=wb.h[:, dc, mc * 128:(mc + 1) * 128],
                                                         rhs=hT.h[:, dc, tg * 512:(tg + 1) * 512],
                                                         start=(dc == 0), stop=(dc == DC - 1)),
                     reads=[(wb, "all")] + HT_ALL[tg * 4:(tg + 1) * 4], writes=[(pb, "all")])
            evac(pb)

        def proj_tok(wb, tt, evac, ncols=512):
            pb = ps[pj_banks[pj_i[0] % 4]]
            pj_i[0] += 1
            for dc in range(DC):
                P.op("pe", lambda e, dc=dc, pb=pb: e.matmul(pb.h[:, 0:ncols], lhsT=hT.h[:, dc, tt * 128:(tt + 1) * 128],
                                                         rhs=wb.h[:, dc, 0:ncols],
                                                         start=(dc == 0), stop=(dc == DC - 1)),
                     reads=[(wb, "all"), (hT, tt)], writes=[(pb, "all")])
            evac(pb)

        ev_i = [0]

        def evac_copy(dst_ap, dst_rw, scale=None, ncols=512):
            def f(pb):
                ev_i[0] += 1
                src = pb.h[:, 0:ncols]
                if ev_i[0] % 2 == 0:
                    if scale is None:
                        P.op("act", lambda e: e.copy(out=dst_ap, in_=src), reads=[(pb, "all")], writes=dst_rw)
                    else:
                        P.op("act", lambda e: e.activation(out=dst_ap, in_=src, func=AF.Copy, scale=scale),
                             reads=[(pb, "all")], writes=dst_rw)
                else:
                    if scale is None:
                        P.op("dve", lambda e: e.tensor_copy(out=dst_ap, in_=src), reads=[(pb, "all")], writes=dst_rw)
                    else:
                        P.op("dve", lambda e: e.tensor_scalar(out=dst_ap, in0=src, scalar1=scale, scalar2=None,
                                                            op0=ALU.mult), reads=[(pb, "all")], writes=dst_rw)
            return f

        def gate_group(col0, half):
            wb = load_group(col0, 512)
            for tt in range(NT):
                def ev(pb, tt=tt):
                    P.op("act", lambda e: e.activation(out=mg.h[:, tt, half * 512:(half + 1) * 512], in_=pb.h[:],
                                                       func=AF.Silu),
                         reads=[(pb, "all")], writes=[(mg, (tt, half))])
                proj_tok(wb, tt, ev)

        for b2 in range(2):
            P.op("pool", lambda e, b2=b2: e.memset(cin[b2].h[:, 0:4], 0.0), writes=[(cin[b2], "pad")])
        cci = [0]

        PJB = [2, 3, 4]
        CVB = [5, 6]
        ONB = [7, 0]
        cnt3 = {"pj": 0, "cv": 0, "on": 0, "st": 0}

        class Ch:
            pass

        chunks = []
        for gi in range(3):
            for mc in range(4):
                c_ = Ch()
                c_.gi, c_.mc, c_.cc = gi, mc, gi * 4 + mc
                c_.ci = cin[len(chunks) % 2]
                c_.dg = dgw[len(chunks) % 2]
                c_.pbs = {}
                c_.sbufs = {}
                chunks.append(c_)
        wbs = {}

        def st_proj(c_, tg):
            if c_.gi not in wbs:
                wbs[c_.gi] = load_group(2048 + 512 * c_.gi, 512)
            wb_ = wbs[c_.gi]
            if tg == 0:
                for i in range(4):
                    P.op("dve", lambda e, i=i: e.tensor_scalar(out=c_.dg.h[:, i, :], in0=ident_f.h[:],
                                                             scalar1=cw.h[:, c_.cc, i:i + 1], scalar2=None, op0=ALU.mult),
                         reads=[(ident_f, "all"), (cw, "all")], writes=[(c_.dg, i)])
            pb = ps[PJB[cnt3["pj"] % 3]]
            cnt3["pj"] += 1
            for dc in range(DC):
                P.op("pe", lambda e, dc=dc: e.matmul(pb.h[:], lhsT=wb_.h[:, dc, c_.mc * 128:(c_.mc + 1) * 128],
                                                    rhs=hT.h[:, dc, tg * 512:(tg + 1) * 512],
                                                    start=(dc == 0), stop=(dc == DC - 1)),
                     reads=[(wb_, "all")] + HT_ALL[tg * 4:(tg + 1) * 4], writes=[(pb, "all")])
            evac_copy(c_.ci.h[:, 4 + tg * 512:4 + (tg + 1) * 512], [(c_.ci, tg)])(pb)

        def st_conv(c_, tg):
            ci, dg = c_.ci, c_.dg
            pb = ps[CVB[cnt3["cv"] % 2]]
            cnt3["cv"] += 1
            rd = [(ci, tg)] + ([(ci, tg - 1)] if tg > 0 else [(ci, "pad")])
            for i in range(4):
                P.op("pe", lambda e, i=i: e.matmul(
                    pb.h[:], lhsT=dg.h[:, i, :], rhs=ci.h[:, 1 + i + tg * 512:1 + i + (tg + 1) * 512],
                    start=(i == 0), stop=(i == 3)),
                    reads=rd + [(dg, i)], writes=[(pb, "all")])
            dst = gqkv.h[:, c_.cc, tg * 512:(tg + 1) * 512]
            act_fn(dst, pb.h[:], AF.Silu, [(pb, "all")], [(gqkv, (c_.cc, tg))])

        def st_ones(c_, tg):
            return

        NCHK = len(chunks)
        for i in range(NCHK + 1):
            cur = chunks[i] if i < NCHK else None
            prv = chunks[i - 1] if i >= 1 else None
            for tg in range(4):
                if cur is not None:
                    st_proj(cur, tg)
                if prv is not None:
                    st_conv(prv, tg)
                    if tg >= 1:
                        st_ones(prv, tg - 1)
            if prv is not None:
                st_ones(prv, 3)
        gate_group(3584, 1)
        wb = load_group(4096, 8)
        for tt in range(NT):
            proj_tok(wb, tt, evac_copy(bd.h[:, tt, :], [(bd, tt)], ncols=8), ncols=8)

        gate_group(1536, 0)

        sqn = [sbuf(s2, "sqn%d" % i, [128, 512], BF16) for i in range(3)]
        rn = [sbuf(s2, "rn%d" % i, [128, 512], F32) for i in range(2)]
        l2jobs = [(cc, tg) for cc in range(8) for tg in range(4)]
        L2B = [6, 7, 0, 1]

        def l2_s1(i):
            cc, tg = l2jobs[i]
            q_ = sqn[i % 3]
            src = gqkv.h[:, cc, tg * 512:(tg + 1) * 512]
            P.op("dve", lambda e: e.tensor_tensor(out=q_.h[:], in0=src, in1=src, op=ALU.mult),
                 reads=[(gqkv, (cc, tg))], writes=[(q_, "all")])

        def l2_s1b(i):
            q_ = sqn[i % 3]
            pb_ = ps[L2B[i % 4]]
            P.op("pe", lambda e: e.matmul(pb_.h[:], lhsT=ones_bf.h[:], rhs=q_.h[:], start=True, stop=True),
                 reads=[(ones_bf, "all"), (q_, "all")], writes=[(pb_, "all")])

        def l2_s2(i):
            pb_ = ps[L2B[i % 4]]
            r_ = rn[i % 2]
            act_fn(r_.h[:], pb_.h[:], AF.Ln, [(pb_, "all"), (eps_c, "all")], [(r_, "all")], bias=eps_c.h[:, 0:1])
            act_fn(r_.h[:], r_.h[:], AF.Exp, [], [(r_, "all")], scale=-0.5)

        def l2_s3(i):
            cc, tg = l2jobs[i]
            r_ = rn[i % 2]
            dst = gqkv.h[:, cc, tg * 512:(tg + 1) * 512]
            sc = (128.0 ** -0.5) if cc < 4 else 1.0
            P.op("dve", lambda e: e.scalar_tensor_tensor(out=dst, in0=dst, scalar=sc, in1=r_.h[:],
                                                        op0=ALU.mult, op1=ALU.mult),
                 reads=[(r_, "all")], writes=[(gqkv, (cc, tg))])

        l2t = [0]

        def l2_step():
            t_ = l2t[0]
            if t_ >= len(l2jobs) + 3:
                return
            l2t[0] += 1
            if 0 <= t_ - 1 < len(l2jobs):
                l2_s1b(t_ - 1)
            if t_ < len(l2jobs):
                l2_s1(t_)
            if 0 <= t_ - 2 < len(l2jobs):
                l2_s2(t_ - 2)
            if 0 <= t_ - 3 < len(l2jobs):
                l2_s3(t_ - 3)

        wb = load_group(0, 512)
        for mc in range(4):
            for tg in range(4):
                proj_feat(wb, mc, tg, evac_copy(qT.h[:, mc, tg * 512:(tg + 1) * 512], [(qT, (mc, tg))], scale=0.125))
                l2_step()
        wb = load_group(512, 512)
        for mc in range(4):
            for tg in range(4):
                proj_feat(wb, mc, tg, evac_copy(kT.h[:, mc, tg * 512:(tg + 1) * 512], [(kT, (mc, tg))]))
                l2_step()
        wb = load_group(1024, 512)
        for tt in range(NT):
            proj_tok(wb, tt, evac_copy(vS.h[:, tt, :], [(vS, tt)]))
            l2_step()
        while l2t[0] < len(l2jobs) + 3:
            l2_step()

        dbg_dump("qT", qT, [(mc, tg) for mc in range(4) for tg in range(4)], [128, 4, T], BF16)
        dbg_dump("gqkv", gqkv, [(cc, tg) for cc in range(12) for tg in range(4)], [128, 12, T], BF16)
        dbg_dump("bd", bd, list(range(NT)), [128, NT, 8], F32)
        P.barrier()
        s2.close()

        s4 = contextlib.ExitStack()
        maskSB = sbuf(s4, "maskSB", [128, 4, 512], BF16)
        for i in range(4):
            const_tri(maskSB, 0.0, ALU.is_gt, fill=NEG, pattern=[[1, 512]],
                      base=-128 * i, cm=-1, sl=maskSB.h[:, i, :])
        Epr = [sbuf(s4, "Ep%d" % i, [128, 2, 512], F32) for i in range(2)]
        SPp = [sbuf(s4, "SPp%d" % i, [128, 2, 512], BF16) for i in range(4)]
        SrF = [sbuf(s4, "SrF%d" % i, [128, 512], F32) for i in range(2)]
        SrB = [sbuf(s4, "SrB%d" % i, [128, 512], BF16) for i in range(3)]
        Apr = [sbuf(s4, "Ap%d" % i, [128, 2, 512], BF16) for i in range(4)]
        oSqs = [sbuf(s4, "oSq%d" % i, [128, 4, 512], F32) for i in range(2)]
        sqt = sbuf(s4, "sqt", [128, 4, 512], F32)
        ss4 = sbuf(s4, "ss4", [128, 32], F32)

        class Blk:
            pass

        pairs = []
        gidx = 0
        for qg in range(4):
            for h in range(8):
                n = 4 * (qg + 1)
                for jp in range(n // 2):
                    q_ = Blk()
                    q_.h, q_.qg, q_.n, q_.jp = h, qg, n, jp
                    q_.kb = [n - 1 - 2 * jp, n - 2 - 2 * jp]
                    q_.g = gidx
                    q_.idx = len(pairs)
                    q_.zi = q_.idx % 3
                    q_.zb = [ps[2 * q_.zi], ps[2 * q_.zi + 1]]
                    q_.E = Epr[q_.idx % 2]
                    q_.SP = SPp[q_.idx % 4]
                    q_.SrF = SrF[gidx % 2]
                    q_.SrBout = SrB[q_.idx % 3]
                    q_.SrBin = SrB[(q_.idx - 1) % 3]
                    q_.A = Apr[q_.idx % 4]
                    q_.ob = ps[6 + (gidx % 2)]
                    q_.cp = 128 * max(0, q_.kb[1] - 4 * qg)
                    q_.cpn = 128 * max(0, q_.kb[1] - 2 - 4 * qg)
                    q_.first = (jp == 0)
                    q_.last = (jp == n // 2 - 1)
                    pairs.append(q_)
                gidx += 1
        first_av = {}

        def head_norm_gate(src, nh, hd, wbc, tt0, ntt, col0, tmp, ssb, rkeys, tag, defer=None):
            width = nh * hd
            v3 = lambda ap: ap.rearrange("p a (h d) -> p (a h) d", d=hd)
            nn = ntt * nh
            half = col0 // 512
            mkeys = [(mg, (tt, half)) for tt in range(tt0, tt0 + ntt)]
            steps = [
                lambda: P.op("dve", lambda e: e.tensor_tensor(out=tmp.h[:, 0:ntt, 0:width], in0=src.h[:, 0:ntt, 0:width],
                                                            in1=src.h[:, 0:ntt, 0:width], op=ALU.mult),
                             reads=rkeys, writes=[(tmp, "all")]),
                lambda: (P.op("dve", lambda e: e.tensor_reduce(out=ssb.h[:, 0:nn], in_=v3(tmp.h[:, 0:ntt, 0:width]),
                                                             axis=AX.X, op=ALU.add),
                              reads=[(tmp, "all")], writes=[(ssb, "all")]),
                         rsqrt_small(ssb.h[:, 0:nn], ssb.h[:, 0:nn], 1.0 / hd, [], (ssb, "all"))),
                lambda: P.op("dve", lambda e: e.tensor_tensor(out=v3(tmp.h[:, 0:ntt, 0:width]), in0=v3(src.h[:, 0:ntt, 0:width]),
                                                            in1=ssb.h[:, 0:nn].unsqueeze(2).to_broadcast([128, nn, hd]),
                                                            op=ALU.mult),
                             reads=rkeys + [(ssb, "all")], writes=[(tmp, "all")]),
                lambda: P.op("dve", lambda e: e.tensor_tensor(out=v3(tmp.h[:, 0:ntt, 0:width]), in0=v3(tmp.h[:, 0:ntt, 0:width]),
                                                            in1=wbc.h[:, 0:hd].unsqueeze(1).to_broadcast([128, nn, hd]),
                                                            op=ALU.mult),
                             reads=[(wbc, "all")], writes=[(tmp, "all")]),
                lambda: P.op("dve", lambda e: e.tensor_tensor(out=mg.h[:, tt0:tt0 + ntt, col0:col0 + width],
                                                            in0=tmp.h[:, 0:ntt, 0:width],
                                                            in1=mg.h[:, tt0:tt0 + ntt, col0:col0 + width], op=ALU.mult),
                             reads=[(tmp, "all")], writes=mkeys),
            ]
            if defer is None:
                for f in steps:
                    f()
            else:
                defer.extend(steps)

        def zpair(q_):
            return psq[q_.zi][:, :].rearrange("p (b c) -> p b c", b=2)

        def ZRW(q_):
            return [(q_.zb[0], "all"), (q_.zb[1], "all")]

        def S1(q_):
            h, qg, cp = q_.h, q_.qg, q_.cp
            c, p0 = h // 2, 64 * (h % 2)
            for bi in range(2):
                kb = q_.kb[bi]
                zb = q_.zb[bi]
                diag = kb >= 4 * qg
                P.op("pe", lambda e, kb=kb, zb=zb, diag=diag: e.matmul(
                    zb.h[:, cp:], lhsT=kT.h[p0:p0 + 64, c, kb * 128:(kb + 1) * 128],
                    rhs=qT.h[p0:p0 + 64, c, qg * 512 + cp:(qg + 1) * 512], start=True, stop=not diag),
                    reads=[(qT, (c, qg)), (kT, (c, kb // 4))], writes=[(zb, "all")])
                if diag:
                    P.op("pe", lambda e, kb=kb, zb=zb: e.matmul(zb.h[:, cp:], lhsT=ident_bf.h[:],
                                                             rhs=maskSB.h[:, kb - 4 * qg, cp:], start=False, stop=True),
                         reads=[(ident_bf, "all"), (maskSB, "all")], writes=[(zb, "all")])

        def S2(q_):
            cp, E, SP_ = q_.cp, q_.E, q_.SP
            P.op("act", lambda e: e.activation(out=E.h[:, :, cp:], in_=zpair(q_)[:, :, cp:], func=AF.Exp),
                 reads=ZRW(q_), writes=[(E, "all")])
            P.op("act", lambda e: e.activation(out=SP_.h[:, :, cp:], in_=E.h[:, :, cp:], func=AF.Ln, bias=1.0),
                 reads=[(E, "all")], writes=[(SP_, "all")])

        def S3(q_):
            if q_.last:
                return
            F_, SP_, Bo, cp, cpn = q_.SrF, q_.SP, q_.SrBout, q_.cp, q_.cpn
            if q_.first:
                if cp > 0:
                    P.op("dve", lambda e: e.memset(F_.h[:, 0:cp], 0.0), writes=[(F_, "all")])
                P.op("dve", lambda e: e.tensor_tensor(out=F_.h[:, cp:], in0=SP_.h[:, 0, cp:], in1=SP_.h[:, 1, cp:], op=ALU.add),
                     reads=[(SP_, "all")], writes=[(F_, "all")])
            else:
                for bi in range(2):
                    P.op("dve", lambda e, bi=bi: e.tensor_tensor(out=F_.h[:, cp:], in0=F_.h[:, cp:], in1=SP_.h[:, bi, cp:],
                                                                op=ALU.add),
                         reads=[(SP_, "all")], writes=[(F_, "all")])
            P.op("dve", lambda e: e.tensor_copy(out=Bo.h[:, cpn:], in_=F_.h[:, cpn:]), reads=[(F_, "all")],
                 writes=[(Bo, "all")])

        def S4(q_):
            SP_, Bi, cp = q_.SP, q_.SrBin, q_.cp
            for bi in range(2):
                zb = q_.zb[bi]
                P.op("pe", lambda e, bi=bi, zb=zb: e.matmul(zb.h[:, cp:], lhsT=negU.h[:], rhs=SP_.h[:, bi, cp:],
                                                           start=False, stop=True, skip_group_check=True),
                     reads=[(negU, "all"), (SP_, "all")], writes=[(zb, "all")])
                if bi == 1:
                    P.op("pe", lambda e, zb=zb: e.matmul(zb.h[:, cp:], lhsT=negOnes.h[:], rhs=SP_.h[:, 0, cp:],
                                                        start=False, stop=True, skip_group_check=True),
                         reads=[(negOnes, "all"), (SP_, "all")], writes=[(zb, "all")])
                if not q_.first:
                    P.op("pe", lambda e, zb=zb: e.matmul(zb.h[:, cp:], lhsT=negOnes.h[:], rhs=Bi.h[:, cp:],
                                                        start=False, stop=True, skip_group_check=True),
                         reads=[(negOnes, "all"), (Bi, "all")], writes=[(zb, "all")])

        def S5(q_):
            cp, A_ = q_.cp, q_.A
            P.op("act", lambda e: e.activation(out=A_.h[:, :, cp:], in_=zpair(q_)[:, :, cp:], func=AF.Exp),
                 reads=ZRW(q_), writes=[(A_, "all")])

        def S6(q_):
            h, qg, ob, A_ = q_.h, q_.qg, q_.ob, q_.A
            for bi in range(2):
                kb = q_.kb[bi]
                for qc in range(4):
                    if kb - 4 * qg > qc:
                        continue
                    st_flag = q_.g not in first_av
                    first_av[q_.g] = True
                    P.op("pe", lambda e, qc=qc, st_flag=st_flag, bi=bi, kb=kb: e.matmul(
                        ob.h[:, qc * 64:(qc + 1) * 64], lhsT=A_.h[:, bi, qc * 128:(qc + 1) * 128],
                        rhs=vS.h[:, kb, h * 64:(h + 1) * 64], start=st_flag, stop=True, skip_group_check=True),
                        reads=[(A_, "all"), (vS, kb)], writes=[(ob, "all")])
            if q_.last:
                oSq = oSqs[qg % 2]
                P.op("dve", lambda e: e.tensor_copy(
                    out=oSq.h[:, :, h * 64:(h + 1) * 64],
                    in_=ob.h[:, 0:256].rearrange("p (a b) -> p a b", a=4)),
                    reads=[(ob, "all")], writes=[(oSq, h)])
                if h == 7:
                    if "oS" in dbg:
                        d = nc.dram_tensor("dbg_oS%d" % qg, [128, 4, 512], F32, kind="ExternalOutput").ap()
                        P.dma("sp", lambda e, d=d: e.dma_start(out=d, in_=oSq.h[:]),
                              reads=[(oSq, hh) for hh in range(8)], final=True)
                    head_norm_gate(oSq, 8, 64, sbw_bc, 4 * qg, 4, 0, sqt, ss4, [(oSq, hh) for hh in range(8)], "sb",
                                   defer=sb_defer)

        sb_defer = []
        NB = len(pairs)
        for t in range(NB + 3):
            if 0 <= t - 2 < NB:
                S4(pairs[t - 2])
            if t < NB:
                S1(pairs[t])
            if 0 <= t - 1 < NB:
                S2(pairs[t - 1])
                S3(pairs[t - 1])
            if 0 <= t - 2 < NB:
                S5(pairs[t - 2])
            if 0 <= t - 3 < NB:
                S6(pairs[t - 3])
            if sb_defer and t % 3 == 0:
                sb_defer.pop(0)()
        while sb_defer:
            sb_defer.pop(0)()
        P.barrier()
        s4.close()
        s1.close()

        if stage < 2.5:
            for tt in range(NT):
                P.op("pool", lambda e, tt=tt: e.memset(mg.h[:, tt, 512:1024], 0.0), writes=[(mg, (tt, 1))])
        else:
            s6 = contextlib.ExitStack()
            g6 = lambda name, shape, dt=F32: sbuf(s6, name, shape, dt)
            ones_f = g6("ones_f", [128, 128], F32)
            Lcum = g6("Lcum", [128, 128], F32)
            Lsel = g6("Lsel", [128, 128], F32)
            maskD = g6("maskD", [128, 128], F32)
            maskDs = g6("maskDs", [128, 128], F32)
            P.op("pool", lambda e: e.memset(ones_f.h[:], 1.0), writes=[(ones_f, "all")])
            const_tri(Lcum, 1.0, ALU.is_ge, pattern=[[1, 128]], cm=-1)
            P.op("pool", lambda e: e.memset(Lcum.h[0:64, 64:128], 0.0), writes=[(Lcum, "all")])
            for hf in range(2):
                const_tri(Lsel, 1.0, ALU.is_equal, pattern=[[0, 64]], cm=1, base=-(63 + 64 * hf),
                          sl=Lsel.h[:, 64 * hf:64 * hf + 64])
            const_tri(maskD, 0.0, ALU.is_ge, fill=NEG, pattern=[[-1, 128]], cm=1)
            P.op("pool", lambda e: e.memset(maskD.h[64:128, 0:64], NEG), writes=[(maskD, "all")])
            const_tri(maskDs, 0.0, ALU.is_gt, fill=NEG, pattern=[[-1, 128]], cm=1)
            P.op("pool", lambda e: e.memset(maskDs.h[64:128, 0:64], NEG), writes=[(maskDs, "all")])

            par = g6("par", [128, 8], F32)
            beta = g6("beta", [128, NT, 4], F32)
            gdec = g6("gdec", [128, NT, 4], F32)
            gc = g6("gc", [128, NT, 4], F32)
            ngc = g6("ngc", [128, NT, 4], F32)
            ekd = g6("ekd", [128, NT, 4], F32)
            begc = g6("begc", [128, NT, 4], F32)
            eglB = g6("eglB", [128, 4, 32], F32)
            P.dma("sp", lambda e: e.dma_start(out=par.h[:, 0:4], in_=dt_bias.partition_broadcast(128)), writes=[(par, "a")])
            P.dma("sp", lambda e: e.dma_start(out=par.h[:, 4:8], in_=A_log.partition_broadcast(128)), writes=[(par, "b")])
            act_fn(par.h[:, 4:8], par.h[:, 4:8], AF.Exp, [], [(par, "b")])
            P.op("dve", lambda e: e.tensor_scalar(out=par.h[:, 4:8], in0=par.h[:, 4:8], scalar1=-1.0, scalar2=None,
                                                  op0=ALU.mult), writes=[(par, "b")])
            BD_ALL = [(bd, tt) for tt in range(NT)]
            act_fn(beta.h[:], bd.h[:, :, 0:4], AF.Sigmoid, BD_ALL, [(beta, "all")])
            P.op("dve", lambda e: e.tensor_tensor(out=gdec.h[:], in0=bd.h[:, :, 4:8],
                                                  in1=par.h[:, 0:4].unsqueeze(1).to_broadcast([128, NT, 4]), op=ALU.add),
                 reads=BD_ALL + [(par, "a")], writes=[(gdec, "all")])
            act_fn(gdec.h[:], gdec.h[:], AF.Exp, [], [(gdec, "all")])
            act_fn(gdec.h[:], gdec.h[:], AF.Ln, [], [(gdec, "all")], bias=1.0)
            P.op("dve", lambda e: e.tensor_tensor(out=gdec.h[:], in0=gdec.h[:],
                                                  in1=par.h[:, 4:8].unsqueeze(1).to_broadcast([128, NT, 4]), op=ALU.mult),
                 reads=[(par, "b")], writes=[(gdec, "all")])
            flat = lambda b_: b_.h[:].rearrange("p a b -> p (a b)")
            pb = ps[6]
            P.op("pe", lambda e: e.matmul(pb.h[:, 0:64], lhsT=Lcum.h[:], rhs=flat(gdec), start=True, stop=True),
                 reads=[(Lcum, "all"), (gdec, "all")], writes=[(pb, "all")])
            P.op("dve", lambda e: e.tensor_copy(out=flat(gc), in_=pb.h[:, 0:64]), reads=[(pb, "all")], writes=[(gc, "all")])
            P.op("dve", lambda e: e.tensor_scalar(out=flat(ngc), in0=flat(gc), scalar1=-1.0, scalar2=None, op0=ALU.mult),
                 reads=[(gc, "all")], writes=[(ngc, "all")])
            pb2 = ps[7]
            P.op("pe", lambda e: e.matmul(pb2.h[:, 0:64], lhsT=Lsel.h[:], rhs=flat(gc), start=True, stop=True),
                 reads=[(Lsel, "all"), (gc, "all")], writes=[(pb2, "all")])
            P.op("dve", lambda e: e.tensor_tensor(out=flat(ekd), in0=pb2.h[:, 0:64], in1=flat(gc), op=ALU.subtract),
                 reads=[(pb2, "all"), (gc, "all")], writes=[(ekd, "all")])
            act_fn(ekd.h[:], ekd.h[:], AF.Exp, [], [(ekd, "all")])
            act_fn(begc.h[:], gc.h[:], AF.Exp, [(gc, "all")], [(begc, "all")])
            P.op("dve", lambda e: e.tensor_tensor(out=begc.h[:], in0=begc.h[:], in1=beta.h[:], op=ALU.mult),
                 reads=[(beta, "all")], writes=[(begc, "all")])
            gcb = g6("gcb", [128, NT, 4], F32)
            act_fn(gcb.h[:], beta.h[:], AF.Ln, [(beta, "all")], [(gcb, "all")])
            P.op("dve", lambda e: e.tensor_tensor(out=gcb.h[:], in0=gcb.h[:], in1=gc.h[:], op=ALU.add),
                 reads=[(gc, "all")], writes=[(gcb, "all")])
            dbg_dump("gc", gc, "all", [128, NT, 4], F32)
            dbg_dump("beta", beta, "all", [128, NT, 4], F32)

            for tt in range(NT):
                for h in range(4):
                    P.op("pool", lambda e, tt=tt, h=h: e.tensor_tensor(
                        out=mg.h[:, tt, 512 + h * 128:512 + (h + 1) * 128],
                        in0=mg.h[:, tt, 512 + h * 128:512 + (h + 1) * 128], in1=gnw_bc.h[:], op=ALU.mult),
                        reads=[(gnw_bc, "all")], writes=[(mg, (tt, 1))])
            uB = g6("uB", [128, NT, 4, 128], BF16)
            wT = g6("wT", [128, 4, T], BF16)
            qgT = g6("qgT", [128, 4, T], BF16)
            aiT = g6("aiT", [128, NT, 4, 64], BF16)
            kdB = g6("kdB", [128, NT, 4, 128], BF16)

            Sf = [g6("Sf%d" % h, [128, 128], F32) for h in range(4)]
            Sb = [g6("Sb%d" % h, [128, 128], BF16) for h in range(4)]
            vnwA = g6("vnwA", [128, 4, 128], BF16)
            oG = g6("oG", [128, 1, 512], F32)
            sqg = g6("sqg", [128, 1, 128], F32)
            ssg = g6("ssg", [128, 4], F32)
            for h in range(4):
                P.op("pool", lambda e, h=h: e.memset(Sf[h].h[:], 0.0), writes=[(Sf[h], "all")])
                P.op("pool", lambda e, h=h: e.memset(Sb[h].h[:], 0.0), writes=[(Sb[h], "all")])
            s7 = contextlib.ExitStack()
            g7 = lambda name, shape, dt=F32: sbuf(s7, name, shape, dt)
            NCH = 3
            dgc = [g7("dgc%d" % i, [128, 128], F32) for i in range(2)]
            GBm3 = g7("GBm", [128, 1, 512], F32)
            EGt3 = g7("EGt", [128, 1, 512], F32)
            GBs = [g7("GBs%d" % i, [128, 512], F32) for i in range(NCH)]
            Ab = [g7("Ab%d" % i, [128, 512], BF16) for i in range(NCH)]
            ATb = [g7("ATb%d" % i, [128, 512], BF16) for i in range(NCH)]
            Pb = [[g7("Pb%d_%d" % (i, j), [128, 512], BF16) for j in range(2)] for i in range(NCH)]
            PTb = [[g7("PTb%d_%d" % (i, j), [128, 512], BF16) for j in range(2)] for i in range(NCH)]
            Gb = [[g7("Gb%d" % i, [128, 512], BF16)] * 2 for i in range(NCH)]
            Dm = [PTb[i][1] for i in range(NCH)]
            AIb = [Pb[i][1] for i in range(NCH)]
            vb = [g7("vb%d" % i, [128, 512], BF16) for i in range(NCH)]
            rw = [g7("rw%d" % i, [128, 512], BF16) for i in range(NCH)]
            bank_i = [0]

            for i_, b_ in enumerate(ps):
                b_.bidx = i_
            free_banks = [4, 5, 6, 7]

            def get_banks(n):
                while len(free_banks) < n:
                    yield
                return [ps[free_banks.pop(0)] for _ in range(n)]

            def rel(*bs):
                for b_ in bs:
                    free_banks.append(b_.bidx)

            dgi = [0]

            class Chain:
                pass

            def chain_steps(h, tq, ci):
                tts = [4 * tq + k for k in range(4)]
                sl = lambda k: slice(k * 128, (k + 1) * 128)
                tsl = lambda k: slice(tts[k] * 128, (tts[k] + 1) * 128)
                qn = lambda k: gqkv.h[:, h, tsl(k)]
                kn = lambda k: gqkv.h[:, 4 + h, tsl(k)]
                vn_ = lambda k: gqkv.h[:, 8 + h, tsl(k)]
                QR = [(gqkv, (h, tq))]
                KR = [(gqkv, (4 + h, tq))]
                VR = [(gqkv, (8 + h, tq))]
                (pGB,) = yield from get_banks(1)
                for k in range(4):
                    d_ = dgc[dgi[0] % 2]
                    dgi[0] += 1
                    P.op("act", lambda e, d_=d_, k=k: e.activation(out=d_.h[:], in_=ident_f.h[:], func=AF.Copy,
                                                                  scale=gc.h[:, tts[k], h:h + 1]),
                         reads=[(ident_f, "all"), (gc, "all")], writes=[(d_, "all")])
                    P.op("pe", lambda e, d_=d_, k=k: e.matmul(pGB.h[:, sl(k)], lhsT=ones_f.h[:], rhs=d_.h[:],
                                                            start=True, stop=True),
                         reads=[(ones_f, "all"), (d_, "all")], writes=[(pGB, "all")])
                yield
                v4 = lambda ap: ap.rearrange("p (a b) -> p a b", a=4)
                P.op("dve", lambda e: e.tensor_tensor(out=v4(GBm3.h[:, 0, :]), in0=v4(pGB.h[:]),
                                                      in1=maskD.h[:].unsqueeze(1).to_broadcast([128, 4, 128]),
                                                      op=ALU.subtract),
                     reads=[(pGB, "all"), (maskD, "all")], writes=[(GBm3, "all")])
                P.op("dve", lambda e: e.tensor_tensor(out=v4(GBs[ci].h[:]), in0=v4(pGB.h[:]),
                                                      in1=maskDs.h[:].unsqueeze(1).to_broadcast([128, 4, 128]),
                                                      op=ALU.subtract),
                     reads=[(pGB, "all"), (maskDs, "all")], writes=[(GBs[ci], "all")])
                act_fn(EGt3.h[:, 0, :], pGB.h[:], AF.Exp, [(pGB, "all")], [(EGt3, "all")])
                P.op("dve", lambda e: e.tensor_tensor(out=qgT.h[:, h, tq * 512:(tq + 1) * 512],
                                                      in0=gqkv.h[:, h, tq * 512:(tq + 1) * 512], in1=EGt3.h[:, 0, :],
                                                      op=ALU.mult),
                     reads=QR + [(EGt3, "all")], writes=[(qgT, (h, tq))])
                P.op("dve", lambda e: e.tensor_copy(out=eglB.h[:, h, 8 * tq:8 * tq + 8],
                                                    in_=EGt3.h[:, 0, :].rearrange("p (c t) -> p c t", t=64)[:, :, 63]),
                     reads=[(EGt3, "all")], writes=[(eglB, (h, tq))])
                for k in range(4):
                    act_fn(Dm[ci].h[:, sl(k)], GBm3.h[:, 0, sl(k)], AF.Exp, [(GBm3, "all"), (gc, "all")],
                           [(Dm[ci], "all")], bias=gc.h[:, tts[k], h:h + 1], scale=-1.0)
                    act_fn(GBs[ci].h[:, sl(k)], GBs[ci].h[:, sl(k)], AF.Exp, [(gcb, "all")],
                           [(GBs[ci], "all")], bias=gcb.h[:, tts[k], h:h + 1], scale=-1.0)
                rel(pGB)
                pK, pQ = yield from get_banks(2)
                for k in range(4):
                    P.op("pe", lambda e, k=k: e.matmul(pK.h[:, sl(k)], lhsT=kn(k), rhs=kn(k), start=True, stop=True),
                         reads=KR, writes=[(pK, "all")])
                for k in range(4):
                    P.op("pe", lambda e, k=k: e.matmul(pQ.h[:, sl(k)], lhsT=qn(k), rhs=kn(k), start=True, stop=True),
                         reads=KR + QR, writes=[(pQ, "all")])
                yield
                P.op("dve", lambda e: e.tensor_tensor(out=Ab[ci].h[:], in0=pK.h[:], in1=GBs[ci].h[:], op=ALU.mult),
                     reads=[(pK, "all"), (GBs[ci], "all")], writes=[(Ab[ci], "all")])
                P.op("dve", lambda e: e.tensor_tensor(out=AIb[ci].h[:], in0=pQ.h[:], in1=Dm[ci].h[:], op=ALU.mult),
                     reads=[(pQ, "all"), (Dm[ci], "all")], writes=[(AIb[ci], "all")])
                rel(pK, pQ)
                pT1, pT2 = yield from get_banks(2)
                pT1b = pT1.h[:].bitcast(BF16)
                for k in range(4):
                    P.op("pe", lambda e, k=k: e.transpose(out=pT1b[:, sl(k)], in_=Ab[ci].h[:, sl(k)], identity=ident_bf.h[:]),
                         reads=[(Ab[ci], "all"), (ident_bf, "all")], writes=[(pT1, "all")])
                for k in range(4):
                    P.op("pe", lambda e, k=k: e.transpose(out=pT1b[:, 512 + k * 128:512 + (k + 1) * 128],
                                                         in_=AIb[ci].h[:, sl(k)], identity=ident_bf.h[:]),
                         reads=[(AIb[ci], "all"), (ident_bf, "all")], writes=[(pT1, "all")])
                pT2b = pT2.h[:].bitcast(BF16)
                for k in range(4):
                    P.op("pe", lambda e, k=k: e.transpose(out=pT2b[:, sl(k)], in_=kn(k), identity=ident_bf.h[:]),
                         reads=KR + [(ident_bf, "all")], writes=[(pT2, "all")])
                for k in range(4):
                    P.op("pe", lambda e, k=k: e.transpose(out=pT2b[:, 512 + k * 128:512 + (k + 1) * 128], in_=vn_(k),
                                                         identity=ident_bf.h[:]),
                         reads=VR + [(ident_bf, "all")], writes=[(pT2, "all")])
                yield
                act_fn(ATb[ci].h[:], pT1b[:, 0:512], AF.Copy, [(pT1, "all")], [(ATb[ci], "all")])
                for hf_ in range(2):
                    rr = slice(64 * hf_, 64 * hf_ + 64)
                    P.op("dve", lambda e, rr=rr: e.tensor_copy(
                        out=aiT.h[rr, tts[0]:tts[0] + 4, h, :],
                        in_=pT1b[rr, 512:1024].rearrange("p (a b) -> p a b", a=4)[:, :, rr]),
                        reads=[(pT1, "all")], writes=[(aiT, (h, tq, hf_))])
                G0 = Gb[ci][0]
                P.op("dve", lambda e: e.tensor_tensor(
                    out=G0.h[:].rearrange("p (a b) -> p a b", a=4),
                    in0=ident_bf.h[:].unsqueeze(1).to_broadcast([128, 4, 128]),
                    in1=ATb[ci].h[:].rearrange("p (a b) -> p a b", a=4), op=ALU.subtract),
                    reads=[(ATb[ci], "all"), (ident_bf, "all")], writes=[(G0, "all")])
                for k in range(4):
                    act_fn(rw[ci].h[:, sl(k)], pT2b[:, sl(k)], AF.Copy, [(pT2, "all"), (begc, "all")], [(rw[ci], k)],
                           scale=begc.h[:, tts[k], h:h + 1])
                    act_fn(kdB.h[:, tts[k], h, :], pT2b[:, sl(k)], AF.Copy, [(pT2, "all"), (ekd, "all")],
                           [(kdB, (h, tts[k]))], scale=ekd.h[:, tts[k], h:h + 1])
                    act_fn(vb[ci].h[:, sl(k)], pT2b[:, 512 + k * 128:512 + (k + 1) * 128], AF.Copy,
                           [(pT2, "all"), (beta, "all")], [(vb[ci], k)], scale=beta.h[:, tts[k], h:h + 1])
                rel(pT1, pT2)
                Pc, PTc = Ab[ci], ATb[ci]
                Gc = G0
                for lv in range(5):
                    last = (lv == 4)
                    Pn = Pb[ci][lv % 2]
                    PTn = PTb[ci][lv % 2]
                    Gn = Gb[ci][(lv + 1) % 2]
                    if last:
                        (pP,) = yield from get_banks(1)
                    else:
                        pP, pPT = yield from get_banks(2)
                    for k in range(4):
                        P.op("pe", lambda e, k=k, pP=pP, Pc=Pc, PTc=PTc: e.matmul(
                            pP.h[:, sl(k)], lhsT=PTc.h[:, sl(k)], rhs=Pc.h[:, sl(k)], start=True, stop=True),
                            reads=[(Pc, "all"), (PTc, "all")], writes=[(pP, "all")])
                    if not last:
                        for k in range(4):
                            P.op("pe", lambda e, k=k, pPT=pPT, Pc=Pc, PTc=PTc: e.matmul(
                                pPT.h[:, sl(k)], lhsT=Pc.h[:, sl(k)], rhs=PTc.h[:, sl(k)], start=True, stop=True),
                                reads=[(Pc, "all"), (PTc, "all")], writes=[(pPT, "all")])
                    yield
                    act_fn(Pn.h[:], pP.h[:], AF.Copy, [(pP, "all")], [(Pn, "all")])
                    if not last:
                        P.op("dve", lambda e, PTn=PTn, pPT=pPT: e.tensor_copy(out=PTn.h[:], in_=pPT.h[:]),
                             reads=[(pPT, "all")], writes=[(PTn, "all")])
                    rel(pP)
                    if not last:
                        rel(pPT)
                    (pG,) = yield from get_banks(1)
                    for k in range(4):
                        P.op("pe", lambda e, k=k, pG=pG, Pn=Pn, Gc=Gc: e.matmul(
                            pG.h[:, sl(k)], lhsT=Pn.h[:, sl(k)], rhs=Gc.h[:, sl(k)], start=True, stop=True),
                            reads=[(Pn, "all"), (Gc, "all")], writes=[(pG, "all")])
                    yield
                    P.op("dve", lambda e, pG=pG, Gc=Gc, Gn=Gn: e.tensor_tensor(out=Gn.h[:], in0=pG.h[:], in1=Gc.h[:], op=ALU.add),
                         reads=[(pG, "all"), (Gc, "all")], writes=[(Gn, "all")])
                    rel(pG)
                    Pc, PTc, Gc = Pn, PTn, Gn
                GT = Gc
                pU, pW = yield from get_banks(2)
                for k in range(4):
                    P.op("pe", lambda e, k=k: e.matmul(pU.h[:, sl(k)], lhsT=GT.h[:, sl(k)], rhs=vb[ci].h[:, sl(k)],
                                                      start=True, stop=True),
                         reads=[(GT, "all"), (vb[ci], k)], writes=[(pU, "all")])
                for k in range(4):
                    P.op("pe", lambda e, k=k: e.matmul(pW.h[:, sl(k)], lhsT=rw[ci].h[:, sl(k)], rhs=GT.h[:, sl(k)],
                                                      start=True, stop=True),
                         reads=[(GT, "all"), (rw[ci], k)], writes=[(pW, "all")])
                yield
                act_fn(uB.h[:, tts[0]:tts[0] + 4, h, :], pU.h[:].rearrange("p (a b) -> p a b", a=4), AF.Copy,
                       [(pU, "all")], [(uB, (h, tq))])
                P.op("dve", lambda e: e.tensor_copy(out=wT.h[:, h, tq * 512:(tq + 1) * 512], in_=pW.h[:]),
                     reads=[(pW, "all")], writes=[(wT, (h, tq))])
                rel(pU, pW)

            jobs = [(h, tq) for tq in range(4) for h in range(4)]
            if cut < 99:
                jobs = jobs[:NCH] if cut > 0 else []
            dbg_dump("uB", uB, [(h, tq) for h in range(4) for tq in range(4)], [128, NT, 4, 128], BF16)
            dbg_dump("wT", wT, [(h, tq) for h in range(4) for tq in range(4)], [128, 4, T], BF16)


            def tail_load(tt):
                b = tt % 2
                P.dma("sp", lambda e: e.dma_start(out=xr[b].h[:], in_=x[tt * 128:(tt + 1) * 128, :]),
                      writes=[(xr[b], "all")])

            DENSE_TAIL = True
            tasks = []
            emitted = {}
            slot = [0]
            last_ids = {}

            def add_task(kind, fn, deps, name=None, delay=0):
                tid = len(emitted) + len(tasks)
                tasks.append({"kind": kind, "fn": fn, "deps": [d for d in deps if d is not None], "id": tid,
                              "nb": slot[0] + delay})
                if name is not None:
                    last_ids[name] = tid
                return tid

            def pump(kind, n=1):
                slot[0] += 1
                cnt = 0
                i = 0
                while i < len(tasks):
                    t_ = tasks[i]
                    ok = all((d in emitted) and emitted[d] < slot[0] for d in t_["deps"]) and slot[0] > t_["nb"]
                    if ok and (t_["kind"] == "act" or (t_["kind"] == kind and cnt < n)):
                        if t_["kind"] != "act":
                            cnt += 1
                        tasks.pop(i)
                        t_["fn"]()
                        emitted[t_["id"]] = slot[0]
                        continue
                    i += 1

            def force_tasks(pred):
                last = -1
                for i, t_ in enumerate(tasks):
                    if pred(t_):
                        last = i
                slot[0] += 1
                for _ in range(last + 1):
                    t_ = tasks.pop(0)
                    t_["fn"]()
                    emitted[t_["id"]] = slot[0]

            def add_tile_tasks(tt, po):
                b = tt % 2
                pb = ps[6]
                pbf = pb.h[:].bitcast(BF16)
                po2 = ps[7]

                def n2(hs, gate):
                    def f():
                        for h in hs:
                            P.op("act", lambda e, h=h: e.activation(
                                out=oG.h[:, 0, h * 128:(h + 1) * 128], in_=po.h[:, h * 128:(h + 1) * 128],
                                func=AF.Copy, scale=ssg.h[:, h:h + 1]),
                                reads=[(po, "all"), (ssg, "all")], writes=[(oG, "all")])
                        if gate:
                            P.op("pool", lambda e: e.tensor_tensor(out=mg.h[:, tt, 512:1024], in0=oG.h[:, 0, :],
                                                                   in1=mg.h[:, tt, 512:1024], op=ALU.mult),
                                 reads=[(oG, "all")], writes=[(mg, (tt, 1))])
                    return f

                def tr(ecs, evac):
                    def f():
                        for ec in ecs:
                            P.op("pe", lambda e, ec=ec: e.transpose(out=pbf[:, ec * 128:(ec + 1) * 128],
                                                                   in_=mg.h[:, tt, ec * 128:(ec + 1) * 128],
                                                                   identity=ident_bf.h[:]),
                                 reads=[(mg, (tt, ec // 4)), (ident_bf, "all")], writes=[(pb, "all")])
                    return f

                def evac_mT():
                    src = pbf.rearrange("p (c t) -> p c t", c=DC)
                    P.op("act", lambda e: e.copy(out=mT[b].h[:], in_=src), reads=[(pb, "all")], writes=[(mT[b], "all")])

                def op_mm(hf, ecs):
                    def f():
                        for ec in ecs:
                            P.op("pe", lambda e, ec=ec: e.matmul(
                                po2.h[:], lhsT=mT[b].h[:, ec, :], rhs=wo_bf.h[:, ec, hf * 512:(hf + 1) * 512],
                                start=(ec == 0), stop=(ec == DC - 1)),
                                reads=[(mT[b], "all"), (wo_bf, (ec, hf))], writes=[(po2, "all")])
                    return f

                def add_res(hf):
                    def f():
                        P.op("dve", lambda e: e.tensor_tensor(
                            out=x2[b].h[:, hf * 512:(hf + 1) * 512], in0=po2.h[:], in1=xr[b].h[:, hf * 512:(hf + 1) * 512],
                            op=ALU.add), reads=[(po2, "all"), (xr[b], "all")], writes=[(x2[b], hf)])
                    return f

                def fin():
                    P.op("act", lambda e: e.activation(out=junkT.h[:], in_=x2[b].h[:], func=AF.Square,
                                                       accum_out=ssq2.h[:, tt:tt + 1]),
                         reads=[(x2[b], 0), (x2[b], 1)], writes=[(junkT, "all"), (ssq2, tt)])
                    rsqrt_small(ssq2.h[:, tt:tt + 1], ssq2.h[:, tt:tt + 1], 1.0 / D, [], (ssq2, tt))
                    P.op("act", lambda e: e.activation(out=x2[b].h[:], in_=x2[b].h[:], func=AF.Copy,
                                                       scale=ssq2.h[:, tt:tt + 1]),
                         reads=[(ssq2, tt)], writes=[(x2[b], 0), (x2[b], 1)])
                    P.op("pool", lambda e: e.tensor_tensor(out=x2[b].h[:], in0=x2[b].h[:], in1=fnw_bc.h[:], op=ALU.mult),
                         reads=[(fnw_bc, "all")], writes=[(x2[b], 0), (x2[b], 1)])
                    P.dma("sp", lambda e: e.dma_start(out=y[tt * 128:(tt + 1) * 128, :], in_=x2[b].h[:]),
                          reads=[(x2[b], 0), (x2[b], 1)], final=True)
                    if tt + 2 < NT:
                        tail_load(tt + 2)

                L = last_ids.get
                n2b = None
                if po is not None:
                    n2a = add_task("act", n2([0, 1], False), [], "n2a", delay=1)
                    n2b = add_task("act", n2([2, 3], True), [n2a], "n2b")
                    tasks[-1]["is_n2"] = True
                    tasks[-2]["is_n2"] = True
                if DENSE_TAIL and po is not None:
                    return
                t1 = add_task("pe", tr([0, 1, 2, 3], False), [n2b, L("tev")])
                t2 = add_task("pe", tr([4, 5, 6, 7], True), [n2b, L("tev")])
                tev = add_task("act", evac_mT, [t1, t2, L("m1b")], "tev")
                m0a = add_task("pe", op_mm(0, [0, 1, 2, 3]), [tev, L("r1")])
                m0b = add_task("pe", op_mm(0, [4, 5, 6, 7]), [tev, L("r1")])
                r0 = add_task("dve", add_res(0), [m0b, L("fin2")])
                m1a = add_task("pe", op_mm(1, [0, 1, 2, 3]), [r0])
                m1b = add_task("pe", op_mm(1, [4, 5, 6, 7]), [r0], "m1b")
                r1 = add_task("dve", add_res(1), [m1b], "r1")
                last_ids["fin2"] = last_ids.get("fin1")
                add_task("act", fin, [r1], "fin1")

            def drain_tasks():
                force_tasks(lambda t_: True)

            def scan_chunk(c):
                tt, hf = c // 2, c % 2
                r0 = 64 * hf
                rs = slice(r0, r0 + 64)
                cs = slice(c * 64, (c + 1) * 64)
                pv = ps[0]
                po = ps[2 + (tt % 2)]
                pS = ps[1]
                for h in range(4):
                    P.op("pe", lambda e, h=h: e.matmul(pv.h[rs, h * 128:(h + 1) * 128], lhsT=wT.h[:, h, cs], rhs=Sb[h].h[:],
                                                      start=True, stop=True, skip_group_check=True),
                         reads=[(wT, (h, tt // 4)), (Sb[h], "all")], writes=[(pv, "all")])
                for h in range(4):
                    P.op("pe", lambda e, h=h: e.matmul(po.h[rs, h * 128:(h + 1) * 128], lhsT=qgT.h[:, h, cs], rhs=Sb[h].h[:],
                                                      start=(h == 0), stop=False, skip_group_check=True),
                         reads=[(qgT, (h, tt // 4)), (Sb[h], "all")], writes=[(po, "all")])
                pump("pe", 2)
                yield
                P.op("dve", lambda e: e.tensor_tensor(out=vnwA.h[rs, :, :].rearrange("p a b -> p (a b)"),
                                                      in0=uB.h[rs, tt, :, :].rearrange("p a b -> p (a b)"),
                                                      in1=pv.h[rs, :], op=ALU.subtract),
                     reads=[(uB, (h, tt // 4)) for h in range(4)] + [(pv, "all")], writes=[(vnwA, hf)])
                pump("dve", 1)
                yield
                for h in range(4):
                    P.op("pe", lambda e, h=h: e.matmul(pS.h[:, h * 128:(h + 1) * 128], lhsT=kdB.h[rs, tt, h, :],
                                                      rhs=vnwA.h[rs, h, :], start=True, stop=True, skip_group_check=True),
                         reads=[(kdB, (h, tt)), (vnwA, hf)], writes=[(pS, "all")])
                for h in range(4):
                    P.op("pe", lambda e, h=h: e.matmul(po.h[rs, h * 128:(h + 1) * 128], lhsT=aiT.h[rs, tt, h, :],
                                                      rhs=vnwA.h[rs, h, :], start=False, stop=True, skip_group_check=True),
                         reads=[(aiT, (h, tt // 4, hf)), (vnwA, hf)], writes=[(po, "all")])
                pump("pe", 1)
                yield
                for h in range(4):
                    P.op("dve", lambda e, h=h: e.scalar_tensor_tensor(
                        out=Sb[h].h[:], in0=Sf[h].h[:], scalar=eglB.h[:, h, c:c + 1], in1=pS.h[:, h * 128:(h + 1) * 128],
                        op0=ALU.mult, op1=ALU.add),
                        reads=[(eglB, (h, tt // 4)), (pS, "all"), (Sf[h], "all")], writes=[(Sb[h], "all")])
                for h in range(4):
                    P.op("dve", lambda e, h=h: e.scalar_tensor_tensor(
                        out=Sf[h].h[:], in0=Sf[h].h[:], scalar=eglB.h[:, h, c:c + 1], in1=pS.h[:, h * 128:(h + 1) * 128],
                        op0=ALU.mult, op1=ALU.add),
                        reads=[(eglB, (h, tt // 4)), (pS, "all")], writes=[(Sf[h], "all")])
                if hf == 1:
                    if "oG" in dbg:
                        P.op("act", lambda e: e.copy(out=oG.h[:, 0, :], in_=po.h[:]),
                             reads=[(po, "all")], writes=[(oG, "all")])
                        d = nc.dram_tensor("dbg_oG%d" % tt, [128, 512], F32, kind="ExternalOutput").ap()
                        P.dma("sp", lambda e: e.dma_start(out=d, in_=oG.h[:, 0, :]), reads=[(oG, "all")], final=True)
                    force_tasks(lambda t_: t_.get("is_n2", False))
                    for h in range(4):
                        P.op("act", lambda e, h=h: e.activation(out=sqg.h[:, 0, :],
                                                              in_=po.h[:, h * 128:(h + 1) * 128], func=AF.Square,
                                                              accum_out=ssg.h[:, h:h + 1]),
                             reads=[(po, "all")], writes=[(sqg, "all"), (ssg, "all")])
                    rsqrt_small(ssg.h[:, 0:4], ssg.h[:, 0:4], 1.0 / 128, [], (ssg, "all"))
                    add_tile_tasks(tt, po)
                pump("dve", 1)

            if stage >= 3:
                pending = list(jobs)
                slots = [None] * NCH
                slot_job = [None] * NCH
                STAGGER = 7
                start_round = [i * STAGGER for i in range(NCH)]
                done_tq = [0, 0, 0, 0]
                scan_gen = [None]
                next_chunk = [0]

                def ready_upto():
                    k = 0
                    while k < 4 and done_tq[k] == 4:
                        k += 1
                    return 8 * k

                def scan_step():
                    if scan_gen[0] is None:
                        if next_chunk[0] >= ready_upto():
                            return False
                        scan_gen[0] = scan_chunk(next_chunk[0])
                        next_chunk[0] += 1
                    try:
                        next(scan_gen[0])
                    except StopIteration:
                        scan_gen[0] = None
                    return True

                rnd = 0
                while pending or any(g_ is not None for g_ in slots):
                    for ci in range(NCH):
                        if slots[ci] is None and pending and rnd >= start_round[ci]:
                            h_, tq_ = pending.pop(0)
                            slots[ci] = chain_steps(h_, tq_, ci)
                            slot_job[ci] = tq_
                        if slots[ci] is not None:
                            try:
                                next(slots[ci])
                            except StopIteration:
                                slots[ci] = None
                                done_tq[slot_job[ci]] += 1
                    scan_step()
                    rnd += 1
                P.barrier()
                s7.close()
                fnw_bc = g6("fnw_bc", [128, D], F32)
                P.dma("sp", lambda e: e.dma_start(out=fnw_bc.h[:], in_=final_norm_w.partition_broadcast(128)),
                      writes=[(fnw_bc, "all")])
                wo_bf = g6("wo_bf", [128, DC, D], BF16)
                mT = [g6("mT0", [128, DC, 128], BF16)] * 2
                xr = [g6("xr%d" % i, [128, D], F32) for i in range(2)]
                x2 = [g6("x2%d" % i, [128, D], F32) for i in range(2)]
                ssq2 = g6("ssq2", [128, NT], F32)
                junkT = g6("junkT", [128, D], BF16)
                wo_view = w_out.rearrange("(c p) e -> p c e", p=128)
                for hf in range(2):
                    P.dma("pool", lambda e, hf=hf: e.dma_start(out=wo_bf.h[:, :, hf * 512:(hf + 1) * 512],
                                                             in_=wo_view[:, :, hf * 512:(hf + 1) * 512]),
                          writes=[(wo_bf, (dc, hf)) for dc in range(DC)])


                tail_load(0)
                tail_load(1)
                while next_chunk[0] < 32 or scan_gen[0] is not None:
                    scan_step()
                drain_tasks()
                if DENSE_TAIL:
                    mTd = [mT[0], oG]

                    def mT_ap(i):
                        if i == 0:
                            return mT[0].h[:]
                        return oG.h[:, 0, :].bitcast(BF16).rearrange("p (c t) -> p c t", c=DC)

                    def TA(tt):
                        pb = ps[tt % 2]
                        pbf = pb.h[:].bitcast(BF16)
                        for ec in range(DC):
                            P.op("pe", lambda e, ec=ec: e.transpose(out=pbf[:, ec * 128:(ec + 1) * 128],
                                                                   in_=mg.h[:, tt, ec * 128:(ec + 1) * 128],
                                                                   identity=ident_bf.h[:]),
                                 reads=[(mg, (tt, ec // 4)), (ident_bf, "all")], writes=[(pb, "all")])
                        src = pbf.rearrange("p (c t) -> p c t", c=DC)
                        mb = mTd[tt % 2]
                        P.op("act", lambda e: e.copy(out=mT_ap(tt % 2), in_=src), reads=[(pb, "all")], writes=[(mb, "all")])

                    def TB(tt):
                        b = tt % 2
                        mb = mTd[tt % 2]
                        for hf in range(2):
                            po2 = ps[2 + (2 * tt + hf) % 4]
                            for ec in range(DC):
                                P.op("pe", lambda e, ec=ec, hf=hf, po2=po2: e.matmul(
                                    po2.h[:], lhsT=mT_ap(tt % 2)[:, ec, :], rhs=wo_bf.h[:, ec, hf * 512:(hf + 1) * 512],
                                    start=(ec == 0), stop=(ec == DC - 1)),
                                    reads=[(mb, "all"), (wo_bf, (ec, hf))], writes=[(po2, "all")])
                            P.op("dve", lambda e, hf=hf, po2=po2: e.tensor_tensor(
                                out=x2[b].h[:, hf * 512:(hf + 1) * 512], in0=po2.h[:],
                                in1=xr[b].h[:, hf * 512:(hf + 1) * 512], op=ALU.add),
                                reads=[(po2, "all"), (xr[b], "all")], writes=[(x2[b], hf)])

                    def TC(tt):
                        b = tt % 2
                        P.op("act", lambda e: e.activation(out=junkT.h[:], in_=x2[b].h[:], func=AF.Square,
                                                           accum_out=ssq2.h[:, tt:tt + 1]),
                             reads=[(x2[b], 0), (x2[b], 1)], writes=[(junkT, "all"), (ssq2, tt)])
                        rsqrt_small(ssq2.h[:, tt:tt + 1], ssq2.h[:, tt:tt + 1], 1.0 / D, [], (ssq2, tt))
                        P.op("act", lambda e: e.activation(out=x2[b].h[:], in_=x2[b].h[:], func=AF.Copy,
                                                           scale=ssq2.h[:, tt:tt + 1]),
                             reads=[(ssq2, tt)], writes=[(x2[b], 0), (x2[b], 1)])
                        P.op("pool", lambda e: e.tensor_tensor(out=x2[b].h[:], in0=x2[b].h[:], in1=fnw_bc.h[:], op=ALU.mult),
                             reads=[(fnw_bc, "all")], writes=[(x2[b], 0), (x2[b], 1)])
                        P.dma("sp", lambda e: e.dma_start(out=y[tt * 128:(tt + 1) * 128, :], in_=x2[b].h[:]),
                              reads=[(x2[b], 0), (x2[b], 1)], final=True)
                        if tt + 2 < NT:
                            tail_load(tt + 2)

                    for t in range(NT + 2):
                        if t < NT:
                            TA(t)
                        if 0 <= t - 1 < NT:
                            TB(t - 1)
                        if 0 <= t - 2 < NT:
                            TC(t - 2)
            else:
                pending = list(jobs)
                slots = [None] * NCH
                STAGGER = 7
                start_round = [i * STAGGER for i in range(NCH)]
                rnd = 0
                while pending or any(g_ is not None for g_ in slots):
                    for ci in range(NCH):
                        if slots[ci] is None and pending and rnd >= start_round[ci]:
                            h_, tq_ = pending.pop(0)
                            slots[ci] = chain_steps(h_, tq_, ci)
                        if slots[ci] is not None:
                            try:
                                next(slots[ci])
                            except StopIteration:
                                slots[ci] = None
                    rnd += 1

                P.barrier()
                s7.close()
                fnw_bc = g6("fnw_bc", [128, D], F32)
                P.dma("sp", lambda e: e.dma_start(out=fnw_bc.h[:], in_=final_norm_w.partition_broadcast(128)),
                      writes=[(fnw_bc, "all")])
                wo_bf = g6("wo_bf", [128, DC, D], BF16)
                mT = [g6("mT0", [128, DC, 128], BF16)] * 2
                xr = [g6("xr%d" % i, [128, D], F32) for i in range(2)]
                x2 = [g6("x2%d" % i, [128, D], F32) for i in range(2)]
                ssq2 = g6("ssq2", [128, NT], F32)
                junkT = g6("junkT", [128, D], BF16)
                wo_view = w_out.rearrange("(c p) e -> p c e", p=128)
                for hf in range(2):
                    P.dma("pool", lambda e, hf=hf: e.dma_start(out=wo_bf.h[:, :, hf * 512:(hf + 1) * 512],
                                                             in_=wo_view[:, :, hf * 512:(hf + 1) * 512]),
                          writes=[(wo_bf, (dc, hf)) for dc in range(DC)])


                for tt in range(NT):
                    P.op("pool", lambda e, tt=tt: e.memset(mg.h[:, tt, 512:1024], 0.0), writes=[(mg, (tt, 1))])
                tail_load(0)
                tail_load(1)
                for tt in range(NT):
                    add_tile_tasks(tt, None)
                    drain_tasks()
            dbg_dump("mg", mg, [(tt, hf) for tt in range(NT) for hf in range(2)], [128, NT, 1024], BF16)

        P.emit(st)
        if stage >= 2.5:
            s6.close()
    return nc, dbg_out


def kernel(**inputs):
    nc, _ = build()
    in_maps = _in_maps(inputs)
    res = run_bass_kernel_spmd(nc, in_maps, core_ids=list(range(8)))
    return np.stack([r["y"] for r in res.results], axis=0)


def _in_maps(inputs):
    f = lambda a: np.ascontiguousarray(np.asarray(a, dtype=np.float32))
    maps = []
    for b in range(8):
        maps.append({
            "x": f(inputs["x"][b]),
            "norm1_w": f(inputs["norm1_w"][0]),
            "w_in": f(inputs["w_in"][0]),
            "sb_norm_w": f(inputs["sb_norm_w"][0]),
            "gdn_conv_w": f(inputs["gdn_conv_w"][0]),
            "gdn_A_log": f(inputs["gdn_A_log"][0]),
            "gdn_dt_bias": f(inputs["gdn_dt_bias"][0]),
            "gdn_norm_w": f(inputs["gdn_norm_w"][0]),
            "w_out": f(inputs["w_out"][0]),
            "final_norm_w": f(inputs["final_norm_w"]),
        })
    return maps
```
